# Optimizing a Trainium2 kernel written in Bass

```python
import jax, jax.numpy as jnp
from jax import lax
import numpy as np

D_MODEL = 4096
BATCH = 8
SEQ = 2048
DEPTH = 1
DEC_BATCH = 4
DEC_SEQ = 2048
PAST_LEN = 128

HEAD_SIZE = 64
N_HEADS = D_MODEL // HEAD_SIZE
DECAY_LORA = max(32, int(round(1.8 * D_MODEL ** 0.5 / 32)) * 32)
ICLR_LORA = max(32, int(round(1.8 * D_MODEL ** 0.5 / 32)) * 32)
GATE_LORA = max(32, int(round(0.6 * D_MODEL ** 0.8 / 32)) * 32)
GN_EPS = 64e-5
POOL_WIDTH = D_MODEL // 2
POOL_WINDOWS = (2, 4, 8, 16)
POOL_GROUPS = len(POOL_WINDOWS)
POOL_GROUP_IN = POOL_WIDTH // POOL_GROUPS
POOL_GROUP_OUT = D_MODEL // POOL_GROUPS
N_MEM = 256
X_HEADS = 4
X_HEAD_DIM = D_MODEL // X_HEADS
FFN_HIDDEN = ((8 * D_MODEL + 3 * 256 - 1) // (3 * 256)) * 256
NORM_EPS = 1e-6
RWKV_COLS = 3 * D_MODEL + DECAY_LORA + ICLR_LORA + GATE_LORA
IN_COLS = RWKV_COLS + POOL_WIDTH + 2 * D_MODEL

kernel_name = 'hybrid_rwkv7_pool_xattn_encoder'


def rms_norm(x, g):
    xf = x.astype(jnp.float32)
    y = xf * lax.rsqrt(jnp.mean(xf * xf, axis=-1, keepdims=True) + NORM_EPS)
    return (y * g.astype(jnp.float32)).astype(x.dtype)


def centred_shift(z, taps):
    zp = jnp.pad(z, ((0, 0), (1, 1), (0, 0)))
    return zp[:, :-2] * taps[0] + zp[:, 1:-1] * taps[1] + zp[:, 2:] * taps[2]


def to_heads(x):
    return x.reshape(x.shape[0], x.shape[1], N_HEADS, HEAD_SIZE)


def wkv7_scan(r, w, k, v, a, b, reverse):
    xs = tuple(jnp.moveaxis(to_heads(t).astype(jnp.float32), 1, 0) for t in (r, w, k, v, a, b))
    batch = r.shape[0]

    def step(S, inp):
        r_t, w_t, k_t, v_t, a_t, b_t = inp
        sa = jnp.einsum('bhvk,bhk->bhv', S, a_t)
        S = S * w_t[:, :, None, :] + sa[..., None] * b_t[:, :, None, :] + v_t[..., None] * k_t[:, :, None, :]
        return S, jnp.einsum('bhvk,bhk->bhv', S, r_t)

    S0 = jnp.zeros((batch, N_HEADS, HEAD_SIZE, HEAD_SIZE), jnp.float32)
    _, ys = lax.scan(step, S0, xs, reverse=reverse)
    return jnp.moveaxis(ys, 0, 1)


def rwkv7_direction(r, k, v, kk, wd, ad, w0, w_up, a0, a_up, k_a, r_k, reverse):
    w = -jax.nn.softplus(-(w0 + jnp.tanh(wd) @ w_up).astype(jnp.float32)) - 0.5
    decay = jnp.exp(-jnp.exp(w))
    a = jax.nn.sigmoid((a0 + ad @ a_up).astype(jnp.float32))
    kd = k.astype(jnp.float32) * (1.0 + (a - 1.0) * k_a)
    y = wkv7_scan(r, decay, kd, v, -kk, kk * a, reverse)
    bonus = jnp.sum(to_heads(r).astype(jnp.float32) * to_heads(kd) * r_k, axis=-1, keepdims=True) \
        * to_heads(v).astype(jnp.float32)
    return y, bonus


def rwkv7_branch(z, lp):
    B, T, _ = z.shape
    i1, i2, i3 = D_MODEL, 2 * D_MODEL, 3 * D_MODEL
    i4 = i3 + DECAY_LORA
    i5 = i4 + ICLR_LORA
    r, k, v = z[..., :i1], z[..., i1:i2], z[..., i2:i3]
    wd, ad, gd = z[..., i3:i4], z[..., i4:i5], z[..., i5:]
    g = jax.nn.sigmoid(gd) @ lp['g_up']
    kk = to_heads(k * lp['k_k']).astype(jnp.float32)
    kk = kk * lax.rsqrt(jnp.maximum(jnp.sum(kk * kk, axis=-1, keepdims=True), 1e-12))
    kk = kk.reshape(B, T, D_MODEL)
    y_f, bo_f = rwkv7_direction(r, k, v, kk, wd, ad, lp['w0_f'], lp['w_up_f'], lp['a0_f'], lp['a_up_f'],
                                lp['k_a'], lp['r_k'], False)
    y_b, bo_b = rwkv7_direction(r, k, v, kk, wd, ad, lp['w0_b'], lp['w_up_b'], lp['a0_b'], lp['a_up_b'],
                                lp['k_a'], lp['r_k'], True)
    y = y_f + y_b
    mu = jnp.mean(y, axis=-1, keepdims=True)
    var = jnp.mean(jnp.square(y - mu), axis=-1, keepdims=True)
    gn_g = lp['ln_x_g'].astype(jnp.float32).reshape(N_HEADS, HEAD_SIZE)
    gn_b = lp['ln_x_b'].astype(jnp.float32).reshape(N_HEADS, HEAD_SIZE)
    y = (y - mu) * lax.rsqrt(var + GN_EPS) * gn_g + gn_b + bo_f + bo_b
    return y.reshape(B, T, D_MODEL).astype(z.dtype) * g


def centred_window_mean(p, win):
    T = p.shape[1]
    c = jnp.pad(jnp.cumsum(p, axis=1), ((0, 0), (1, 0), (0, 0)))
    t = jnp.arange(T)
    lo = jnp.clip(t - win // 2, 0, T)
    hi = jnp.clip(t + win - win // 2, 0, T)
    s = jnp.take(c, hi, axis=1) - jnp.take(c, lo, axis=1)
    return s / (hi - lo).astype(jnp.float32)[None, :, None]


def pool_branch(p, lp):
    B, T, _ = p.shape
    pg = p.reshape(B, T, POOL_GROUPS, POOL_GROUP_IN).astype(jnp.float32)
    d = jnp.stack([centred_window_mean(pg[:, :, gi], win) - pg[:, :, gi]
                   for gi, win in enumerate(POOL_WINDOWS)], axis=2)
    out = jnp.einsum('btgi,gio->btgo', d.astype(p.dtype), lp['pool_w'])
    return out.reshape(B, T, D_MODEL) * lp['pool_scale']


def cross_attention(hn, mn, lp):
    B, T, _ = hn.shape
    M = mn.shape[1]
    q = (hn @ lp['xq']).reshape(B, T, X_HEADS, X_HEAD_DIM)
    k = (mn @ lp['xk']).reshape(B, M, X_HEADS, X_HEAD_DIM)
    v = (mn @ lp['xv']).reshape(B, M, X_HEADS, X_HEAD_DIM)
    s = jnp.einsum('bqhd,bkhd->bhqk', q, k).astype(jnp.float32) * (X_HEAD_DIM ** -0.5)
    probs = jax.nn.softmax(s, axis=-1).astype(v.dtype)
    o = jnp.einsum('bhqk,bkhd->bqhd', probs, v).reshape(B, T, D_MODEL)
    return o @ lp['xo']


def swiglu(hn, lp):
    u = hn @ lp['ffn_w13']
    gate, up = u[..., :FFN_HIDDEN], u[..., FFN_HIDDEN:]
    return (jax.nn.silu(gate) * up) @ lp['ffn_w2']


def encoder_layer(h, mem, lp):
    xn = rms_norm(h, lp['norm_mix_g'])
    z = xn @ lp['w_in']
    p0 = RWKV_COLS
    p1 = p0 + POOL_WIDTH
    p2 = p1 + D_MODEL
    y_a = rwkv7_branch(centred_shift(z[..., :p0], lp['shift_w']), lp)
    y_b = pool_branch(z[..., p0:p1], lp)
    merged = jax.nn.sigmoid(z[..., p1:p2]) * y_a + jax.nn.sigmoid(z[..., p2:]) * y_b
    h = h + merged @ lp['w_out']
    h = h + cross_attention(rms_norm(h, lp['norm_x_g']), rms_norm(mem, lp['norm_mem_g']), lp)
    h = h + swiglu(rms_norm(h, lp['norm_ffn_g']), lp)
    return h


def encoder_trunk(x, mem, params, norm_final_g):
    h = x
    for l in range(DEPTH):
        lp = {name: arr[l] for name, arr in params.items()}
        h = encoder_layer(h, mem, lp)
    return rms_norm(h, norm_final_g)


def setup_inputs(seed: int = 0) -> dict:
    key = jax.random.key(seed)
    ks = iter(jax.random.split(key, 40))
    f32 = jnp.float32
    L, D = DEPTH, D_MODEL

    def nrm(shape, scale):
        return jax.random.normal(next(ks), shape, f32) * scale

    def gain(shape):
        return 1.0 + nrm(shape, 0.02)

    shift_base = jnp.array([0.25, 0.5, 0.25], f32)[None, :, None]
    return {
        'x_prompt': nrm((BATCH, SEQ, D), 1.0),
        'x_sample': nrm((DEC_BATCH, DEC_SEQ, D), 1.0),
        'mem_prompt': nrm((BATCH, N_MEM, D), 1.0),
        'mem_sample': nrm((DEC_BATCH, N_MEM, D), 1.0),
        'norm_mix_g': gain((L, D)),
        'w_in': nrm((L, D, IN_COLS), D ** -0.5),
        'shift_w': shift_base + nrm((L, 3, RWKV_COLS), 0.05),
        'w0_f': jax.random.uniform(next(ks), (L, D), f32, -6.0, -1.0),
        'w_up_f': nrm((L, DECAY_LORA, D), 0.5 * DECAY_LORA ** -0.5),
        'w0_b': jax.random.uniform(next(ks), (L, D), f32, -6.0, -1.0),
        'w_up_b': nrm((L, DECAY_LORA, D), 0.5 * DECAY_LORA ** -0.5),
        'a0_f': nrm((L, D), 0.02),
        'a_up_f': nrm((L, ICLR_LORA, D), ICLR_LORA ** -0.5),
        'a0_b': nrm((L, D), 0.02),
        'a_up_b': nrm((L, ICLR_LORA, D), ICLR_LORA ** -0.5),
        'g_up': nrm((L, GATE_LORA, D), GATE_LORA ** -0.5),
        'k_k': 0.85 + nrm((L, D), 0.02),
        'k_a': gain((L, D)),
        'r_k': nrm((L, N_HEADS, HEAD_SIZE), 0.1),
        'ln_x_g': gain((L, D)),
        'ln_x_b': nrm((L, D), 0.02),
        'pool_w': nrm((L, POOL_GROUPS, POOL_GROUP_IN, POOL_GROUP_OUT), POOL_GROUP_IN ** -0.5),
        'pool_scale': gain((L, D)),
        'w_out': nrm((L, D, D), D ** -0.5),
        'norm_x_g': gain((L, D)),
        'norm_mem_g': gain((L, D)),
        'xq': nrm((L, D, D), D ** -0.5),
        'xk': nrm((L, D, D), D ** -0.5),
        'xv': nrm((L, D, D), D ** -0.5),
        'xo': nrm((L, D, D), D ** -0.5),
        'norm_ffn_g': gain((L, D)),
        'ffn_w13': nrm((L, D, 2 * FFN_HIDDEN), D ** -0.5),
        'ffn_w2': nrm((L, FFN_HIDDEN, D), FFN_HIDDEN ** -0.5),
        'norm_final_g': gain((D,)),
    }


def reference(x_prompt, x_sample, mem_prompt, mem_sample, norm_mix_g, w_in, shift_w,
              w0_f, w_up_f, w0_b, w_up_b, a0_f, a_up_f, a0_b, a_up_b, g_up, k_k, k_a, r_k,
              ln_x_g, ln_x_b, pool_w, pool_scale, w_out, norm_x_g, norm_mem_g, xq, xk, xv, xo,
              norm_ffn_g, ffn_w13, ffn_w2, norm_final_g):
    params = {
        'norm_mix_g': norm_mix_g, 'w_in': w_in, 'shift_w': shift_w,
        'w0_f': w0_f, 'w_up_f': w_up_f, 'w0_b': w0_b, 'w_up_b': w_up_b,
        'a0_f': a0_f, 'a_up_f': a_up_f, 'a0_b': a0_b, 'a_up_b': a_up_b,
        'g_up': g_up, 'k_k': k_k, 'k_a': k_a, 'r_k': r_k, 'ln_x_g': ln_x_g, 'ln_x_b': ln_x_b,
        'pool_w': pool_w, 'pool_scale': pool_scale, 'w_out': w_out,
        'norm_x_g': norm_x_g, 'norm_mem_g': norm_mem_g, 'xq': xq, 'xk': xk, 'xv': xv, 'xo': xo,
        'norm_ffn_g': norm_ffn_g, 'ffn_w13': ffn_w13, 'ffn_w2': ffn_w2,
    }
    y_prompt = encoder_trunk(x_prompt, mem_prompt, params, norm_final_g)
    y_sample = encoder_trunk(x_sample, mem_sample, params, norm_final_g)
    return (y_prompt, y_sample)
```

```python
import contextlib
import numpy as np
import ml_dtypes
import concourse.bass as bass
import concourse.mybir as mybir
from concourse.bass_utils import run_bass_kernel_spmd

F32 = mybir.dt.float32
BF16 = mybir.dt.bfloat16
AF = mybir.ActivationFunctionType
ALU = mybir.AluOpType
AX = mybir.AxisListType

D = 4096
T = 2048
NSLOT = 2
NMEM = 256
RW = 13024
PW = 2048
INC = 23264
FF = 11008
CH = 128
NCH = T // CH
P0 = RW
P1 = RW + PW
P2 = P1 + D
NEG_EXP_HALF = -0.6065306597126334


class Sy:
    SEG = 30000

    def __init__(self, nc):
        self.nc = nc
        self.eng = {'pe': nc.tensor, 'dve': nc.vector, 'act': nc.scalar, 'pool': nc.gpsimd, 'sp': nc.sync}
        self.cnt = {e: 0 for e in self.eng}
        self.segs = {e: [] for e in self.eng}
        self.waited = {e: {} for e in self.eng}
        self.last_w = {}
        self.readers = {}
        self.dsem = {}
        self.retired = []
        self.nsem = 0

    def _newsem(self, name):
        self.nsem += 1
        return self.nc.alloc_semaphore(name + "_%d" % self.nsem)

    def _wait(self, X, tok):
        if tok is None:
            return
        if tok[0] == 'e':
            _, E, n = tok
            if E == X and X == 'pe':
                return
            if self.waited[X].get(E, 0) >= n:
                return
            self.waited[X][E] = n
            seg = (n - 1) // self.SEG
            self.eng[X].wait_ge(self.segs[E][seg], (n - 1) % self.SEG + 1)
        else:
            _, sem, v, sid = tok
            if self.waited[X].get(sid, 0) >= v:
                return
            self.waited[X][sid] = v
            self.eng[X].wait_ge(sem, v)

    def _deps(self, X, reads, writes):
        for k in reads:
            self._wait(X, self.last_w.get(k))
        for k in writes:
            self._wait(X, self.last_w.get(k))
            for tk in self.readers.get(k, ()):
                self._wait(X, tk)

    def _record(self, tok, reads, writes):
        for k in reads:
            lst = self.readers.setdefault(k, [])
            if tok[0] == 'e':
                lst[:] = [t for t in lst if not (t[0] == 'e' and t[1] == tok[1])]
            lst.append(tok)
        for k in writes:
            self.last_w[k] = tok
            self.readers[k] = []

    def op(self, X, fn, reads=(), writes=()):
        self._deps(X, reads, writes)
        ins = fn(self.eng[X])
        n = self.cnt[X] + 1
        seg = (n - 1) // self.SEG
        while len(self.segs[X]) <= seg:
            self.segs[X].append(self._newsem("pg_" + X))
        ins.then_inc(self.segs[X][seg], 1)
        self.cnt[X] = n
        self._record(('e', X, n), reads, writes)
        return ins

    def dma(self, X, out, in_, reads=(), writes=(), dkey=None):
        self._deps(X, reads, writes)
        if dkey is None:
            dkey = writes[0] if writes else reads[0]
        ent = self.dsem.get(dkey)
        if ent is None or ent[1] + 16 > self.SEG:
            if ent is not None:
                self.retired.append(('d', ent[0], ent[1], ent[2]))
            ent = [self._newsem("dm"), 0, self.nsem]
            self.dsem[dkey] = ent
        ent[1] += 16
        self.eng[X].dma_start(out=out, in_=in_).then_inc(ent[0], 16)
        tok = ('d', ent[0], ent[1], ent[2])
        self._record(tok, reads, writes)
        return tok

    def drain(self, X, skip_wb=False):
        iswb = lambda k: skip_wb and isinstance(k, tuple) and k[0] == 'wb'
        for k, tk in list(self.last_w.items()):
            if not iswb(k):
                self._wait(X, tk)
        for k, lst in list(self.readers.items()):
            if iswb(k):
                continue
            for tk in lst:
                self._wait(X, tk)
        for tk in self.retired:
            self._wait(X, tk)
        for E in self.eng:
            if E != X and self.cnt[E] > 0:
                self._wait(X, ('e', E, self.cnt[E]))


def feature_tiles():
    tiles = []
    f = 0
    while f < RW:
        w = min(128, RW - f)
        tiles.append((f, w, 'rw'))
        f += w
    for seg0, n, kind in ((P0, PW, 'pool'), (P1, D, 'gate'), (P2, D, 'gate')):
        for i in range(n // 128):
            tiles.append((seg0 + i * 128, 128, kind))
    return tiles


def feature_blocks():
    tl = feature_tiles()
    blocks = []
    cur = []
    for t in tl:
        if cur and (len(cur) == 4 or cur[-1][2] != t[2] or cur[-1][1] != 128):
            blocks.append(cur)
            cur = []
        cur.append(t)
    if cur:
        blocks.append(cur)
    return blocks


class SlotAP:
    def __init__(self, aps):
        self.aps = aps

    def __getitem__(self, idx):
        return self.aps[idx[0]][idx[1:]]


class K:
    def __init__(self, nslot=NSLOT, phases=(0, 1, 2, 3), debug=False, ntile1=None, nhp=None, ntile3=None, half=True):
        self.nslot = nslot
        self.phases = phases
        self.debug = debug
        self.ntile1 = ntile1
        self.nhp = nhp
        self.ntile3 = ntile3
        nc = bass.Bass("TRN2", target_bir_lowering=False)
        self.nc = nc
        self.sy = Sy(nc)
        self.q = 0
        self.ee = 0
        ext_in = lambda name, shape, dt=F32: nc.dram_tensor(name, list(shape), dt, kind="ExternalInput").ap()
        self.x = ext_in("x", [nslot, T, D])
        self.mem = ext_in("mem", [nslot, NMEM, D])
        self.w = {}
        wshapes = dict(w_in=[D, INC], w_out=[D, D], xq=[D, D], xk=[D, D], xv=[D, D], xo=[D, D],
                       ffn_w13=[D, 2 * FF], ffn_w2=[FF, D], pool_w=[PW, 1024], g_up=[480, D],
                       w_up_f=[128, D], w_up_b=[128, D], a_up_f=[128, D], a_up_b=[128, D])
        self.wshapes = wshapes
        need = self._needed_weights()
        for name in need:
            self.w[name] = ext_in(name, wshapes[name])
        self.gains = ext_in("gains", [5, D])
        self.gainsT = ext_in("gainsT", [128, 160])
        self.taps = ext_in("taps", [128, 102 * 3])
        self.cp = ext_in("cp", [128, 10 * 32])
        self.consts = ext_in("consts", [128, 1664])
        kind_scr = "ExternalOutput" if debug else "Internal"
        self.wb = {}
        for name in need:
            self.wb[name] = nc.dram_tensor("wb_" + name, wshapes[name], BF16, kind="Internal").ap()
        if 1 in phases:
            self.zT = SlotAP([nc.dram_tensor("zT%d" % s, [INC, T], F32, kind=kind_scr).ap() for s in range(nslot)])
        elif 2 in phases:
            self.zT = SlotAP([ext_in("zT%d" % s, [INC, T]) for s in range(nslot)])
        self.half = half and nslot == 2
        self.t3 = [T] * nslot
        if self.half:
            self.t3[1] = T // 2
            self.sel = ext_in("sel", [128, 2])
            self.x1h = ext_in("x1h", [T // 2, D])
        if 2 in phases:
            self.mT = SlotAP([nc.dram_tensor("mT%d" % s, [D, self.t3[s]], BF16, kind=kind_scr).ap() for s in range(nslot)])
        elif 3 in phases:
            self.mT = SlotAP([ext_in("mT%d" % s, [D, self.t3[s]], BF16) for s in range(nslot)])
        if 3 in phases:
            self.y = SlotAP([nc.dram_tensor("y%d" % s, [self.t3[s], D], F32, kind="ExternalOutput").ap() for s in range(nslot)])
            self.x3 = SlotAP([self.x[s] if self.t3[s] == T else self.x1h for s in range(nslot)])
        if 2 in phases:
            self.gT = nc.dram_tensor("gT", [nslot, D, T], F32, kind="Internal").ap()
            self.ybT = nc.dram_tensor("ybT", [nslot, D, T], F32, kind="Internal").ap()
            self.invcnt = ext_in("invcnt", [4, T])

    def _needed_weights(self):
        need = []
        if 1 in self.phases:
            need += ['w_in']
        if 2 in self.phases:
            need += ['pool_w', 'g_up', 'w_up_f', 'w_up_b', 'a_up_f', 'a_up_b']
        if 3 in self.phases:
            need += ['w_out', 'xq', 'xk', 'xv', 'xo', 'ffn_w13', 'ffn_w2']
        return need

    def dq(self):
        return 'sp'

    def evq(self):
        self.ee ^= 1
        return 'dve' if self.ee else 'act'

    def sb(self, name, shape, dt):
        return self.stack.enter_context(self.nc.sbuf_tensor(name, list(shape), dt))

    def pt(self, name, shape, dt):
        return self.stack.enter_context(self.nc.psum_tensor(name, list(shape), dt))

    LATE = ('w_out', 'xq', 'xk', 'xv', 'xo', 'ffn_w13', 'ffn_w2')

    def phase0(self, late):
        sy = self.sy
        for name in self._needed_weights():
            if (name in self.LATE) != late:
                continue
            src = self.w[name]
            dst = self.wb[name]
            rows = self.wshapes[name][0]
            if name == 'w_in':
                for bi, blk in enumerate(feature_blocks()):
                    b0 = blk[0][0]
                    bw = sum(t[1] for t in blk)
                    for r in range(0, rows, 1024):
                        sy.dma('pool', dst[r:r + 1024, b0:b0 + bw], src[r:r + 1024, b0:b0 + bw],
                               writes=[('wb', 'w_in', bi)], dkey=('wbc', bi % 8))
                continue
            step = 128 if self.wshapes[name][1] > 8192 else 512
            r = 0
            while r < rows:
                n = min(step, rows - r)
                sy.dma('pool', dst[r:r + n, :], src[r:r + n, :], writes=[('wb', name)], dkey=('wb', name))
                r += n

    def load_consts(self):
        sy = self.sy
        self.c_sb = self.sb("c_sb", [128, 1664], F32)
        sy.dma('sp', self.c_sb[:], self.consts, writes=['c_sb'])
        self.ident_f = self.c_sb[:, 0:128]
        self.bones_f = self.c_sb[:, 128:256]
        self.maskA = {0: self.c_sb[:, 256:768], 1: self.c_sb[:, 768:1280]}
        self.maskL = {0: self.c_sb[:, 1280:1408], 1: self.c_sb[:, 1408:1536]}
        self.scanm = self.c_sb[:, 1536:1536 + 128]
        if self.half:
            self.sel_sb = self.sb("sel_sb", [128, 2], F32)
            sy.dma('sp', self.sel_sb[:], self.sel, writes=['sel'])
        self.taps_sb = self.sb("taps_sb", [128, 306], F32)
        sy.dma('sp', self.taps_sb[:], self.taps, writes=['taps_sb'])
        self.cp_sb = self.sb("cp_sb", [128, 352], F32)
        sy.dma('sp', self.cp_sb[:, 0:320], self.cp, writes=['cp_sb'])
        sy.op('dve', lambda e: e.tensor_scalar(out=self.cp_sb[:, 320:352], in0=self.cp_sb[:, 5 * 32:6 * 32], scalar1=-1.0,
                                               scalar2=1.0, op0=ALU.mult, op1=ALU.add), reads=['cp_sb'], writes=['cp_sb'])

    def cpv(self, j, hp):
        return self.cp_sb[:, j * 32 + hp:j * 32 + hp + 1]

    def rms_tok(self, src_ap, src_key, gbc, out_bf, out_key, scr, tag):
        sy = self.sy
        sq, ss, rstd = scr['sq'], scr['ss'], scr['rstd']
        sy.op('act', lambda e: e.activation(out=sq[:], in_=src_ap, func=AF.Square, accum_out=ss[:]),
              reads=[src_key], writes=[tag + 'sq', tag + 'ss'])
        sy.op('act', lambda e: e.activation(out=rstd[:], in_=ss[:], func=AF.Sqrt, bias=self.eps_sb[:], scale=1.0 / D),
              reads=[tag + 'ss', 'eps'], writes=[tag + 'rstd'])
        sy.op('dve', lambda e: e.reciprocal(out=rstd[:], in_=rstd[:]), reads=[tag + 'rstd'], writes=[tag + 'rstd'])
        sy.op('dve', lambda e: e.scalar_tensor_tensor(out=out_bf, in0=src_ap, scalar=rstd[:, 0:1], in1=gbc[:],
                                                      op0=ALU.mult, op1=ALU.mult),
              reads=[src_key, tag + 'rstd', 'gbc'], writes=[out_key])

    def transpose_to_fm(self, src_bf, src_key, dst, dst_key, col0, pst, pst_key, nkc=32):
        sy = self.sy
        for g in range(nkc // 8):
            pk = (pst_key, g % 2)
            pt = pst[g % 2]
            for j in range(8):
                kc = g * 8 + j
                sy.op('pe', lambda e, kc=kc, j=j: e.transpose(out=pt[:, j, :], in_=src_bf[:, kc * 128:(kc + 1) * 128],
                                                             identity=self.ident_bf[:]),
                      reads=[src_key, 'ident_bf'], writes=[pk])
            sy.op(self.evq(), lambda e, g=g: e.tensor_copy(out=dst[:, g * 8:(g + 1) * 8, col0:col0 + 128], in_=pt[:, :, :])
                  if e is self.nc.vector else e.copy(out=dst[:, g * 8:(g + 1) * 8, col0:col0 + 128], in_=pt[:, :, :]),
                  reads=[pk], writes=[dst_key])

    def p1_load_w(self, A, gi):
        if gi >= len(A['items']):
            return
        bi = A['items'][gi][2]
        blk = A['blocks'][bi]
        b0 = blk[0][0]
        bw = sum(t[1] for t in blk)
        self.sy.dma('sp', A['wt'][gi % 3][:, :, 0:bw], self.wb['w_in'][:, b0:b0 + bw].rearrange("(kc p) f -> p kc f", p=128),
                    reads=[('wb', 'w_in', bi)], writes=[('wt', gi % 3)])

    def phase1_all(self, A):
        sy = self.sy
        blocks = feature_blocks()
        A['blocks'] = blocks
        ntile = self.ntile1 or (T // 512)
        A['items'] = [(slot, tt, bi) for slot in range(self.nslot) for tt in range(ntile) for bi in range(len(blocks))]
        self.p1_load_w(A, 0)
        self.p1_load_w(A, 1)
        for gi, (slot, tt, bi) in enumerate(A['items']):
            tok0 = tt * 512
            if bi == 0:
                for sub in range(4):
                    xs = A['xs']
                    sy.dma('sp', xs[:], self.x[slot, tok0 + sub * 128: tok0 + (sub + 1) * 128, :], writes=['xs'])
                    self.rms_tok(xs[:], 'xs', A['gbc'], A['xnb'][:], 'xnb', A, 'p1')
                    self.transpose_to_fm(A['xnb'], 'xnb', A['xnT'], 'xnT', sub * 128, A['pst'], 'pst')
            self.p1_load_w(A, gi + 2)
            blk = blocks[bi]
            b0 = blk[0][0]
            bw = sum(t[1] for t in blk)
            wbuf = A['wt'][gi % 3]
            wkey = ('wt', gi % 3)
            zs = A['zst'][gi % 2]
            zkey = ('zst', gi % 2)
            for ti, (f0, fw, kind) in enumerate(blk):
                A['psn'] = (A['psn'] + 1) % 4
                ps = A['ps'][A['psn']]
                pk = ('ps', A['psn'])
                o = f0 - b0
                for kc in range(32):
                    sy.op('pe', lambda e, kc=kc, o=o, fw=fw, ps=ps: e.matmul(ps[0:fw, :], lhsT=wbuf[:, kc, o:o + fw],
                                                                            rhs=A['xnT'][:, kc, :], start=(kc == 0), stop=(kc == 31)),
                          reads=[wkey, 'xnT'], writes=[pk])
                if kind == 'gate':
                    sy.op('act', lambda e, ti=ti, fw=fw, ps=ps: e.activation(out=zs[0:fw, ti, :], in_=ps[0:fw, :], func=AF.Sigmoid),
                          reads=[pk], writes=[zkey])
                else:
                    sy.op('dve', lambda e, ti=ti, fw=fw, ps=ps: e.tensor_copy(out=zs[0:fw, ti, :], in_=ps[0:fw, :]),
                          reads=[pk], writes=[zkey])
            if all(t[1] == 128 for t in blk):
                sy.dma('sp', self.zT[slot, b0:b0 + bw, tok0:tok0 + 512].rearrange("(ft p) t -> p ft t", p=128),
                       zs[:, 0:len(blk), :], reads=[zkey], writes=[('zT', slot)], dkey=('zT', slot))
            else:
                for ti, (f0, fw, kind) in enumerate(blk):
                    sy.dma('sp', self.zT[slot, f0:f0 + fw, tok0:tok0 + 512], zs[0:fw, ti, :],
                           reads=[zkey], writes=[('zT', slot)], dkey=('zT', slot))

    def alloc_phase1(self):
        A = {}
        A['xs'] = self.sb("p1_xs", [128, D], F32)
        A['sq'] = self.sb("p1_sq", [128, D], BF16)
        A['ss'] = self.sb("p1_ss", [128, 1], F32)
        A['rstd'] = self.sb("p1_rstd", [128, 1], F32)
        A['xnb'] = self.sb("p1_xnb", [128, D], BF16)
        A['xnT'] = self.sb("p1_xnT", [128, 32, 512], BF16)
        A['gbc'] = self.sb("p1_gbc", [128, D], F32)
        A['wt'] = [self.sb("p1_wt%d" % i, [128, 32, 512], BF16) for i in range(3)]
        A['zst'] = [self.sb("p1_zst%d" % i, [128, 4, 512], F32) for i in range(2)]
        A['pst'] = [self.pt("p1_pst%d" % i, [128, 8, 128], BF16) for i in range(2)]
        A['ps'] = [self.pt("p1_ps%d" % i, [128, 512], F32) for i in range(4)]
        A['psn'] = 0
        self.sy.dma('sp', A['gbc'][:], self.gains[0:1, :].to_broadcast([128, D]), writes=['gbc'])
        return A


    def tt(self, X, out, a, b, op, reads, writes):
        return self.sy.op(X, lambda e: e.tensor_tensor(out=out, in0=a, in1=b, op=op), reads=reads, writes=writes)

    def ts(self, X, out, a, s1, s2, op0, op1, reads, writes):
        if s2 is None:
            return self.sy.op(X, lambda e: e.tensor_scalar(out=out, in0=a, scalar1=s1, scalar2=None, op0=op0), reads=reads, writes=writes)
        return self.sy.op(X, lambda e: e.tensor_scalar(out=out, in0=a, scalar1=s1, scalar2=s2, op0=op0, op1=op1), reads=reads, writes=writes)

    def stt(self, X, out, a, sc, b, op0, op1, reads, writes):
        return self.sy.op(X, lambda e: e.scalar_tensor_tensor(out=out, in0=a, scalar=sc, in1=b, op0=op0, op1=op1), reads=reads, writes=writes)

    def actf(self, out, a, func, reads, writes, bias=0.0, scale=1.0):
        return self.sy.op('act', lambda e: e.activation(out=out, in_=a, func=func, bias=bias, scale=scale), reads=reads, writes=writes)

    def mm(self, out, lhsT, rhs, start, stop, reads, writes):
        return self.sy.op('pe', lambda e: e.matmul(out, lhsT=lhsT, rhs=rhs, start=start, stop=stop), reads=reads, writes=writes)

    def alloc_phase2(self):
        B = {}
        for n in ('zr', 'zk', 'zv', 'kkn', 'A1', 'A2', 'A3', 'A4', 'A6', 'A7', 'A8'):
            full = self.sb("p2_" + n, [128, T + 16], F32)
            B[n + '_full'] = full
            B[n] = full[:, 0:T]
        B['lw'] = self.sb("p2_lw", [128, T], BF16)
        B['la'] = self.sb("p2_la", [128, T], BF16)
        B['smask'] = self.sb("p2_smask", [128, T], BF16)
        B['wl'] = [self.sb("p2_wl%d" % i, [128, 8, 128], BF16) for i in range(2)]
        B['rkb'] = [self.sb("p2_rkb%d" % i, [128, T], BF16) for i in range(2)]
        B['AR'] = [[self.sb("p2_AR%d%d" % (d, h), [128, NCH, 256], BF16) for h in range(2)] for d in range(2)]
        B['bT'] = [self.sb("p2_bT%d" % d, [128, NCH, 128], BF16) for d in range(2)]
        B['kT'] = [self.sb("p2_kT%d" % d, [128, NCH, 128], BF16) for d in range(2)]
        B['bh'] = [self.sb("p2_bh%d" % d, [128, NCH, 128], BF16) for d in range(2)]
        B['kh'] = [self.sb("p2_kh%d" % d, [128, NCH, 128], BF16) for d in range(2)]
        B['Vpad'] = self.sb("p2_Vpad", [128, NCH, 2, 128], BF16)
        B['wc'] = [self.sb("p2_wc%d" % d, [128, NCH], F32) for d in range(2)]
        B['Hm'] = [self.sb("p2_Hm%d" % d, [128, 128], F32) for d in range(2)]
        B['Hb'] = [self.sb("p2_Hb%d" % d, [128, 128], BF16) for d in range(2)]
        B['ATm'] = [[self.sb("p2_ATm%d%d" % (d, h), [128, 512], BF16) for h in range(2)] for d in range(2)]
        B['Lm'] = [self.sb("p2_Lm%d" % d, [128, 2, 128], BF16) for d in range(2)]
        B['SQ'] = [[self.sb("p2_SQ%d%d" % (d, i), [128, 4, 128], BF16) for i in range(2)] for d in range(2)]
        B['Ub'] = [self.sb("p2_Ub%d" % d, [128, 128], BF16) for d in range(2)]
        B['Upad'] = [self.sb("p2_Upad%d" % d, [128, 2, 128], BF16) for d in range(2)]
        B['tmp5'] = B['kkn'][:, 0:512]
        B['PA'] = [[self.pt("p2_PA%d%d" % (d, h), [128, 512], F32) for h in range(2)] for d in range(2)]
        B['PB'] = [self.pt("p2_PB%d" % d, [128, 512], F32) for d in range(2)]
        B['PS'] = [self.pt("p2_PS%d" % d, [128, 512], F32) for d in range(2)]
        sy = self.sy
        sy.op('dve', lambda e: e.tensor_copy(out=B['smask'][:].rearrange("p (c t) -> p c t", t=CH),
                                             in_=self.scanm.unsqueeze(1).to_broadcast([128, NCH, CH])),
              reads=['c_sb'], writes=['smask'])
        for d in range(2):
            sy.op('pool', lambda e, d=d: e.memset(B['Upad'][d][:], 0.0), writes=[('Upad', d)])
        sy.op('pool', lambda e: e.memset(B['Vpad'][:], 0.0), writes=['Vpad'])
        for d in range(2):
            for h in range(2):
                sy.op('pool', lambda e, d=d, h=h: e.memset(B['AR'][d][h][:], 0.0), writes=[('AR', d, h)])
        return B

    def shift(self, src, dst, ti, rows, skey, dkey):
        tp = lambda j: self.taps_sb[0:rows, ti * 3 + j:ti * 3 + j + 1]
        self.ts('dve', dst[0:rows, :], src[0:rows, :], tp(1), None, ALU.mult, None, [skey, 'taps_sb'], [dkey])
        self.stt('dve', dst[0:rows, 1:T], src[0:rows, 0:T - 1], tp(0), dst[0:rows, 1:T], ALU.mult, ALU.add, [skey, 'taps_sb'], [dkey])
        self.stt('dve', dst[0:rows, 0:T - 1], src[0:rows, 1:T], tp(2), dst[0:rows, 0:T - 1], ALU.mult, ALU.add, [skey, 'taps_sb'], [dkey])

    def psum_rr(self, B):
        B['rr'] = (B.get('rr', -1) + 1) % 4
        i = B['rr']
        return B['PA'][i // 2][i % 2], ('PA', i // 2, i % 2)

    def phase2_partA(self, slot, B):
        sy = self.sy
        zT = self.zT
        for ti, dst, fn in ((96, B['lw'], AF.Tanh), (97, B['la'], AF.Identity)):
            f0 = ti * 128
            sy.dma('sp', B['A8'][:], zT[slot, f0:f0 + 128, :], reads=[('zT', slot)], writes=['A8'])
            self.shift(B['A8'], B['A7'], ti, 128, 'A8', 'A7')
            self.actf(dst[:], B['A7'][:], fn, ['A7'], [('lora', ti)])
        lg = [B['A1'][:].bitcast(BF16), B['A2'][:].bitcast(BF16)]
        lgv = lambda j: lg[j // 2][:, (j % 2) * T:(j % 2 + 1) * T]
        lgk = lambda j: 'A1' if j < 2 else 'A2'
        for j in range(4):
            ti = 98 + j
            rows = 128 if j < 3 else 96
            f0 = ti * 128
            sy.dma('sp', B['A8'][0:rows, :], zT[slot, f0:f0 + rows, :], reads=[('zT', slot)], writes=['A8'])
            self.shift(B['A8'], B['A7'], ti, rows, 'A8', 'A7')
            self.actf(lgv(j)[0:rows, :], B['A7'][0:rows, :], AF.Sigmoid, ['A7'], [lgk(j)])
        nhp = self.nhp or 32
        for hp in range(nhp):
            wl = B['wl'][hp % 2]
            wk = ('wl', hp % 2)
            gu = self.wb['g_up']
            sy.dma('sp', wl[:, 0:3, :], gu[0:384, hp * 128:(hp + 1) * 128].rearrange("(kc p) f -> p kc f", p=128),
                   reads=[('wb', 'g_up')], writes=[wk])
            sy.dma('sp', wl[0:96, 3, :], gu[384:480, hp * 128:(hp + 1) * 128], reads=[('wb', 'g_up')], writes=[wk])
            gst = B['A6'] if hp % 2 == 0 else B['A4']
            gk = 'A6' if hp % 2 == 0 else 'A4'
            for blk in range(4):
                ps, pk = self.psum_rr(B)
                for j in range(4):
                    rows = 128 if j < 3 else 96
                    self.mm(ps[:, :], wl[0:rows, j, :], lgv(j)[0:rows, blk * 512:(blk + 1) * 512], j == 0, j == 3,
                            [wk, lgk(j)], [pk])
                X = self.evq()
                sy.op(X, (lambda e, ps=ps, blk=blk: e.tensor_copy(out=gst[:, blk * 512:(blk + 1) * 512], in_=ps[:, :])) if X == 'dve'
                      else (lambda e, ps=ps, blk=blk: e.copy(out=gst[:, blk * 512:(blk + 1) * 512], in_=ps[:, :])),
                      reads=[pk], writes=[gk])
            sy.dma('sp', self.gT[slot, hp * 128:(hp + 1) * 128, :], gst[:], reads=[gk], writes=[('gT', slot)], dkey=('gT', slot))

    def phase2_pool(self, slot, B):
        sy = self.sy
        zT = self.zT
        pp = B['pp']
        sy.op('pool', lambda e: e.memset(pp[:, 0:8], 0.0), writes=['zr'])
        sy.op('pool', lambda e: e.memset(pp[:, T + 8:T + 16], 0.0), writes=['zr'])
        dT = [B['A1'][:].bitcast(BF16), B['A2'][:].bitcast(BF16)]
        dv = lambda j: dT[j // 2][:, (j % 2) * T:(j % 2 + 1) * T]
        dk = lambda j: 'A1' if j < 2 else 'A2'
        for gi, win in enumerate((2, 4, 8, 16)):
            h = win // 2
            sy.dma('sp', B['A7'][:], self.invcnt[gi:gi + 1, :].to_broadcast([128, T]), writes=['A7'])
            pwt = B['A3'][:].bitcast(BF16).rearrange("p (j f) -> p j f", j=4)
            sy.dma('sp', pwt, self.wb['pool_w'][gi * 512:(gi + 1) * 512, :].rearrange("(j p) f -> p j f", p=128),
                   reads=[('wb', 'pool_w')], writes=['A3'])
            for j in range(4):
                f0 = P0 + gi * 512 + j * 128
                sy.dma('sp', pp[:, 8:T + 8], zT[slot, f0:f0 + 128, :], reads=[('zT', slot)], writes=['zr'])
                cur = pp
                ck = 'zr'
                L = T + 16
                w = 1
                bufs = [(B['pa'], 'zk'), (B['pb'], 'zv')]
                bi = 0
                while w < win:
                    nxt, nk = bufs[bi]
                    bi ^= 1
                    n = L - (2 * w - 1)
                    self.tt('dve', nxt[:, 0:n], cur[:, 0:n], cur[:, w:w + n], ALU.add, [ck], [nk])
                    cur, ck = nxt, nk
                    w *= 2
                self.tt('dve', B['A4'][:], cur[:, 8 - h:8 - h + T], B['A7'][:], ALU.mult, [ck, 'A7'], ['A4'])
                self.tt('dve', dv(j), B['A4'][:], pp[:, 8:T + 8], ALU.subtract, ['A4', 'zr'], [dk(j)])
            for ot in range(8):
                hpi = gi * 8 + ot
                yst = B['A6'] if ot % 2 == 0 else B['A4']
                yk = 'A6' if ot % 2 == 0 else 'A4'
                for blk in range(4):
                    ps, pk = self.psum_rr(B)
                    for j in range(4):
                        self.mm(ps[:, :], pwt[:, j, ot * 128:(ot + 1) * 128], dv(j)[:, blk * 512:(blk + 1) * 512], j == 0, j == 3,
                                ['A3', dk(j)], [pk])
                    self.ts('dve', yst[:, blk * 512:(blk + 1) * 512], ps[:, :], self.cpv(9, hpi), None, ALU.mult, None, [pk, 'cp_sb'], [yk])
                sy.dma('sp', self.ybT[slot, hpi * 128:(hpi + 1) * 128, :], yst[:], reads=[yk], writes=[('ybT', slot)], dkey=('ybT', slot))

    def phase2_prep_dir(self, slot, hp, d, B):
        sy = self.sy
        wl = B['wl'][hp % 2]
        wk = ('wl', hp % 2)
        c3 = lambda ap: ap.rearrange("p (c t) -> p c t", t=CH)
        for blk in range(4):
            bs = slice(blk * 512, (blk + 1) * 512)
            ps, pk = self.psum_rr(B)
            self.mm(ps[:, :], wl[:, d, :], B['lw'][:, bs], True, True, [wk, ('lora', 96)], [pk])
            self.actf(B['A2'][:, bs], ps[:, :], AF.Sigmoid, [pk, 'cp_sb'], ['A2'], bias=self.cpv(0 + d, hp))
            ps, pk = self.psum_rr(B)
            self.mm(ps[:, :], wl[:, 2 + d, :], B['la'][:, bs], True, True, [wk, ('lora', 97)], [pk])
            self.actf(B['A1'][:, bs], ps[:, :], AF.Sigmoid, [pk, 'cp_sb'], ['A1'], bias=self.cpv(2 + d, hp))
        self.ts('dve', B['A2'][:], B['A2'][:], NEG_EXP_HALF, None, ALU.mult, None, ['A2'], ['A2'])
        self.tt('dve', B['A3'][:], B['kkn'][:], B['A1'][:], ALU.mult, ['kkn', 'A1'], ['A3'])
        self.ts('dve', B['A4'][:], B['A1'][:], self.cpv(5, hp), self.cpv(10, hp), ALU.mult, ALU.add, ['A1', 'cp_sb'], ['A4'])
        self.tt('dve', B['A4'][:], B['zk'][:], B['A4'][:], ALU.mult, ['zk', 'A4'], ['A4'])
        self.stt('dve', B['rkb'][d][:], B['zr'][:], self.cpv(6, hp), B['A4'][:], ALU.mult, ALU.mult, ['zr', 'A4', 'cp_sb'], [('rkb', d)])
        sy.op('dve', lambda e: e.tensor_tensor_scan(out=B['A1'][:], data0=B['smask'][:], data1=B['A2'][:], initial=0.0,
                                                    op0=ALU.mult, op1=ALU.add), reads=['smask', 'A2'], writes=['A1'])
        tot = c3(B['A1'][:])[:, :, CH - 1:CH]
        totb = tot.to_broadcast([128, NCH, CH])
        if d == 0:
            self.tt('dve', B['A6'][:], B['A1'][:], B['A2'][:], ALU.subtract, ['A1', 'A2'], ['A6'])
            cT, ck = B['A1'], 'A1'
        else:
            self.tt('dve', c3(B['A6'][:]), totb, c3(B['A1'][:]), ALU.subtract, ['A1'], ['A6'])
            self.tt('dve', B['A2'][:], B['A6'][:], B['A2'][:], ALU.add, ['A6', 'A2'], ['A2'])
            cT, ck = B['A2'], 'A2'
        self.tt('dve', c3(B['A7'][:]), totb, c3(cT[:]), ALU.subtract, ['A1', ck], ['A7'])
        self.actf(B['wc'][d][:], c3(B['A1'][:])[:, :, CH - 1], AF.Exp, ['A1'], [('wc', d)])
        self.actf(B['A8'][:], cT[:], AF.Exp, [ck], ['A8'])
        for h in range(2):
            hs = slice(h * 64, (h + 1) * 64)
            self.tt('dve', B['AR'][d][h][hs, :, 128:256], c3(B['zr'][hs, :]), c3(B['A8'][hs, :]), ALU.mult, ['zr', 'A8'], [('AR', d, h)])
        self.actf(B['A8'][:], B['A6'][:], AF.Exp, ['A6'], ['A8'])
        for h in range(2):
            hs = slice(h * 64, (h + 1) * 64)
            self.stt('dve', B['AR'][d][h][hs, :, 0:128], c3(B['kkn'][hs, :]), -1.0, c3(B['A8'][hs, :]), ALU.mult, ALU.mult,
                     ['kkn', 'A8'], [('AR', d, h)])
        self.actf(B['A8'][:], cT[:], AF.Exp, [ck], ['A8'], scale=-1.0)
        self.tt('dve', B['bT'][d][:].rearrange("p c t -> p (c t)"), B['A3'][:], B['A8'][:], ALU.mult, ['A3', 'A8'], [('bT', d)])
        self.tt('dve', B['kT'][d][:].rearrange("p c t -> p (c t)"), B['A4'][:], B['A8'][:], ALU.mult, ['A4', 'A8'], [('kT', d)])
        self.actf(B['A8'][:], B['A7'][:], AF.Exp, ['A7'], ['A8'])
        hatT = B['A6'][:].bitcast(BF16)
        self.tt('dve', hatT[:, 0:T], B['A3'][:], B['A8'][:], ALU.mult, ['A3', 'A8'], ['A6'])
        self.tt('dve', hatT[:, T:2 * T], B['A4'][:], B['A8'][:], ALU.mult, ['A4', 'A8'], ['A6'])
        for which, dst, dkn in ((0, B['bh'][d], ('bh', d)), (1, B['kh'][d], ('kh', d))):
            for g in range(NCH // 8):
                ps, pk = self.psum_rr(B)
                pv = ps[:, :].bitcast(BF16)[:, 0:1024].rearrange("p (j t) -> p j t", t=128)
                for j in range(8):
                    c = g * 8 + j
                    sy.op('pe', lambda e, c=c, j=j, pv=pv, which=which: e.transpose(
                        out=pv[:, j, :], in_=hatT[:, which * T + c * CH: which * T + (c + 1) * CH], identity=self.ident_bf[:]),
                        reads=['A6', 'ident_bf'], writes=[pk])
                X = self.evq()
                sy.op(X, (lambda e, g=g, pv=pv, dst=dst: e.tensor_copy(out=dst[:, g * 8:(g + 1) * 8, :], in_=pv)) if X == 'dve'
                      else (lambda e, g=g, pv=pv, dst=dst: e.copy(out=dst[:, g * 8:(g + 1) * 8, :], in_=pv)),
                      reads=[pk], writes=[dkn])

    def chunk_gen(self, hp, d, B):
        sy = self.sy
        AR, bT, kT, bh, kh = B['AR'][d], B['bT'][d], B['kT'][d], B['bh'][d], B['kh'][d]
        Vpad, Hm, Hb, ATm, Lm, SQ, Ub, Upad = B['Vpad'], B['Hm'][d], B['Hb'][d], B['ATm'][d], B['Lm'][d], B['SQ'][d], B['Ub'][d], B['Upad'][d]
        PA, PB, PS = B['PA'][d], B['PB'][d], B['PS'][d]
        kPA = [('PA', d, 0), ('PA', d, 1)]
        kL, kXU, kYH, kPS = ('PB', d), ('PB', d), ('PB', d), ('PS', d)
        kAR = [('AR', d, 0), ('AR', d, 1)]
        kATm = [('ATm', d, 0), ('ATm', d, 1)]
        kSQ = [('SQ', d, 0), ('SQ', d, 1)]
        PL = PB[:, 0:256].rearrange("p (h s) -> p h s", h=2)
        PXU = PB[:, 256:384]
        PYH = PB[:, 384:512]
        PSv = PS[:, :].rearrange("p (i s) -> p i s", i=4)
        YT = B['A1']
        sy.op('pool', lambda e: e.memset(Hm[:], 0.0), writes=[('Hm', d)])
        sy.op('pool', lambda e: e.memset(Hb[:], 0.0), writes=[('Hb', d)])
        order = range(NCH) if d == 0 else range(NCH - 1, -1, -1)
        for c in order:
            for h in range(2):
                self.mm(PA[h][:, 0:256], bT[:, c, :], AR[h][:, c, :], True, True, [('bT', d), kAR[h]], [kPA[h]])
                self.mm(PA[h][:, 256:512], kT[:, c, :], AR[h][:, c, :], True, True, [('kT', d), kAR[h]], [kPA[h]])
                self.mm(PL[:, h, :], AR[h][:, c, 0:128], bT[:, c, :], True, True, [('bT', d), kAR[h]], [kL])
            for h in range(2):
                self.tt('dve', ATm[h][:], PA[h][:, :], self.maskA[d], ALU.mult, [kPA[h], 'c_sb'], [kATm[h]])
            self.tt('dve', Lm[:], PL, self.maskL[d].unsqueeze(1).to_broadcast([128, 2, 128]), ALU.mult, [kL, 'c_sb'], [('Lm', d)])
            yield
            self.mm(PXU, AR[0][:, c, 0:128], Hb[:], True, False, [kAR[0], ('Hb', d)], [kXU])
            self.mm(PXU, AR[1][:, c, 0:128], Hb[:], False, False, [kAR[1], ('Hb', d)], [kXU])
            for h in range(2):
                hs = slice(h * 64, (h + 1) * 64)
                self.mm(PXU[:, hs], ATm[h][:, 256:384], Vpad[:, c, h, hs], False, h == 1, [kATm[h], 'Vpad'], [kXU])
            sy.op('act', lambda e: e.copy(out=Ub[:], in_=PXU), reads=[kXU], writes=[('Ub', d)])
            for h in range(2):
                self.mm(PSv[:, h, :], Lm[:, h, :], ATm[h][:, 0:128], True, True, [('Lm', d), kATm[h]], [kPS])
                self.mm(PSv[:, 2 + h, :], ATm[h][:, 0:128], Lm[:, h, :], True, True, [('Lm', d), kATm[h]], [kPS])
            sy.op('act', lambda e: e.copy(out=SQ[0][:], in_=PSv), reads=[kPS], writes=[kSQ[0]])
            yield
            for j in range(7):
                for h in range(2):
                    hs = slice(h * 64, (h + 1) * 64)
                    if j == 0:
                        Nj, nk = ATm[h][:, 0:128], kATm[h]
                    else:
                        Nj, nk = SQ[(j - 1) % 2][:, h, :], kSQ[(j - 1) % 2]
                    self.mm(PXU[:, hs], Nj, Ub[:, hs], True, True, [nk, ('Ub', d)], [kXU])
                self.tt('dve', Ub[:], PXU, Ub[:], ALU.add, [kXU, ('Ub', d)], [('Ub', d)])
                if j + 2 <= 6:
                    src, sk = SQ[j % 2], kSQ[j % 2]
                    for h in range(2):
                        self.mm(PSv[:, h, :], src[:, 2 + h, :], src[:, h, :], True, True, [sk], [kPS])
                        self.mm(PSv[:, 2 + h, :], src[:, h, :], src[:, 2 + h, :], True, True, [sk], [kPS])
                    sy.op('act', lambda e, j=j: e.copy(out=SQ[(j + 1) % 2][:], in_=PSv), reads=[kPS], writes=[kSQ[(j + 1) % 2]])
                yield
            for h in range(2):
                hs = slice(h * 64, (h + 1) * 64)
                sy.op('act', lambda e, h=h, hs=hs: e.copy(out=Upad[:, h, hs], in_=Ub[:, hs]), reads=[('Ub', d)], writes=[('Upad', d)])
            self.mm(PYH, Hb[:], AR[0][:, c, 128:256], True, False, [('Hb', d), kAR[0]], [kYH])
            self.mm(PYH, Hb[:], AR[1][:, c, 128:256], False, False, [('Hb', d), kAR[1]], [kYH])
            for h in range(2):
                self.mm(PYH, Upad[:, h, :], ATm[h][:, 128:256], False, False, [('Upad', d), kATm[h]], [kYH])
                self.mm(PYH, Vpad[:, c, h, :], ATm[h][:, 384:512], False, h == 1, ['Vpad', kATm[h]], [kYH])
            first = (d == 0 and c < NCH // 2) or (d == 1 and c >= NCH // 2)
            ys = YT[:, c * CH:(c + 1) * CH]
            if first:
                sy.op('act', lambda e, ys=ys: e.copy(out=ys, in_=PYH), reads=[kYH], writes=[('YT', c)])
            else:
                self.tt('dve', ys, PYH, ys, ALU.add, [kYH, ('YT', c)], [('YT', c)])
            self.mm(PXU, bh[:, c, :], Ub[:], True, False, [('bh', d), ('Ub', d)], [kXU])
            self.mm(PXU, kh[:, c, :], Vpad[:, c, 0, :], False, False, [('kh', d), 'Vpad'], [kXU])
            self.mm(PXU, kh[:, c, :], Vpad[:, c, 1, :], False, True, [('kh', d), 'Vpad'], [kXU])
            for h in range(2):
                hs = slice(h * 64, (h + 1) * 64)
                self.stt('dve', Hm[hs, hs], Hm[hs, hs], B['wc'][d][hs, c:c + 1], PXU[hs, hs], ALU.mult, ALU.add,
                         [('Hm', d), ('wc', d), kXU], [('Hm', d)])
            sy.op('act', lambda e: e.copy(out=Hb[:], in_=Hm[:]), reads=[('Hm', d)], writes=[('Hb', d)])
            yield

    def phase2_hp(self, slot, hp, B):
        sy = self.sy
        zT = self.zT
        c3 = lambda ap: ap.rearrange("p (c t) -> p c t", t=CH)
        wl = B['wl'][hp % 2]
        wk = ('wl', hp % 2)
        for i, name in enumerate(('w_up_f', 'w_up_b', 'a_up_f', 'a_up_b')):
            sy.dma('sp', wl[:, i, :], self.wb[name][:, hp * 128:(hp + 1) * 128], reads=[('wb', name)], writes=[wk])
        for j, (dst, dk, raw, rk) in enumerate(((B['zr'], 'zr', B['A8'], 'A8'), (B['zk'], 'zk', B['A7'], 'A7'), (B['zv'], 'zv', B['A6'], 'A6'))):
            f0 = j * D + hp * 128
            sy.dma('sp', raw[:], zT[slot, f0:f0 + 128, :], reads=[('zT', slot)], writes=[rk])
            self.shift(raw, dst, j * 32 + hp, 128, rk, dk)
        vb = B['A3'][:].bitcast(BF16)[:, 0:T]
        sy.op('act', lambda e: e.copy(out=vb, in_=B['zv'][:]), reads=['zv'], writes=['A3'])
        for g in range(NCH // 8):
            ps, pk = self.psum_rr(B)
            pv = ps[:, :].bitcast(BF16)[:, 0:1024].rearrange("p (j t) -> p j t", t=128)
            for j in range(8):
                c = g * 8 + j
                sy.op('pe', lambda e, c=c, j=j, pv=pv: e.transpose(out=pv[:, j, :], in_=vb[:, c * CH:(c + 1) * CH], identity=self.ident_bf[:]),
                      reads=['A3', 'ident_bf'], writes=[pk])
            for h in range(2):
                hs = slice(h * 64, (h + 1) * 64)
                X = self.evq()
                sy.op(X, (lambda e, g=g, pv=pv, h=h, hs=hs: e.tensor_copy(out=B['Vpad'][:, g * 8:(g + 1) * 8, h, hs], in_=pv[:, :, hs])) if X == 'dve'
                      else (lambda e, g=g, pv=pv, h=h, hs=hs: e.copy(out=B['Vpad'][:, g * 8:(g + 1) * 8, h, hs], in_=pv[:, :, hs])),
                      reads=[pk], writes=['Vpad'])
        self.ts('dve', B['kkn'][:], B['zk'][:], self.cpv(4, hp), None, ALU.mult, None, ['zk', 'cp_sb'], ['kkn'])
        sqb = B['A4'][:].bitcast(BF16)[:, 0:T]
        self.actf(sqb, B['kkn'][:], AF.Square, ['kkn'], ['A4'])
        for blk in range(4):
            bs = slice(blk * 512, (blk + 1) * 512)
            ps, pk = self.psum_rr(B)
            self.mm(ps[:, :], self.bones_bf[:], sqb[:, bs], True, True, ['bones_bf', 'A4'], [pk])
            self.ts('dve', B['A2'][:, bs], ps[:, :], 1e-12, None, ALU.max, None, [pk], ['A2'])
        self.actf(B['A2'][:], B['A2'][:], AF.Sqrt, ['A2'], ['A2'])
        sy.op('dve', lambda e: e.reciprocal(out=B['A2'][:], in_=B['A2'][:]), reads=['A2'], writes=['A2'])
        self.tt('dve', B['kkn'][:], B['kkn'][:], B['A2'][:], ALU.mult, ['kkn', 'A2'], ['kkn'])
        for d in range(2):
            self.phase2_prep_dir(slot, hp, d, B)
        gens = [self.chunk_gen(hp, 0, B), self.chunk_gen(hp, 1, B)]
        alive = [True, True]
        while any(alive):
            for i in range(2):
                if alive[i]:
                    try:
                        next(gens[i])
                    except StopIteration:
                        alive[i] = False
        YT = B['A1']
        ykeys = [('YT', c) for c in range(NCH)]
        sy.dma('sp', B['A8'][:], self.gT[slot, hp * 128:(hp + 1) * 128, :], reads=[('gT', slot)], writes=['A8'])
        sy.dma('sp', B['A7'][:], zT[slot, P1 + hp * 128:P1 + (hp + 1) * 128, :], reads=[('zT', slot)], writes=['A7'])
        sy.dma('sp', B['A6'][:], zT[slot, P2 + hp * 128:P2 + (hp + 1) * 128, :], reads=[('zT', slot)], writes=['A6'])
        sy.dma('sp', B['A4'][:], self.ybT[slot, hp * 128:(hp + 1) * 128, :], reads=[('ybT', slot)], writes=['A4'])
        for blk in range(4):
            bs = slice(blk * 512, (blk + 1) * 512)
            yk = ykeys[blk * 4:(blk + 1) * 4]
            ps, pk = self.psum_rr(B)
            self.mm(ps[:, :], self.bones_f, YT[:, bs], True, True, ['c_sb'] + yk, [pk])
            self.stt('dve', B['A2'][:, bs], ps[:, :], -1.0 / 64, YT[:, bs], ALU.mult, ALU.add, [pk] + yk, ['A2'])
            self.actf(B['A3'][:, bs], B['A2'][:, bs], AF.Square, ['A2'], ['A3'])
            ps2, pk2 = self.psum_rr(B)
            self.mm(ps2[:, :], self.bones_f, B['A3'][:, bs], True, True, ['c_sb', 'A3'], [pk2])
            self.ts('dve', B['tmp5'], ps2[:, :], 1.0 / 64, 64e-5, ALU.mult, ALU.add, [pk2], ['kkn'])
            self.actf(B['tmp5'], B['tmp5'], AF.Sqrt, ['kkn'], ['kkn'])
            sy.op('dve', lambda e: e.reciprocal(out=B['tmp5'], in_=B['tmp5']), reads=['kkn'], writes=['kkn'])
            self.tt('dve', B['A2'][:, bs], B['A2'][:, bs], B['tmp5'], ALU.mult, ['A2', 'kkn'], ['A2'])
            self.ts('dve', B['A2'][:, bs], B['A2'][:, bs], self.cpv(7, hp), self.cpv(8, hp), ALU.mult, ALU.add, ['A2', 'cp_sb'], ['A2'])
            ps3, pk3 = self.psum_rr(B)
            self.mm(ps3[:, :], self.bones_bf[:], B['rkb'][0][:, bs], True, False, ['bones_bf', ('rkb', 0)], [pk3])
            self.mm(ps3[:, :], self.bones_bf[:], B['rkb'][1][:, bs], False, True, ['bones_bf', ('rkb', 1)], [pk3])
            self.tt('dve', B['A3'][:, bs], ps3[:, :], B['zv'][:, bs], ALU.mult, [pk3, 'zv'], ['A3'])
            self.tt('dve', B['A2'][:, bs], B['A2'][:, bs], B['A3'][:, bs], ALU.add, ['A2', 'A3'], ['A2'])
        self.tt('dve', B['A2'][:], B['A2'][:], B['A8'][:], ALU.mult, ['A2', 'A8'], ['A2'])
        self.tt('dve', B['A2'][:], B['A2'][:], B['A7'][:], ALU.mult, ['A2', 'A7'], ['A2'])
        self.tt('dve', B['A4'][:], B['A4'][:], B['A6'][:], ALU.mult, ['A4', 'A6'], ['A4'])
        mb = B['A3'][:].bitcast(BF16)[:, 0:T]
        self.tt('dve', mb, B['A2'][:], B['A4'][:], ALU.add, ['A2', 'A4'], ['A3'])
        if self.t3[slot] == T:
            sy.dma('sp', self.mT[slot, hp * 128:(hp + 1) * 128, :], mb, reads=['A3'], writes=[('mT', slot)], dkey=('mT', slot))
        else:
            H = T // 2
            ms = B['A2'][:].bitcast(BF16)[:, 0:H]
            self.ts('dve', ms, mb[:, 0:H], self.sel_sb[:, 0:1], None, ALU.mult, None, ['A3', 'sel'], ['A2'])
            self.stt('dve', ms, mb[:, H:T], self.sel_sb[:, 1:2], ms, ALU.mult, ALU.add, ['A3', 'sel', 'A2'], ['A2'])
            sy.dma('sp', self.mT[slot, hp * 128:(hp + 1) * 128, :], ms, reads=['A2'], writes=[('mT', slot)], dkey=('mT', slot))

    def phase2(self, slot, B):
        self.phase2_partA(slot, B)
        self.phase2_pool(slot, B)
        for hp in range(self.nhp or 32):
            self.phase2_hp(slot, hp, B)

    def alloc_phase3(self):
        C = {}
        sy = self.sy
        C['hT'] = self.sb("p3_hT", [128, 32, 512], F32)
        C['hnb'] = self.sb("p3_hnb", [128, 8192], F32)
        C['Rb'] = self.sb("p3_Rb", [128, 8192], F32)
        C['KT'] = self.sb("p3_KT", [128, 32, 256], BF16)
        C['V'] = self.sb("p3_V", [128, 2, 4096], BF16)
        C['W'] = [self.sb("p3_W%d" % i, [128, 32, 256], BF16) for i in range(2)]
        C['Eb'] = self.sb("p3_E", [128, 512], F32)
        C['rz'] = self.sb("p3_rz", [128, 512], F32)
        C['gT'] = self.sb("p3_gT", [128, 160], F32)
        C['ones'] = self.sb("p3_ones", [128, 128], BF16)
        C['idf'] = self.sb("p3_idf", [128, 128], F32)
        C['sq'] = [self.sb("p3_sq%d" % i, [128, 4, 512], BF16) for i in range(2)]
        C['ps'] = [self.pt("p3_ps%d" % i, [128, 512], F32) for i in range(4)]
        C['ptr'] = [self.pt("p3_ptr%d" % i, [128, 4, 128], F32) for i in range(2)]
        C['pss'] = self.pt("p3_pss", [128, 512], F32)
        C['psz'] = self.pt("p3_psz", [128, 512], F32)
        hnbf = C['hnb'][:].bitcast(BF16)
        C['hn'] = hnbf.rearrange("p (k t) -> p k t", t=512)
        C['mn'] = hnbf[:, 0:8192].rearrange("p (k t) -> p k t", t=256)
        C['xs'] = [C['hnb'][:, 0:4096], C['hnb'][:, 4096:8192]]
        Rbf = C['Rb'][:].bitcast(BF16)
        C['QT'] = Rbf.rearrange("p (k t) -> p k t", t=512)
        C['E'] = C['Eb'][:].bitcast(BF16).rearrange("p (m t) -> p m t", t=512)
        C['psn'] = 0
        C['wn'] = 0
        sy.dma('sp', C['gT'][:], self.gainsT, writes=['gT'])
        sy.dma('sp', C['idf'][:], self.consts[:, 0:128], writes=['idf'])
        sy.op('pool', lambda e: e.memset(C['ones'][:], 1.0), writes=['ones'])
        return C

    HN = ['hnA', 'hnB']

    @staticmethod
    def hk(kc):
        return 'hnA' if kc < 16 else 'hnB'

    def ps_rr(self, C):
        C['psn'] = (C['psn'] + 1) % 4
        return C['ps'][C['psn']], ('ps', C['psn'])

    def wblock(self, C, name, r0, nkc, c0, ncol=256):
        C['wn'] = (C['wn'] + 1) % 2
        i = C['wn']
        buf = C['W'][i]
        key = ('W', i)
        self.sy.dma('sp', buf[:, 0:nkc, 0:ncol],
                    self.wb[name][r0:r0 + nkc * 128, c0:c0 + ncol].rearrange("(kc p) f -> p kc f", p=128),
                    reads=[('wb', name)], writes=[key])
        return buf, key

    def copy_ev(self, out, in_, reads, writes):
        X = self.evq()
        if X == 'dve':
            return self.sy.op('dve', lambda e: e.tensor_copy(out=out, in_=in_), reads=reads, writes=writes)
        return self.sy.op('act', lambda e: e.copy(out=out, in_=in_), reads=reads, writes=writes)

    def load_T(self, C, rows_fn, nsub, dst3):
        sy = self.sy
        for sub in range(nsub):
            xs = C['xs'][sub % 2]
            xk = self.HN[sub % 2]
            sy.dma('sp', xs, rows_fn(sub), writes=[xk])
            for g in range(8):
                pt = C['ptr'][g % 2]
                pk = ('ptr', g % 2)
                for j in range(4):
                    kc = g * 4 + j
                    sy.op('pe', lambda e, kc=kc, j=j, pt=pt, xs=xs: e.transpose(out=pt[:, j, :], in_=xs[:, kc * 128:(kc + 1) * 128],
                                                                               identity=C['idf'][:]),
                          reads=[xk, 'idf'], writes=[pk])
                self.copy_ev(dst3[:, g * 4:(g + 1) * 4, sub * 128:(sub + 1) * 128], pt[:, :, :], [pk],
                             [('hT', g * 4 + j) for j in range(4)])

    def fm_norm(self, C, src3, N, gidx, dst3, dkey_fn):
        sy = self.sy
        pss = C['pss']
        for g in range(8):
            sq = C['sq'][g % 2]
            sk = ('sq', g % 2)
            sy.op('act', lambda e, g=g, sq=sq: e.activation(out=sq[:, :, 0:N], in_=src3[:, g * 4:(g + 1) * 4, :], func=AF.Square),
                  reads=[('hT', g * 4 + j) for j in range(4)], writes=[sk])
            for j in range(4):
                self.mm(pss[:, 0:N], C['ones'][:], sq[:, j, 0:N], g == 0 and j == 0, g == 7 and j == 3, ['ones', sk], ['pss'])
        rz = C['rz']
        sy.op('act', lambda e: e.activation(out=rz[:, 0:N], in_=pss[:, 0:N], func=AF.Sqrt, bias=self.eps_sb[:], scale=1.0 / D),
              reads=['pss', 'eps'], writes=['rz'])
        sy.op('dve', lambda e: e.reciprocal(out=rz[:, 0:N], in_=rz[:, 0:N]), reads=['rz'], writes=['rz'])
        for kc in range(32):
            self.stt('dve', dst3[:, kc, :], src3[:, kc, :], C['gT'][:, gidx * 32 + kc:gidx * 32 + kc + 1], rz[:, 0:N],
                     ALU.mult, ALU.mult, [('hT', kc), 'gT', 'rz'], [dkey_fn(kc)])

    def proj_fm(self, C, name, N, rhs_fn, rkey_fn, evac_fn, r0=0, nkc=32, c0=0, nft=32):
        for b in range(0, nft, 2):
            W, wk = self.wblock(C, name, r0, nkc, c0 + b * 128)
            for j in range(2):
                ps, pk = self.ps_rr(C)
                for kc in range(nkc):
                    self.mm(ps[:, 0:N], W[:, kc, j * 128:(j + 1) * 128], rhs_fn(kc), kc == 0, kc == nkc - 1,
                            [wk, rkey_fn(kc)], [pk])
                evac_fn(b + j, ps, pk)

    def phase3_mem(self, slot, C):
        sy = self.sy
        memT = C['hT'][:, :, 0:256]
        self.load_T(C, lambda sub: self.mem[slot, sub * 128:(sub + 1) * 128, :], 2, memT)
        self.fm_norm(C, memT, 256, 2, C['mn'], lambda kc: 'hnA')
        mn = C['mn']
        KT, V = C['KT'], C['V']
        self.proj_fm(C, 'xk', 256, lambda kc: mn[:, kc, :], lambda kc: 'hnA',
                     lambda ft, ps, pk: self.copy_ev(KT[:, ft, :], ps[:, 0:256], [pk], ['KT']))
        for fb in range(16):
            W, wk = self.wblock(C, 'xv', 0, 32, fb * 256)
            for mt in range(2):
                ps, pk = self.ps_rr(C)
                for kc in range(32):
                    self.mm(ps[:, 0:256], mn[:, kc, mt * 128:(mt + 1) * 128], W[:, kc, :], kc == 0, kc == 31, [wk, 'hnA'], [pk])
                self.copy_ev(V[:, mt, fb * 256:(fb + 1) * 256], ps[:, 0:256], [pk], ['V'])

    def phase3_tile(self, slot, tok0, C):
        sy = self.sy
        hT, hn, QT, KT, V, E, rz = C['hT'], C['hn'], C['QT'], C['KT'], C['V'], C['E'], C['rz']
        hk = self.hk

        def add_evac(ft, ps, pk):
            self.tt('dve', hT[:, ft, :], ps[:, :], hT[:, ft, :], ALU.add, [pk, ('hT', ft)], [('hT', ft)])

        self.load_T(C, lambda sub: self.x3[slot, tok0 + sub * 128:tok0 + (sub + 1) * 128, :], 4, hT)
        for q in range(4):
            sy.dma('sp', hn[:, q * 8:(q + 1) * 8, :],
                   self.mT[slot, q * 1024:(q + 1) * 1024, tok0:tok0 + 512].rearrange("(kc p) t -> p kc t", p=128),
                   reads=[('mT', slot)], writes=[hk(q * 8)])
        self.proj_fm(C, 'w_out', 512, lambda kc: hn[:, kc, :], hk, add_evac)
        self.fm_norm(C, hT, 512, 1, hn, hk)
        self.proj_fm(C, 'xq', 512, lambda kc: hn[:, kc, :], hk,
                     lambda ft, ps, pk: self.copy_ev(QT[:, ft, :], ps[:, :], [pk], ['R']))
        for h in range(4):
            for mt in range(2):
                ps, pk = self.ps_rr(C)
                for j in range(8):
                    ft = h * 8 + j
                    self.mm(ps[:, :], KT[:, ft, mt * 128:(mt + 1) * 128], QT[:, ft, :], j == 0, j == 7, ['KT', 'R'], [pk])
                sy.op('act', lambda e, mt=mt, ps=ps: e.activation(out=E[:, mt, :], in_=ps[:, :], func=AF.Exp, scale=1.0 / 32.0),
                      reads=[pk], writes=['E'])
            for mt in range(2):
                self.mm(C['psz'][:, :], C['ones'][:], E[:, mt, :], mt == 0, mt == 1, ['ones', 'E'], ['psz'])
            sy.op('dve', lambda e: e.reciprocal(out=rz[:, :], in_=C['psz'][:, :]), reads=['psz'], writes=['rz'])
            for j in range(8):
                dt_ = h * 8 + j
                ps, pk = self.ps_rr(C)
                for mt in range(2):
                    self.mm(ps[:, :], V[:, mt, dt_ * 128:(dt_ + 1) * 128], E[:, mt, :], mt == 0, mt == 1, ['V', 'E'], [pk])
                self.tt('dve', hn[:, dt_, :], ps[:, :], rz[:, :], ALU.mult, [pk, 'rz'], [hk(dt_)])
        self.proj_fm(C, 'xo', 512, lambda kc: hn[:, kc, :], hk, add_evac)
        self.fm_norm(C, hT, 512, 3, hn, hk)
        aT = QT
        sg = [C['Eb'], C['rz']]
        sgk = ['E', 'rz']
        t0 = 0
        for npair in (11, 11, 11, 10):
            for p in range(npair):
                j0 = t0 + 2 * p
                Wg, wgk = self.wblock(C, 'ffn_w13', 0, 32, j0 * 128)
                for j in range(2):
                    ps, pk = self.ps_rr(C)
                    for kc in range(32):
                        self.mm(ps[:, :], Wg[:, kc, j * 128:(j + 1) * 128], hn[:, kc, :], kc == 0, kc == 31, [wgk, hk(kc)], [pk])
                    sy.op('act', lambda e, j=j, ps=ps: e.activation(out=sg[j][:, :], in_=ps[:, :], func=AF.Silu),
                          reads=[pk], writes=[sgk[j]])
                Wu, wuk = self.wblock(C, 'ffn_w13', 0, 32, FF + j0 * 128)
                for j in range(2):
                    ps, pk = self.ps_rr(C)
                    for kc in range(32):
                        self.mm(ps[:, :], Wu[:, kc, j * 128:(j + 1) * 128], hn[:, kc, :], kc == 0, kc == 31, [wuk, hk(kc)], [pk])
                    self.tt('dve', aT[:, 2 * p + j, :], ps[:, :], sg[j][:, :], ALU.mult, [pk, sgk[j]], ['R'])
            nt = 2 * npair
            self.proj_fm(C, 'ffn_w2', 512, lambda kc: aT[:, kc, :], lambda kc: 'R', add_evac, r0=t0 * 128, nkc=nt)
            t0 += nt
        self.fm_norm(C, hT, 512, 4, hT, lambda kc: ('hT', kc))
        for sub in range(4):
            ys = C['xs'][sub % 2]
            yk = self.HN[sub % 2]
            for g in range(8):
                pt = C['ptr'][g % 2]
                pk = ('ptr', g % 2)
                for j in range(4):
                    kc = g * 4 + j
                    sy.op('pe', lambda e, kc=kc, j=j, pt=pt, sub=sub: e.transpose(out=pt[:, j, :], in_=hT[:, kc, sub * 128:(sub + 1) * 128],
                                                                                 identity=C['idf'][:]),
                          reads=[('hT', kc), 'idf'], writes=[pk])
                self.copy_ev(ys[:, g * 512:(g + 1) * 512].rearrange("p (j t) -> p j t", t=128), pt[:, :, :], [pk], [yk])
            sy.dma('sp', self.y[slot, tok0 + sub * 128:tok0 + (sub + 1) * 128, :], ys, reads=[yk], writes=[('y', slot)], dkey=('y', slot))

    def phase3(self, slot, C):
        self.phase3_mem(slot, C)
        for tt in range(self.ntile3 or (self.t3[slot] // 512)):
            self.phase3_tile(slot, tt * 512, C)

    def barrier(self):
        for X in ('pe', 'dve', 'act', 'pool', 'sp'):
            self.sy.drain(X, skip_wb=True)

    def build(self):
        nc = self.nc
        sy = self.sy
        self.gstack = contextlib.ExitStack()
        self.stack = self.gstack
        self.eps_sb = self.sb("eps_sb", [128, 1], F32)
        self.ident_bf = self.sb("ident_bf", [128, 128], BF16)
        self.bones_bf = self.sb("bones_bf", [128, 128], BF16)
        sy.op('dve', lambda e: e.memset(self.eps_sb[:], 1e-6), writes=['eps'])
        if 0 in self.phases:
            self.phase0(False)
        with contextlib.ExitStack() as st:
            self.stack = st
            tmp = self.sb("c_tmp", [128, 256], F32)
            sy.dma('sp', tmp[:], self.consts[:, 0:256], writes=['c_tmp'])
            sy.op('dve', lambda e: e.tensor_copy(out=self.ident_bf[:], in_=tmp[:, 0:128]), reads=['c_tmp'], writes=['ident_bf'])
            sy.op('dve', lambda e: e.tensor_copy(out=self.bones_bf[:], in_=tmp[:, 128:256]), reads=['c_tmp'], writes=['bones_bf'])
            self.barrier()
        self.stack = self.gstack
        if 1 in self.phases:
            with contextlib.ExitStack() as st:
                self.stack = st
                A = self.alloc_phase1()
                self.phase1_all(A)
                self.barrier()
            self.stack = self.gstack
        if 0 in self.phases:
            self.phase0(True)
        if 2 in self.phases:
            with contextlib.ExitStack() as st:
                self.stack = st
                self.load_consts()
                B = self.alloc_phase2()
                B['pp'] = B['zr_full']
                B['pa'] = B['zk_full']
                B['pb'] = B['zv_full']
                for slot in range(self.nslot):
                    self.phase2(slot, B)
                self.barrier()
            self.stack = self.gstack
        if 3 in self.phases:
            with contextlib.ExitStack() as st:
                self.stack = st
                C = self.alloc_phase3()
                for slot in range(self.nslot):
                    self.phase3(slot, C)
                self.barrier()
            self.stack = self.gstack
        sy.drain('sp')
        return nc


def make_consts():
    c = np.zeros((128, 1664), np.float32)
    p = np.arange(128)
    c[:, 0:128] = np.eye(128, dtype=np.float32)
    c[:, 128:256] = (p[:, None] // 64 == p[None, :] // 64).astype(np.float32)
    s = p[:, None]
    t = p[None, :]
    strict_f = (s < t).astype(np.float32)
    incl_f = (s <= t).astype(np.float32)
    strict_b = (s > t).astype(np.float32)
    incl_b = (s >= t).astype(np.float32)
    c[:, 256:768] = np.concatenate([strict_f, incl_f, strict_f, incl_f], axis=1)
    c[:, 768:1280] = np.concatenate([strict_b, incl_b, strict_b, incl_b], axis=1)
    c[:, 1280:1408] = strict_b
    c[:, 1408:1536] = strict_f
    c[:, 1536:1664] = 1.0
    c[:, 1536] = 0.0
    return c


def prep_shared(inp):
    sh = {}
    sh['gains'] = np.ascontiguousarray(np.stack([inp['norm_mix_g'][0], inp['norm_x_g'][0], inp['norm_mem_g'][0],
                                                 inp['norm_ffn_g'][0], inp['norm_final_g']], axis=0).astype(np.float32))
    sh['gainsT'] = np.ascontiguousarray(sh['gains'].reshape(5, 32, 128).transpose(2, 0, 1).reshape(128, 160))
    sw = np.zeros((3, 102 * 128), np.float32)
    sw[:, :RW] = inp['shift_w'][0]
    sh['taps'] = np.ascontiguousarray(sw.reshape(3, 102, 128).transpose(2, 1, 0).reshape(128, 306))
    vecs = [inp['w0_f'][0], inp['w0_b'][0], inp['a0_f'][0], inp['a0_b'][0], inp['k_k'][0], inp['k_a'][0],
            inp['r_k'][0].reshape(-1), inp['ln_x_g'][0], inp['ln_x_b'][0], inp['pool_scale'][0]]
    cp = np.stack([v.reshape(32, 128).T for v in vecs], axis=1)
    sh['cp'] = np.ascontiguousarray(cp.reshape(128, 320).astype(np.float32))
    sh['consts'] = make_consts()
    t = np.arange(T)
    ic = np.zeros((4, T), np.float32)
    for gi, win in enumerate((2, 4, 8, 16)):
        lo = np.clip(t - win // 2, 0, T)
        hi = np.clip(t + win - win // 2, 0, T)
        ic[gi] = 1.0 / (hi - lo).astype(np.float32)
    sh['invcnt'] = ic
    wmap = dict(w_in=inp['w_in'][0], w_out=inp['w_out'][0], xq=inp['xq'][0], xk=inp['xk'][0], xv=inp['xv'][0],
                xo=inp['xo'][0], ffn_w13=inp['ffn_w13'][0], ffn_w2=inp['ffn_w2'][0],
                pool_w=inp['pool_w'][0].reshape(PW, 1024), g_up=inp['g_up'][0],
                w_up_f=inp['w_up_f'][0], w_up_b=inp['w_up_b'][0], a_up_f=inp['a_up_f'][0], a_up_b=inp['a_up_b'][0])
    sh.update(wmap)
    return sh


WEIGHT_NAMES = ('w_in', 'w_out', 'xq', 'xk', 'xv', 'xo', 'ffn_w13', 'ffn_w2', 'pool_w', 'g_up',
                'w_up_f', 'w_up_b', 'a_up_f', 'a_up_b')


def kernel(**inputs):
    inp = {k: np.asarray(v) for k, v in inputs.items()}
    sh = prep_shared(inp)
    k = K()
    nc = k.build()
    in_maps = []
    for c in range(8):
        m = {"x": np.stack([inp['x_prompt'][c], inp['x_sample'][c % 4]], axis=0),
             "mem": np.stack([inp['mem_prompt'][c], inp['mem_sample'][c % 4]], axis=0)}
        hsel = c // 4
        m["x1h"] = np.ascontiguousarray(inp['x_sample'][c % 4][hsel * (T // 2):(hsel + 1) * (T // 2)])
        s = np.zeros((128, 2), np.float32)
        s[:, hsel] = 1.0
        m["sel"] = s
        for n in ('gains', 'gainsT', 'taps', 'cp', 'consts', 'invcnt') + WEIGHT_NAMES:
            m[n] = sh[n]
        in_maps.append(m)
    res = run_bass_kernel_spmd(nc, in_maps, core_ids=list(range(8)))
    y_prompt = np.stack([np.asarray(res.results[c]["y0"]) for c in range(8)], axis=0).astype(np.float32, copy=False)
    y_sample = np.stack([np.concatenate([np.asarray(res.results[c]["y1"]), np.asarray(res.results[c + 4]["y1"])], axis=0)
                         for c in range(4)], axis=0).astype(np.float32, copy=False)
    return (y_prompt, y_sample)
```

```python
import contextlib
import numpy as np
import ml_dtypes
import concourse.bass as bass
import concourse.mybir as mybir
from concourse.bass_utils import run_bass_kernel_spmd

F32 = mybir.dt.float32
BF16 = mybir.dt.bfloat16
AF = mybir.ActivationFunctionType
ALU = mybir.AluOpType
AX = mybir.AxisListType

D = 4096
T = 2048
NSLOT = 2
NMEM = 256
RW = 13024
PW = 2048
INC = 23264
FF = 11008
CH = 128
NCH = T // CH
P0 = RW
P1 = RW + PW
P2 = P1 + D
NEG_EXP_HALF = -0.6065306597126334


class Sy:
    SEG = 30000

    def __init__(self, nc):
        self.nc = nc
        self.eng = {'pe': nc.tensor, 'dve': nc.vector, 'act': nc.scalar, 'pool': nc.gpsimd, 'sp': nc.sync}
        self.cnt = {e: 0 for e in self.eng}
        self.segs = {e: [] for e in self.eng}
        self.waited = {e: {} for e in self.eng}
        self.last_w = {}
        self.readers = {}
        self.dsem = {}
        self.retired = []
        self.nsem = 0

    def _newsem(self, name):
        self.nsem += 1
        return self.nc.alloc_semaphore(name + "_%d" % self.nsem)

    def _wait(self, X, tok):
        if tok is None:
            return
        if tok[0] == 'e':
            _, E, n = tok
            if E == X and X == 'pe':
                return
            if self.waited[X].get(E, 0) >= n:
                return
            self.waited[X][E] = n
            seg = (n - 1) // self.SEG
            self.eng[X].wait_ge(self.segs[E][seg], (n - 1) % self.SEG + 1)
        else:
            _, sem, v, sid = tok
            if self.waited[X].get(sid, 0) >= v:
                return
            self.waited[X][sid] = v
            self.eng[X].wait_ge(sem, v)

    def _deps(self, X, reads, writes):
        for k in reads:
            self._wait(X, self.last_w.get(k))
        for k in writes:
            self._wait(X, self.last_w.get(k))
            for tk in self.readers.get(k, ()):
                self._wait(X, tk)

    def _record(self, tok, reads, writes):
        for k in reads:
            lst = self.readers.setdefault(k, [])
            if tok[0] == 'e':
                lst[:] = [t for t in lst if not (t[0] == 'e' and t[1] == tok[1])]
            lst.append(tok)
        for k in writes:
            self.last_w[k] = tok
            self.readers[k] = []

    def op(self, X, fn, reads=(), writes=()):
        self._deps(X, reads, writes)
        ins = fn(self.eng[X])
        n = self.cnt[X] + 1
        seg = (n - 1) // self.SEG
        while len(self.segs[X]) <= seg:
            self.segs[X].append(self._newsem("pg_" + X))
        ins.then_inc(self.segs[X][seg], 1)
        self.cnt[X] = n
        self._record(('e', X, n), reads, writes)
        return ins

    def dma(self, X, out, in_, reads=(), writes=(), dkey=None):
        self._deps(X, reads, writes)
        if dkey is None:
            dkey = writes[0] if writes else reads[0]
        ent = self.dsem.get(dkey)
        if ent is None or ent[1] + 16 > self.SEG:
            if ent is not None:
                self.retired.append(('d', ent[0], ent[1], ent[2]))
            ent = [self._newsem("dm"), 0, self.nsem]
            self.dsem[dkey] = ent
        ent[1] += 16
        self.eng[X].dma_start(out=out, in_=in_).then_inc(ent[0], 16)
        tok = ('d', ent[0], ent[1], ent[2])
        self._record(tok, reads, writes)
        return tok

    def drain(self, X, skip_wb=False):
        iswb = lambda k: skip_wb and isinstance(k, tuple) and k[0] == 'wb'
        for k, tk in list(self.last_w.items()):
            if not iswb(k):
                self._wait(X, tk)
        for k, lst in list(self.readers.items()):
            if iswb(k):
                continue
            for tk in lst:
                self._wait(X, tk)
        for tk in self.retired:
            self._wait(X, tk)
        for E in self.eng:
            if E != X and self.cnt[E] > 0:
                self._wait(X, ('e', E, self.cnt[E]))


def feature_tiles():
    tiles = []
    f = 0
    while f < RW:
        w = min(128, RW - f)
        tiles.append((f, w, 'rw'))
        f += w
    for seg0, n, kind in ((P0, PW, 'pool'), (P1, D, 'gate'), (P2, D, 'gate')):
        for i in range(n // 128):
            tiles.append((seg0 + i * 128, 128, kind))
    return tiles


def feature_blocks():
    tl = feature_tiles()
    blocks = []
    cur = []
    for t in tl:
        if cur and (len(cur) == 4 or cur[-1][2] != t[2] or cur[-1][1] != 128):
            blocks.append(cur)
            cur = []
        cur.append(t)
    if cur:
        blocks.append(cur)
    return blocks


class SlotAP:
    def __init__(self, aps):
        self.aps = aps

    def __getitem__(self, idx):
        return self.aps[idx[0]][idx[1:]]


class K:
    def __init__(self, nslot=NSLOT, phases=(0, 1, 2, 3), debug=False, ntile1=None, nhp=None, ntile3=None, half=True):
        self.nslot = nslot
        self.phases = phases
        self.debug = debug
        self.ntile1 = ntile1
        self.nhp = nhp
        self.ntile3 = ntile3
        nc = bass.Bass("TRN2", target_bir_lowering=False)
        self.nc = nc
        self.sy = Sy(nc)
        self.q = 0
        self.ee = 0
        ext_in = lambda name, shape, dt=F32: nc.dram_tensor(name, list(shape), dt, kind="ExternalInput").ap()
        self.x = ext_in("x", [nslot, T, D])
        self.mem = ext_in("mem", [nslot, NMEM, D])
        self.w = {}
        wshapes = dict(w_in=[D, INC], w_out=[D, D], xq=[D, D], xk=[D, D], xv=[D, D], xo=[D, D],
                       ffn_w13=[D, 2 * FF], ffn_w2=[FF, D], pool_w=[PW, 1024], g_up=[480, D],
                       w_up_f=[128, D], w_up_b=[128, D], a_up_f=[128, D], a_up_b=[128, D])
        self.wshapes = wshapes
        need = self._needed_weights()
        for name in need:
            self.w[name] = ext_in(name, wshapes[name])
        self.gains = ext_in("gains", [5, D])
        self.gainsT = ext_in("gainsT", [128, 160])
        self.taps = ext_in("taps", [128, 102 * 3])
        self.cp = ext_in("cp", [128, 10 * 32])
        self.consts = ext_in("consts", [128, 1664])
        kind_scr = "ExternalOutput" if debug else "Internal"
        self.wb = {}
        for name in need:
            self.wb[name] = nc.dram_tensor("wb_" + name, wshapes[name], BF16, kind="Internal").ap()
        if 1 in phases:
            self.zT = SlotAP([nc.dram_tensor("zT%d" % s, [INC, T], F32, kind=kind_scr).ap() for s in range(nslot)])
        elif 2 in phases:
            self.zT = SlotAP([ext_in("zT%d" % s, [INC, T]) for s in range(nslot)])
        self.half = half and nslot == 2
        self.t3 = [T] * nslot
        if self.half:
            self.t3[1] = T // 2
            self.sel = ext_in("sel", [128, 2])
            self.x1h = ext_in("x1h", [T // 2, D])
        if 2 in phases:
            self.mT = SlotAP([nc.dram_tensor("mT%d" % s, [D, self.t3[s]], BF16, kind=kind_scr).ap() for s in range(nslot)])
        elif 3 in phases:
            self.mT = SlotAP([ext_in("mT%d" % s, [D, self.t3[s]], BF16) for s in range(nslot)])
        if 3 in phases:
            self.y = SlotAP([nc.dram_tensor("y%d" % s, [self.t3[s], D], F32, kind="ExternalOutput").ap() for s in range(nslot)])
            self.x3 = SlotAP([self.x[s] if self.t3[s] == T else self.x1h for s in range(nslot)])
        if 2 in phases:
            self.gT = nc.dram_tensor("gT", [nslot, D, T], F32, kind="Internal").ap()
            self.ybT = nc.dram_tensor("ybT", [nslot, D, T], F32, kind="Internal").ap()
            self.invcnt = ext_in("invcnt", [4, T])

    def _needed_weights(self):
        need = []
        if 1 in self.phases:
            need += ['w_in']
        if 2 in self.phases:
            need += ['pool_w', 'g_up', 'w_up_f', 'w_up_b', 'a_up_f', 'a_up_b']
        if 3 in self.phases:
            need += ['w_out', 'xq', 'xk', 'xv', 'xo', 'ffn_w13', 'ffn_w2']
        return need

    def dq(self):
        return 'sp'

    def evq(self):
        self.ee ^= 1
        return 'dve' if self.ee else 'act'

    def sb(self, name, shape, dt):
        return self.stack.enter_context(self.nc.sbuf_tensor(name, list(shape), dt))

    def pt(self, name, shape, dt):
        return self.stack.enter_context(self.nc.psum_tensor(name, list(shape), dt))

    LATE = ('w_out', 'xq', 'xk', 'xv', 'xo', 'ffn_w13', 'ffn_w2')

    def phase0(self, late):
        sy = self.sy
        thunks = []
        for name in self._needed_weights():
            if (name in self.LATE) != late:
                continue
            src = self.w[name]
            dst = self.wb[name]
            rows = self.wshapes[name][0]
            if name == 'w_in':
                for bi, blk in enumerate(feature_blocks()):
                    b0 = blk[0][0]
                    bw = sum(t[1] for t in blk)
                    for r in range(0, rows, 1024):
                        thunks.append(lambda r=r, b0=b0, bw=bw, bi=bi, dst=dst, src=src: sy.dma(
                            'pool', dst[r:r + 1024, b0:b0 + bw], src[r:r + 1024, b0:b0 + bw],
                            writes=[('wb', 'w_in', bi)], dkey=('wbc', bi % 8)))
                continue
            step = 128 if self.wshapes[name][1] > 8192 else 512
            r = 0
            while r < rows:
                n = min(step, rows - r)
                thunks.append(lambda r=r, n=n, dst=dst, src=src, name=name: sy.dma(
                    'pool', dst[r:r + n, :], src[r:r + n, :], writes=[('wb', name)], dkey=('wb', name)))
                r += n
        return thunks

    def load_consts(self):
        sy = self.sy
        self.c_sb = self.sb("c_sb", [128, 1664], F32)
        sy.dma('sp', self.c_sb[:], self.consts, writes=['c_sb'])
        self.ident_f = self.c_sb[:, 0:128]
        self.bones_f = self.c_sb[:, 128:256]
        self.maskA = {0: self.c_sb[:, 256:768], 1: self.c_sb[:, 768:1280]}
        self.maskL = {0: self.c_sb[:, 1280:1408], 1: self.c_sb[:, 1408:1536]}
        self.scanm = self.c_sb[:, 1536:1536 + 128]
        if self.half:
            self.sel_sb = self.sb("sel_sb", [128, 2], F32)
            sy.dma('sp', self.sel_sb[:], self.sel, writes=['sel'])
        self.taps_sb = self.sb("taps_sb", [128, 306], F32)
        sy.dma('sp', self.taps_sb[:], self.taps, writes=['taps_sb'])
        self.cp_sb = self.sb("cp_sb", [128, 352], F32)
        sy.dma('sp', self.cp_sb[:, 0:320], self.cp, writes=['cp_sb'])
        sy.op('dve', lambda e: e.tensor_scalar(out=self.cp_sb[:, 320:352], in0=self.cp_sb[:, 5 * 32:6 * 32], scalar1=-1.0,
                                               scalar2=1.0, op0=ALU.mult, op1=ALU.add), reads=['cp_sb'], writes=['cp_sb'])

    def cpv(self, j, hp):
        return self.cp_sb[:, j * 32 + hp:j * 32 + hp + 1]

    def rms_tok(self, src_ap, src_key, gbc, out_bf, out_key, scr, tag):
        sy = self.sy
        sq, ss, rstd = scr['sq'], scr['ss'], scr['rstd']
        sy.op('act', lambda e: e.activation(out=sq[:], in_=src_ap, func=AF.Square, accum_out=ss[:]),
              reads=[src_key], writes=[tag + 'sq', tag + 'ss'])
        sy.op('act', lambda e: e.activation(out=rstd[:], in_=ss[:], func=AF.Sqrt, bias=self.eps_sb[:], scale=1.0 / D),
              reads=[tag + 'ss', 'eps'], writes=[tag + 'rstd'])
        sy.op('dve', lambda e: e.reciprocal(out=rstd[:], in_=rstd[:]), reads=[tag + 'rstd'], writes=[tag + 'rstd'])
        sy.op('dve', lambda e: e.scalar_tensor_tensor(out=out_bf, in0=src_ap, scalar=rstd[:, 0:1], in1=gbc[:],
                                                      op0=ALU.mult, op1=ALU.mult),
              reads=[src_key, tag + 'rstd', 'gbc'], writes=[out_key])

    def transpose_to_fm(self, src_bf, src_key, dst, dst_key, col0, pst, pst_key, nkc=32):
        sy = self.sy
        for g in range(nkc // 8):
            pk = (pst_key, g % 2)
            pt = pst[g % 2]
            for j in range(8):
                kc = g * 8 + j
                sy.op('pe', lambda e, kc=kc, j=j: e.transpose(out=pt[:, j, :], in_=src_bf[:, kc * 128:(kc + 1) * 128],
                                                             identity=self.ident_bf[:]),
                      reads=[src_key, 'ident_bf'], writes=[pk])
            sy.op(self.evq(), lambda e, g=g: e.tensor_copy(out=dst[:, g * 8:(g + 1) * 8, col0:col0 + 128], in_=pt[:, :, :])
                  if e is self.nc.vector else e.copy(out=dst[:, g * 8:(g + 1) * 8, col0:col0 + 128], in_=pt[:, :, :]),
                  reads=[pk], writes=[dst_key])

    def p1_load_w(self, A, gi):
        if gi >= len(A['items']):
            return
        bi = A['items'][gi][2]
        blk = A['blocks'][bi]
        b0 = blk[0][0]
        bw = sum(t[1] for t in blk)
        self.sy.dma('sp', A['wt'][gi % 3][:, :, 0:bw], self.wb['w_in'][:, b0:b0 + bw].rearrange("(kc p) f -> p kc f", p=128),
                    reads=[('wb', 'w_in', bi)], writes=[('wt', gi % 3)])

    def phase1_all(self, A):
        sy = self.sy
        blocks = feature_blocks()
        A['blocks'] = blocks
        ntile = self.ntile1 or (T // 512)
        A['items'] = [(slot, tt, bi) for slot in range(self.nslot) for tt in range(ntile) for bi in range(len(blocks))]
        self.p1_load_w(A, 0)
        self.p1_load_w(A, 1)
        for gi, (slot, tt, bi) in enumerate(A['items']):
            tok0 = tt * 512
            if bi == 0:
                for sub in range(4):
                    xs = A['xs']
                    sy.dma('sp', xs[:], self.x[slot, tok0 + sub * 128: tok0 + (sub + 1) * 128, :], writes=['xs'])
                    self.rms_tok(xs[:], 'xs', A['gbc'], A['xnb'][:], 'xnb', A, 'p1')
                    self.transpose_to_fm(A['xnb'], 'xnb', A['xnT'], 'xnT', sub * 128, A['pst'], 'pst')
            self.p1_load_w(A, gi + 2)
            blk = blocks[bi]
            b0 = blk[0][0]
            bw = sum(t[1] for t in blk)
            wbuf = A['wt'][gi % 3]
            wkey = ('wt', gi % 3)
            zs = A['zst'][gi % 2]
            zkey = ('zst', gi % 2)
            for ti, (f0, fw, kind) in enumerate(blk):
                A['psn'] = (A['psn'] + 1) % 4
                ps = A['ps'][A['psn']]
                pk = ('ps', A['psn'])
                o = f0 - b0
                for kc in range(32):
                    sy.op('pe', lambda e, kc=kc, o=o, fw=fw, ps=ps: e.matmul(ps[0:fw, :], lhsT=wbuf[:, kc, o:o + fw],
                                                                            rhs=A['xnT'][:, kc, :], start=(kc == 0), stop=(kc == 31)),
                          reads=[wkey, 'xnT'], writes=[pk])
                if kind == 'gate':
                    sy.op('act', lambda e, ti=ti, fw=fw, ps=ps: e.activation(out=zs[0:fw, ti, :], in_=ps[0:fw, :], func=AF.Sigmoid),
                          reads=[pk], writes=[zkey])
                else:
                    sy.op('dve', lambda e, ti=ti, fw=fw, ps=ps: e.tensor_copy(out=zs[0:fw, ti, :], in_=ps[0:fw, :]),
                          reads=[pk], writes=[zkey])
            if all(t[1] == 128 for t in blk):
                sy.dma('sp', self.zT[slot, b0:b0 + bw, tok0:tok0 + 512].rearrange("(ft p) t -> p ft t", p=128),
                       zs[:, 0:len(blk), :], reads=[zkey], writes=[('zT', slot)], dkey=('zT', slot))
            else:
                for ti, (f0, fw, kind) in enumerate(blk):
                    sy.dma('sp', self.zT[slot, f0:f0 + fw, tok0:tok0 + 512], zs[0:fw, ti, :],
                           reads=[zkey], writes=[('zT', slot)], dkey=('zT', slot))

    def alloc_phase1(self):
        A = {}
        A['xs'] = self.sb("p1_xs", [128, D], F32)
        A['sq'] = self.sb("p1_sq", [128, D], BF16)
        A['ss'] = self.sb("p1_ss", [128, 1], F32)
        A['rstd'] = self.sb("p1_rstd", [128, 1], F32)
        A['xnb'] = self.sb("p1_xnb", [128, D], BF16)
        A['xnT'] = self.sb("p1_xnT", [128, 32, 512], BF16)
        A['gbc'] = self.sb("p1_gbc", [128, D], F32)
        A['wt'] = [self.sb("p1_wt%d" % i, [128, 32, 512], BF16) for i in range(3)]
        A['zst'] = [self.sb("p1_zst%d" % i, [128, 4, 512], F32) for i in range(2)]
        A['pst'] = [self.pt("p1_pst%d" % i, [128, 8, 128], BF16) for i in range(2)]
        A['ps'] = [self.pt("p1_ps%d" % i, [128, 512], F32) for i in range(4)]
        A['psn'] = 0
        self.sy.dma('sp', A['gbc'][:], self.gains[0:1, :].to_broadcast([128, D]), writes=['gbc'])
        return A


    def tt(self, X, out, a, b, op, reads, writes):
        return self.sy.op(X, lambda e: e.tensor_tensor(out=out, in0=a, in1=b, op=op), reads=reads, writes=writes)

    def ts(self, X, out, a, s1, s2, op0, op1, reads, writes):
        if s2 is None:
            return self.sy.op(X, lambda e: e.tensor_scalar(out=out, in0=a, scalar1=s1, scalar2=None, op0=op0), reads=reads, writes=writes)
        return self.sy.op(X, lambda e: e.tensor_scalar(out=out, in0=a, scalar1=s1, scalar2=s2, op0=op0, op1=op1), reads=reads, writes=writes)

    def stt(self, X, out, a, sc, b, op0, op1, reads, writes):
        return self.sy.op(X, lambda e: e.scalar_tensor_tensor(out=out, in0=a, scalar=sc, in1=b, op0=op0, op1=op1), reads=reads, writes=writes)

    def actf(self, out, a, func, reads, writes, bias=0.0, scale=1.0):
        return self.sy.op('act', lambda e: e.activation(out=out, in_=a, func=func, bias=bias, scale=scale), reads=reads, writes=writes)

    def mm(self, out, lhsT, rhs, start, stop, reads, writes):
        return self.sy.op('pe', lambda e: e.matmul(out, lhsT=lhsT, rhs=rhs, start=start, stop=stop), reads=reads, writes=writes)

    def alloc_phase2(self):
        B = {}
        for n in ('zr', 'zk', 'zv', 'kkn', 'A1', 'A2', 'A3', 'A4', 'A6', 'A7', 'A8'):
            full = self.sb("p2_" + n, [128, T + 16], F32)
            B[n + '_full'] = full
            B[n] = full[:, 0:T]
        B['lw'] = self.sb("p2_lw", [128, T], BF16)
        B['la'] = self.sb("p2_la", [128, T], BF16)
        B['smask'] = self.sb("p2_smask", [128, T], BF16)
        B['wl'] = [self.sb("p2_wl%d" % i, [128, 8, 128], BF16) for i in range(2)]
        B['rkb'] = [self.sb("p2_rkb%d" % i, [128, T], BF16) for i in range(2)]
        B['AR'] = [[self.sb("p2_AR%d%d" % (d, h), [128, NCH, 256], BF16) for h in range(2)] for d in range(2)]
        B['bT'] = [self.sb("p2_bT%d" % d, [128, NCH, 128], BF16) for d in range(2)]
        B['kT'] = [self.sb("p2_kT%d" % d, [128, NCH, 128], BF16) for d in range(2)]
        B['bh'] = [self.sb("p2_bh%d" % d, [128, NCH, 128], BF16) for d in range(2)]
        B['kh'] = [self.sb("p2_kh%d" % d, [128, NCH, 128], BF16) for d in range(2)]
        B['Vpad'] = self.sb("p2_Vpad", [128, NCH, 2, 128], BF16)
        B['wc'] = [self.sb("p2_wc%d" % d, [128, NCH], F32) for d in range(2)]
        B['Hm'] = [self.sb("p2_Hm%d" % d, [128, 128], F32) for d in range(2)]
        B['Hb'] = [self.sb("p2_Hb%d" % d, [128, 128], BF16) for d in range(2)]
        B['ATm'] = [[self.sb("p2_ATm%d%d" % (d, h), [128, 512], BF16) for h in range(2)] for d in range(2)]
        B['Lm'] = [self.sb("p2_Lm%d" % d, [128, 2, 128], BF16) for d in range(2)]
        B['SQ'] = [[self.sb("p2_SQ%d%d" % (d, i), [128, 4, 128], BF16) for i in range(2)] for d in range(2)]
        B['Ub'] = [self.sb("p2_Ub%d" % d, [128, 128], BF16) for d in range(2)]
        B['Upad'] = [self.sb("p2_Upad%d" % d, [128, 2, 128], BF16) for d in range(2)]
        B['tmp5'] = B['kkn'][:, 0:512]
        B['PA'] = [[self.pt("p2_PA%d%d" % (d, h), [128, 512], F32) for h in range(2)] for d in range(2)]
        B['PB'] = [self.pt("p2_PB%d" % d, [128, 512], F32) for d in range(2)]
        B['PS'] = [self.pt("p2_PS%d" % d, [128, 512], F32) for d in range(2)]
        sy = self.sy
        sy.op('dve', lambda e: e.tensor_copy(out=B['smask'][:].rearrange("p (c t) -> p c t", t=CH),
                                             in_=self.scanm.unsqueeze(1).to_broadcast([128, NCH, CH])),
              reads=['c_sb'], writes=['smask'])
        for d in range(2):
            sy.op('pool', lambda e, d=d: e.memset(B['Upad'][d][:], 0.0), writes=[('Upad', d)])
        sy.op('pool', lambda e: e.memset(B['Vpad'][:], 0.0), writes=['Vpad'])
        for d in range(2):
            for h in range(2):
                sy.op('pool', lambda e, d=d, h=h: e.memset(B['AR'][d][h][:], 0.0), writes=[('AR', d, h)])
        return B

    def shift(self, src, dst, ti, rows, skey, dkey):
        tp = lambda j: self.taps_sb[0:rows, ti * 3 + j:ti * 3 + j + 1]
        self.ts('dve', dst[0:rows, :], src[0:rows, :], tp(1), None, ALU.mult, None, [skey, 'taps_sb'], [dkey])
        self.stt('dve', dst[0:rows, 1:T], src[0:rows, 0:T - 1], tp(0), dst[0:rows, 1:T], ALU.mult, ALU.add, [skey, 'taps_sb'], [dkey])
        self.stt('dve', dst[0:rows, 0:T - 1], src[0:rows, 1:T], tp(2), dst[0:rows, 0:T - 1], ALU.mult, ALU.add, [skey, 'taps_sb'], [dkey])

    def psum_rr(self, B):
        B['rr'] = (B.get('rr', -1) + 1) % 4
        i = B['rr']
        return B['PA'][i // 2][i % 2], ('PA', i // 2, i % 2)

    def phase2_partA(self, slot, B):
        sy = self.sy
        zT = self.zT
        for ti, dst, fn in ((96, B['lw'], AF.Tanh), (97, B['la'], AF.Identity)):
            f0 = ti * 128
            sy.dma('sp', B['A8'][:], zT[slot, f0:f0 + 128, :], reads=[('zT', slot)], writes=['A8'])
            self.shift(B['A8'], B['A7'], ti, 128, 'A8', 'A7')
            self.actf(dst[:], B['A7'][:], fn, ['A7'], [('lora', ti)])
        lg = [B['A1'][:].bitcast(BF16), B['A2'][:].bitcast(BF16)]
        lgv = lambda j: lg[j // 2][:, (j % 2) * T:(j % 2 + 1) * T]
        lgk = lambda j: 'A1' if j < 2 else 'A2'
        for j in range(4):
            ti = 98 + j
            rows = 128 if j < 3 else 96
            f0 = ti * 128
            sy.dma('sp', B['A8'][0:rows, :], zT[slot, f0:f0 + rows, :], reads=[('zT', slot)], writes=['A8'])
            self.shift(B['A8'], B['A7'], ti, rows, 'A8', 'A7')
            self.actf(lgv(j)[0:rows, :], B['A7'][0:rows, :], AF.Sigmoid, ['A7'], [lgk(j)])
        nhp = self.nhp or 32
        for hp in range(nhp):
            wl = B['wl'][hp % 2]
            wk = ('wl', hp % 2)
            gu = self.wb['g_up']
            sy.dma('sp', wl[:, 0:3, :], gu[0:384, hp * 128:(hp + 1) * 128].rearrange("(kc p) f -> p kc f", p=128),
                   reads=[('wb', 'g_up')], writes=[wk])
            sy.dma('sp', wl[0:96, 3, :], gu[384:480, hp * 128:(hp + 1) * 128], reads=[('wb', 'g_up')], writes=[wk])
            gst = B['A6'] if hp % 2 == 0 else B['A4']
            gk = 'A6' if hp % 2 == 0 else 'A4'
            for blk in range(4):
                ps, pk = self.psum_rr(B)
                for j in range(4):
                    rows = 128 if j < 3 else 96
                    self.mm(ps[:, :], wl[0:rows, j, :], lgv(j)[0:rows, blk * 512:(blk + 1) * 512], j == 0, j == 3,
                            [wk, lgk(j)], [pk])
                X = self.evq()
                sy.op(X, (lambda e, ps=ps, blk=blk: e.tensor_copy(out=gst[:, blk * 512:(blk + 1) * 512], in_=ps[:, :])) if X == 'dve'
                      else (lambda e, ps=ps, blk=blk: e.copy(out=gst[:, blk * 512:(blk + 1) * 512], in_=ps[:, :])),
                      reads=[pk], writes=[gk])
            sy.dma('sp', self.gT[slot, hp * 128:(hp + 1) * 128, :], gst[:], reads=[gk], writes=[('gT', slot)], dkey=('gT', slot))

    def phase2_pool(self, slot, B):
        sy = self.sy
        zT = self.zT
        pp = B['pp']
        sy.op('pool', lambda e: e.memset(pp[:, 0:8], 0.0), writes=['zr'])
        sy.op('pool', lambda e: e.memset(pp[:, T + 8:T + 16], 0.0), writes=['zr'])
        dT = [B['A1'][:].bitcast(BF16), B['A2'][:].bitcast(BF16)]
        dv = lambda j: dT[j // 2][:, (j % 2) * T:(j % 2 + 1) * T]
        dk = lambda j: 'A1' if j < 2 else 'A2'
        for gi, win in enumerate((2, 4, 8, 16)):
            h = win // 2
            sy.dma('sp', B['A7'][:], self.invcnt[gi:gi + 1, :].to_broadcast([128, T]), writes=['A7'])
            pwt = B['A3'][:].bitcast(BF16).rearrange("p (j f) -> p j f", j=4)
            sy.dma('sp', pwt, self.wb['pool_w'][gi * 512:(gi + 1) * 512, :].rearrange("(j p) f -> p j f", p=128),
                   reads=[('wb', 'pool_w')], writes=['A3'])
            for j in range(4):
                f0 = P0 + gi * 512 + j * 128
                sy.dma('sp', pp[:, 8:T + 8], zT[slot, f0:f0 + 128, :], reads=[('zT', slot)], writes=['zr'])
                cur = pp
                ck = 'zr'
                L = T + 16
                w = 1
                bufs = [(B['pa'], 'zk'), (B['pb'], 'zv')]
                bi = 0
                while w < win:
                    nxt, nk = bufs[bi]
                    bi ^= 1
                    n = L - (2 * w - 1)
                    self.tt('dve', nxt[:, 0:n], cur[:, 0:n], cur[:, w:w + n], ALU.add, [ck], [nk])
                    cur, ck = nxt, nk
                    w *= 2
                self.tt('dve', B['A4'][:], cur[:, 8 - h:8 - h + T], B['A7'][:], ALU.mult, [ck, 'A7'], ['A4'])
                self.tt('dve', dv(j), B['A4'][:], pp[:, 8:T + 8], ALU.subtract, ['A4', 'zr'], [dk(j)])
            for ot in range(8):
                hpi = gi * 8 + ot
                yst = B['A6'] if ot % 2 == 0 else B['A4']
                yk = 'A6' if ot % 2 == 0 else 'A4'
                for blk in range(4):
                    ps, pk = self.psum_rr(B)
                    for j in range(4):
                        self.mm(ps[:, :], pwt[:, j, ot * 128:(ot + 1) * 128], dv(j)[:, blk * 512:(blk + 1) * 512], j == 0, j == 3,
                                ['A3', dk(j)], [pk])
                    self.ts('dve', yst[:, blk * 512:(blk + 1) * 512], ps[:, :], self.cpv(9, hpi), None, ALU.mult, None, [pk, 'cp_sb'], [yk])
                sy.dma('sp', self.ybT[slot, hpi * 128:(hpi + 1) * 128, :], yst[:], reads=[yk], writes=[('ybT', slot)], dkey=('ybT', slot))

    def phase2_prep_dir(self, slot, hp, d, B):
        sy = self.sy
        wl = B['wl'][hp % 2]
        wk = ('wl', hp % 2)
        c3 = lambda ap: ap.rearrange("p (c t) -> p c t", t=CH)
        for blk in range(4):
            bs = slice(blk * 512, (blk + 1) * 512)
            ps, pk = self.psum_rr(B)
            self.mm(ps[:, :], wl[:, d, :], B['lw'][:, bs], True, True, [wk, ('lora', 96)], [pk])
            self.actf(B['A2'][:, bs], ps[:, :], AF.Sigmoid, [pk, 'cp_sb'], ['A2'], bias=self.cpv(0 + d, hp))
            ps, pk = self.psum_rr(B)
            self.mm(ps[:, :], wl[:, 2 + d, :], B['la'][:, bs], True, True, [wk, ('lora', 97)], [pk])
            self.actf(B['A1'][:, bs], ps[:, :], AF.Sigmoid, [pk, 'cp_sb'], ['A1'], bias=self.cpv(2 + d, hp))
        self.ts('dve', B['A2'][:], B['A2'][:], NEG_EXP_HALF, None, ALU.mult, None, ['A2'], ['A2'])
        self.tt('dve', B['A3'][:], B['kkn'][:], B['A1'][:], ALU.mult, ['kkn', 'A1'], ['A3'])
        self.ts('dve', B['A4'][:], B['A1'][:], self.cpv(5, hp), self.cpv(10, hp), ALU.mult, ALU.add, ['A1', 'cp_sb'], ['A4'])
        self.tt('dve', B['A4'][:], B['zk'][:], B['A4'][:], ALU.mult, ['zk', 'A4'], ['A4'])
        self.stt('dve', B['rkb'][d][:], B['zr'][:], self.cpv(6, hp), B['A4'][:], ALU.mult, ALU.mult, ['zr', 'A4', 'cp_sb'], [('rkb', d)])
        sy.op('dve', lambda e: e.tensor_tensor_scan(out=B['A1'][:], data0=B['smask'][:], data1=B['A2'][:], initial=0.0,
                                                    op0=ALU.mult, op1=ALU.add), reads=['smask', 'A2'], writes=['A1'])
        tot = c3(B['A1'][:])[:, :, CH - 1:CH]
        totb = tot.to_broadcast([128, NCH, CH])
        if d == 0:
            self.tt('dve', B['A6'][:], B['A1'][:], B['A2'][:], ALU.subtract, ['A1', 'A2'], ['A6'])
            cT, ck = B['A1'], 'A1'
        else:
            self.tt('dve', c3(B['A6'][:]), totb, c3(B['A1'][:]), ALU.subtract, ['A1'], ['A6'])
            self.tt('dve', B['A2'][:], B['A6'][:], B['A2'][:], ALU.add, ['A6', 'A2'], ['A2'])
            cT, ck = B['A2'], 'A2'
        self.tt('dve', c3(B['A7'][:]), totb, c3(cT[:]), ALU.subtract, ['A1', ck], ['A7'])
        self.actf(B['wc'][d][:], c3(B['A1'][:])[:, :, CH - 1], AF.Exp, ['A1'], [('wc', d)])
        self.actf(B['A8'][:], cT[:], AF.Exp, [ck], ['A8'])
        for h in range(2):
            hs = slice(h * 64, (h + 1) * 64)
            self.tt('dve', B['AR'][d][h][hs, :, 128:256], c3(B['zr'][hs, :]), c3(B['A8'][hs, :]), ALU.mult, ['zr', 'A8'], [('AR', d, h)])
        self.actf(B['A8'][:], B['A6'][:], AF.Exp, ['A6'], ['A8'])
        for h in range(2):
            hs = slice(h * 64, (h + 1) * 64)
            self.stt('dve', B['AR'][d][h][hs, :, 0:128], c3(B['kkn'][hs, :]), -1.0, c3(B['A8'][hs, :]), ALU.mult, ALU.mult,
                     ['kkn', 'A8'], [('AR', d, h)])
        self.actf(B['A8'][:], cT[:], AF.Exp, [ck], ['A8'], scale=-1.0)
        self.tt('dve', B['bT'][d][:].rearrange("p c t -> p (c t)"), B['A3'][:], B['A8'][:], ALU.mult, ['A3', 'A8'], [('bT', d)])
        self.tt('dve', B['kT'][d][:].rearrange("p c t -> p (c t)"), B['A4'][:], B['A8'][:], ALU.mult, ['A4', 'A8'], [('kT', d)])
        self.actf(B['A8'][:], B['A7'][:], AF.Exp, ['A7'], ['A8'])
        hatT = B['A6'][:].bitcast(BF16)
        self.tt('dve', hatT[:, 0:T], B['A3'][:], B['A8'][:], ALU.mult, ['A3', 'A8'], ['A6'])
        self.tt('dve', hatT[:, T:2 * T], B['A4'][:], B['A8'][:], ALU.mult, ['A4', 'A8'], ['A6'])
        for which, dst, dkn in ((0, B['bh'][d], ('bh', d)), (1, B['kh'][d], ('kh', d))):
            for g in range(NCH // 8):
                ps, pk = self.psum_rr(B)
                pv = ps[:, :].bitcast(BF16)[:, 0:1024].rearrange("p (j t) -> p j t", t=128)
                for j in range(8):
                    c = g * 8 + j
                    sy.op('pe', lambda e, c=c, j=j, pv=pv, which=which: e.transpose(
                        out=pv[:, j, :], in_=hatT[:, which * T + c * CH: which * T + (c + 1) * CH], identity=self.ident_bf[:]),
                        reads=['A6', 'ident_bf'], writes=[pk])
                X = self.evq()
                sy.op(X, (lambda e, g=g, pv=pv, dst=dst: e.tensor_copy(out=dst[:, g * 8:(g + 1) * 8, :], in_=pv)) if X == 'dve'
                      else (lambda e, g=g, pv=pv, dst=dst: e.copy(out=dst[:, g * 8:(g + 1) * 8, :], in_=pv)),
                      reads=[pk], writes=[dkn])

    def chunk_gen(self, hp, d, B):
        sy = self.sy
        AR, bT, kT, bh, kh = B['AR'][d], B['bT'][d], B['kT'][d], B['bh'][d], B['kh'][d]
        Vpad, Hm, Hb, ATm, Lm, SQ, Ub, Upad = B['Vpad'], B['Hm'][d], B['Hb'][d], B['ATm'][d], B['Lm'][d], B['SQ'][d], B['Ub'][d], B['Upad'][d]
        PA, PB, PS = B['PA'][d], B['PB'][d], B['PS'][d]
        kPA = [('PA', d, 0), ('PA', d, 1)]
        kL, kXU, kYH, kPS = ('PB', d), ('PB', d), ('PB', d), ('PS', d)
        kAR = [('AR', d, 0), ('AR', d, 1)]
        kATm = [('ATm', d, 0), ('ATm', d, 1)]
        kSQ = [('SQ', d, 0), ('SQ', d, 1)]
        PL = PB[:, 0:256].rearrange("p (h s) -> p h s", h=2)
        PXU = PB[:, 256:384]
        PYH = PB[:, 384:512]
        PSv = PS[:, :].rearrange("p (i s) -> p i s", i=4)
        YT = B['A1']
        sy.op('pool', lambda e: e.memset(Hm[:], 0.0), writes=[('Hm', d)])
        sy.op('pool', lambda e: e.memset(Hb[:], 0.0), writes=[('Hb', d)])
        order = range(NCH) if d == 0 else range(NCH - 1, -1, -1)
        for c in order:
            for h in range(2):
                self.mm(PA[h][:, 0:256], bT[:, c, :], AR[h][:, c, :], True, True, [('bT', d), kAR[h]], [kPA[h]])
                self.mm(PA[h][:, 256:512], kT[:, c, :], AR[h][:, c, :], True, True, [('kT', d), kAR[h]], [kPA[h]])
                self.mm(PL[:, h, :], AR[h][:, c, 0:128], bT[:, c, :], True, True, [('bT', d), kAR[h]], [kL])
            for h in range(2):
                self.tt('dve', ATm[h][:], PA[h][:, :], self.maskA[d], ALU.mult, [kPA[h], 'c_sb'], [kATm[h]])
            self.tt('dve', Lm[:], PL, self.maskL[d].unsqueeze(1).to_broadcast([128, 2, 128]), ALU.mult, [kL, 'c_sb'], [('Lm', d)])
            yield
            self.mm(PXU, AR[0][:, c, 0:128], Hb[:], True, False, [kAR[0], ('Hb', d)], [kXU])
            self.mm(PXU, AR[1][:, c, 0:128], Hb[:], False, False, [kAR[1], ('Hb', d)], [kXU])
            for h in range(2):
                hs = slice(h * 64, (h + 1) * 64)
                self.mm(PXU[:, hs], ATm[h][:, 256:384], Vpad[:, c, h, hs], False, h == 1, [kATm[h], 'Vpad'], [kXU])
            sy.op('act', lambda e: e.copy(out=Ub[:], in_=PXU), reads=[kXU], writes=[('Ub', d)])
            for h in range(2):
                self.mm(PSv[:, h, :], Lm[:, h, :], ATm[h][:, 0:128], True, True, [('Lm', d), kATm[h]], [kPS])
                self.mm(PSv[:, 2 + h, :], ATm[h][:, 0:128], Lm[:, h, :], True, True, [('Lm', d), kATm[h]], [kPS])
            sy.op('act', lambda e: e.copy(out=SQ[0][:], in_=PSv), reads=[kPS], writes=[kSQ[0]])
            yield
            for j in range(7):
                for h in range(2):
                    hs = slice(h * 64, (h + 1) * 64)
                    if j == 0:
                        Nj, nk = ATm[h][:, 0:128], kATm[h]
                    else:
                        Nj, nk = SQ[(j - 1) % 2][:, h, :], kSQ[(j - 1) % 2]
                    self.mm(PXU[:, hs], Nj, Ub[:, hs], True, True, [nk, ('Ub', d)], [kXU])
                self.tt('dve', Ub[:], PXU, Ub[:], ALU.add, [kXU, ('Ub', d)], [('Ub', d)])
                if j + 2 <= 6:
                    src, sk = SQ[j % 2], kSQ[j % 2]
                    for h in range(2):
                        self.mm(PSv[:, h, :], src[:, 2 + h, :], src[:, h, :], True, True, [sk], [kPS])
                        self.mm(PSv[:, 2 + h, :], src[:, h, :], src[:, 2 + h, :], True, True, [sk], [kPS])
                    sy.op('act', lambda e, j=j: e.copy(out=SQ[(j + 1) % 2][:], in_=PSv), reads=[kPS], writes=[kSQ[(j + 1) % 2]])
                yield
            for h in range(2):
                hs = slice(h * 64, (h + 1) * 64)
                sy.op('act', lambda e, h=h, hs=hs: e.copy(out=Upad[:, h, hs], in_=Ub[:, hs]), reads=[('Ub', d)], writes=[('Upad', d)])
            self.mm(PYH, Hb[:], AR[0][:, c, 128:256], True, False, [('Hb', d), kAR[0]], [kYH])
            self.mm(PYH, Hb[:], AR[1][:, c, 128:256], False, False, [('Hb', d), kAR[1]], [kYH])
            for h in range(2):
                self.mm(PYH, Upad[:, h, :], ATm[h][:, 128:256], False, False, [('Upad', d), kATm[h]], [kYH])
                self.mm(PYH, Vpad[:, c, h, :], ATm[h][:, 384:512], False, h == 1, ['Vpad', kATm[h]], [kYH])
            first = (d == 0 and c < NCH // 2) or (d == 1 and c >= NCH // 2)
            ys = YT[:, c * CH:(c + 1) * CH]
            if first:
                sy.op('act', lambda e, ys=ys: e.copy(out=ys, in_=PYH), reads=[kYH], writes=[('YT', c)])
            else:
                self.tt('dve', ys, PYH, ys, ALU.add, [kYH, ('YT', c)], [('YT', c)])
            self.mm(PXU, bh[:, c, :], Ub[:], True, False, [('bh', d), ('Ub', d)], [kXU])
            self.mm(PXU, kh[:, c, :], Vpad[:, c, 0, :], False, False, [('kh', d), 'Vpad'], [kXU])
            self.mm(PXU, kh[:, c, :], Vpad[:, c, 1, :], False, True, [('kh', d), 'Vpad'], [kXU])
            for h in range(2):
                hs = slice(h * 64, (h + 1) * 64)
                self.stt('dve', Hm[hs, hs], Hm[hs, hs], B['wc'][d][hs, c:c + 1], PXU[hs, hs], ALU.mult, ALU.add,
                         [('Hm', d), ('wc', d), kXU], [('Hm', d)])
            sy.op('act', lambda e: e.copy(out=Hb[:], in_=Hm[:]), reads=[('Hm', d)], writes=[('Hb', d)])
            yield

    def flush_late_casts(self, n):
        for _ in range(min(n, len(self.late_casts))):
            self.late_casts.pop(0)()

    def phase2_hp(self, slot, hp, B):
        sy = self.sy
        self.flush_late_casts(4)
        zT = self.zT
        c3 = lambda ap: ap.rearrange("p (c t) -> p c t", t=CH)
        wl = B['wl'][hp % 2]
        wk = ('wl', hp % 2)
        for i, name in enumerate(('w_up_f', 'w_up_b', 'a_up_f', 'a_up_b')):
            sy.dma('sp', wl[:, i, :], self.wb[name][:, hp * 128:(hp + 1) * 128], reads=[('wb', name)], writes=[wk])
        for j, (dst, dk, raw, rk) in enumerate(((B['zr'], 'zr', B['A8'], 'A8'), (B['zk'], 'zk', B['A7'], 'A7'), (B['zv'], 'zv', B['A6'], 'A6'))):
            f0 = j * D + hp * 128
            sy.dma('sp', raw[:], zT[slot, f0:f0 + 128, :], reads=[('zT', slot)], writes=[rk])
            self.shift(raw, dst, j * 32 + hp, 128, rk, dk)
        vb = B['A3'][:].bitcast(BF16)[:, 0:T]
        sy.op('act', lambda e: e.copy(out=vb, in_=B['zv'][:]), reads=['zv'], writes=['A3'])
        for g in range(NCH // 8):
            ps, pk = self.psum_rr(B)
            pv = ps[:, :].bitcast(BF16)[:, 0:1024].rearrange("p (j t) -> p j t", t=128)
            for j in range(8):
                c = g * 8 + j
                sy.op('pe', lambda e, c=c, j=j, pv=pv: e.transpose(out=pv[:, j, :], in_=vb[:, c * CH:(c + 1) * CH], identity=self.ident_bf[:]),
                      reads=['A3', 'ident_bf'], writes=[pk])
            for h in range(2):
                hs = slice(h * 64, (h + 1) * 64)
                X = self.evq()
                sy.op(X, (lambda e, g=g, pv=pv, h=h, hs=hs: e.tensor_copy(out=B['Vpad'][:, g * 8:(g + 1) * 8, h, hs], in_=pv[:, :, hs])) if X == 'dve'
                      else (lambda e, g=g, pv=pv, h=h, hs=hs: e.copy(out=B['Vpad'][:, g * 8:(g + 1) * 8, h, hs], in_=pv[:, :, hs])),
                      reads=[pk], writes=['Vpad'])
        self.ts('dve', B['kkn'][:], B['zk'][:], self.cpv(4, hp), None, ALU.mult, None, ['zk', 'cp_sb'], ['kkn'])
        sqb = B['A4'][:].bitcast(BF16)[:, 0:T]
        self.actf(sqb, B['kkn'][:], AF.Square, ['kkn'], ['A4'])
        for blk in range(4):
            bs = slice(blk * 512, (blk + 1) * 512)
            ps, pk = self.psum_rr(B)
            self.mm(ps[:, :], self.bones_bf[:], sqb[:, bs], True, True, ['bones_bf', 'A4'], [pk])
            self.ts('dve', B['A2'][:, bs], ps[:, :], 1e-12, None, ALU.max, None, [pk], ['A2'])
        self.actf(B['A2'][:], B['A2'][:], AF.Sqrt, ['A2'], ['A2'])
        sy.op('dve', lambda e: e.reciprocal(out=B['A2'][:], in_=B['A2'][:]), reads=['A2'], writes=['A2'])
        self.tt('dve', B['kkn'][:], B['kkn'][:], B['A2'][:], ALU.mult, ['kkn', 'A2'], ['kkn'])
        for d in range(2):
            self.phase2_prep_dir(slot, hp, d, B)
        gens = [self.chunk_gen(hp, 0, B), self.chunk_gen(hp, 1, B)]
        alive = [True, True]
        while any(alive):
            for i in range(2):
                if alive[i]:
                    try:
                        next(gens[i])
                    except StopIteration:
                        alive[i] = False
        YT = B['A1']
        ykeys = [('YT', c) for c in range(NCH)]
        sy.dma('sp', B['A8'][:], self.gT[slot, hp * 128:(hp + 1) * 128, :], reads=[('gT', slot)], writes=['A8'])
        sy.dma('sp', B['A7'][:], zT[slot, P1 + hp * 128:P1 + (hp + 1) * 128, :], reads=[('zT', slot)], writes=['A7'])
        sy.dma('sp', B['A6'][:], zT[slot, P2 + hp * 128:P2 + (hp + 1) * 128, :], reads=[('zT', slot)], writes=['A6'])
        sy.dma('sp', B['A4'][:], self.ybT[slot, hp * 128:(hp + 1) * 128, :], reads=[('ybT', slot)], writes=['A4'])
        for blk in range(4):
            bs = slice(blk * 512, (blk + 1) * 512)
            yk = ykeys[blk * 4:(blk + 1) * 4]
            ps, pk = self.psum_rr(B)
            self.mm(ps[:, :], self.bones_f, YT[:, bs], True, True, ['c_sb'] + yk, [pk])
            self.stt('dve', B['A2'][:, bs], ps[:, :], -1.0 / 64, YT[:, bs], ALU.mult, ALU.add, [pk] + yk, ['A2'])
            self.actf(B['A3'][:, bs], B['A2'][:, bs], AF.Square, ['A2'], ['A3'])
            ps2, pk2 = self.psum_rr(B)
            self.mm(ps2[:, :], self.bones_f, B['A3'][:, bs], True, True, ['c_sb', 'A3'], [pk2])
            self.ts('dve', B['tmp5'], ps2[:, :], 1.0 / 64, 64e-5, ALU.mult, ALU.add, [pk2], ['kkn'])
            self.actf(B['tmp5'], B['tmp5'], AF.Sqrt, ['kkn'], ['kkn'])
            sy.op('dve', lambda e: e.reciprocal(out=B['tmp5'], in_=B['tmp5']), reads=['kkn'], writes=['kkn'])
            self.tt('dve', B['A2'][:, bs], B['A2'][:, bs], B['tmp5'], ALU.mult, ['A2', 'kkn'], ['A2'])
            self.ts('dve', B['A2'][:, bs], B['A2'][:, bs], self.cpv(7, hp), self.cpv(8, hp), ALU.mult, ALU.add, ['A2', 'cp_sb'], ['A2'])
            ps3, pk3 = self.psum_rr(B)
            self.mm(ps3[:, :], self.bones_bf[:], B['rkb'][0][:, bs], True, False, ['bones_bf', ('rkb', 0)], [pk3])
            self.mm(ps3[:, :], self.bones_bf[:], B['rkb'][1][:, bs], False, True, ['bones_bf', ('rkb', 1)], [pk3])
            self.tt('dve', B['A3'][:, bs], ps3[:, :], B['zv'][:, bs], ALU.mult, [pk3, 'zv'], ['A3'])
            self.tt('dve', B['A2'][:, bs], B['A2'][:, bs], B['A3'][:, bs], ALU.add, ['A2', 'A3'], ['A2'])
        self.tt('dve', B['A2'][:], B['A2'][:], B['A8'][:], ALU.mult, ['A2', 'A8'], ['A2'])
        self.tt('dve', B['A2'][:], B['A2'][:], B['A7'][:], ALU.mult, ['A2', 'A7'], ['A2'])
        self.tt('dve', B['A4'][:], B['A4'][:], B['A6'][:], ALU.mult, ['A4', 'A6'], ['A4'])
        mb = B['A3'][:].bitcast(BF16)[:, 0:T]
        self.tt('dve', mb, B['A2'][:], B['A4'][:], ALU.add, ['A2', 'A4'], ['A3'])
        if self.t3[slot] == T:
            sy.dma('sp', self.mT[slot, hp * 128:(hp + 1) * 128, :], mb, reads=['A3'], writes=[('mT', slot)], dkey=('mT', slot))
        else:
            H = T // 2
            ms = B['A2'][:].bitcast(BF16)[:, 0:H]
            self.ts('dve', ms, mb[:, 0:H], self.sel_sb[:, 0:1], None, ALU.mult, None, ['A3', 'sel'], ['A2'])
            self.stt('dve', ms, mb[:, H:T], self.sel_sb[:, 1:2], ms, ALU.mult, ALU.add, ['A3', 'sel', 'A2'], ['A2'])
            sy.dma('sp', self.mT[slot, hp * 128:(hp + 1) * 128, :], ms, reads=['A2'], writes=[('mT', slot)], dkey=('mT', slot))

    def phase2(self, slot, B):
        self.phase2_partA(slot, B)
        self.phase2_pool(slot, B)
        for hp in range(self.nhp or 32):
            self.phase2_hp(slot, hp, B)

    def alloc_phase3(self):
        C = {}
        sy = self.sy
        C['hT'] = self.sb("p3_hT", [128, 32, 512], F32)
        C['hnb'] = self.sb("p3_hnb", [128, 8192], F32)
        C['Rb'] = self.sb("p3_Rb", [128, 8192], F32)
        C['KT'] = self.sb("p3_KT", [128, 32, 256], BF16)
        C['V'] = self.sb("p3_V", [128, 2, 4096], BF16)
        C['W'] = [self.sb("p3_W%d" % i, [128, 32, 256], BF16) for i in range(2)]
        C['Eb'] = self.sb("p3_E", [128, 512], F32)
        C['rz'] = self.sb("p3_rz", [128, 512], F32)
        C['gT'] = self.sb("p3_gT", [128, 160], F32)
        C['ones'] = self.sb("p3_ones", [128, 128], BF16)
        C['idf'] = self.sb("p3_idf", [128, 128], F32)
        C['sq'] = [self.sb("p3_sq%d" % i, [128, 4, 512], BF16) for i in range(2)]
        C['ps'] = [self.pt("p3_ps%d" % i, [128, 512], F32) for i in range(4)]
        C['ptr'] = [self.pt("p3_ptr%d" % i, [128, 4, 128], F32) for i in range(2)]
        C['pss'] = self.pt("p3_pss", [128, 512], F32)
        C['psz'] = self.pt("p3_psz", [128, 512], F32)
        hnbf = C['hnb'][:].bitcast(BF16)
        C['hn'] = hnbf.rearrange("p (k t) -> p k t", t=512)
        C['mn'] = hnbf[:, 0:8192].rearrange("p (k t) -> p k t", t=256)
        C['xs'] = [C['hnb'][:, 0:4096], C['hnb'][:, 4096:8192]]
        Rbf = C['Rb'][:].bitcast(BF16)
        C['QT'] = Rbf.rearrange("p (k t) -> p k t", t=512)
        C['E'] = C['Eb'][:].bitcast(BF16).rearrange("p (m t) -> p m t", t=512)
        C['psn'] = 0
        C['wn'] = 0
        sy.dma('sp', C['gT'][:], self.gainsT, writes=['gT'])
        sy.dma('sp', C['idf'][:], self.consts[:, 0:128], writes=['idf'])
        sy.op('pool', lambda e: e.memset(C['ones'][:], 1.0), writes=['ones'])
        return C

    HN = ['hnA', 'hnB']

    @staticmethod
    def hk(kc):
        return 'hnA' if kc < 16 else 'hnB'

    def ps_rr(self, C):
        C['psn'] = (C['psn'] + 1) % 4
        return C['ps'][C['psn']], ('ps', C['psn'])

    def wblock(self, C, name, r0, nkc, c0, ncol=256):
        C['wn'] = (C['wn'] + 1) % 2
        i = C['wn']
        buf = C['W'][i]
        key = ('W', i)
        self.sy.dma('sp', buf[:, 0:nkc, 0:ncol],
                    self.wb[name][r0:r0 + nkc * 128, c0:c0 + ncol].rearrange("(kc p) f -> p kc f", p=128),
                    reads=[('wb', name)], writes=[key])
        return buf, key

    def copy_ev(self, out, in_, reads, writes):
        X = self.evq()
        if X == 'dve':
            return self.sy.op('dve', lambda e: e.tensor_copy(out=out, in_=in_), reads=reads, writes=writes)
        return self.sy.op('act', lambda e: e.copy(out=out, in_=in_), reads=reads, writes=writes)

    def load_T(self, C, rows_fn, nsub, dst3):
        sy = self.sy
        for sub in range(nsub):
            xs = C['xs'][sub % 2]
            xk = self.HN[sub % 2]
            sy.dma('sp', xs, rows_fn(sub), writes=[xk])
            for g in range(8):
                pt = C['ptr'][g % 2]
                pk = ('ptr', g % 2)
                for j in range(4):
                    kc = g * 4 + j
                    sy.op('pe', lambda e, kc=kc, j=j, pt=pt, xs=xs: e.transpose(out=pt[:, j, :], in_=xs[:, kc * 128:(kc + 1) * 128],
                                                                               identity=C['idf'][:]),
                          reads=[xk, 'idf'], writes=[pk])
                self.copy_ev(dst3[:, g * 4:(g + 1) * 4, sub * 128:(sub + 1) * 128], pt[:, :, :], [pk],
                             [('hT', g * 4 + j) for j in range(4)])

    def fm_norm(self, C, src3, N, gidx, dst3, dkey_fn):
        sy = self.sy
        pss = C['pss']
        for g in range(8):
            sq = C['sq'][g % 2]
            sk = ('sq', g % 2)
            sy.op('act', lambda e, g=g, sq=sq: e.activation(out=sq[:, :, 0:N], in_=src3[:, g * 4:(g + 1) * 4, :], func=AF.Square),
                  reads=[('hT', g * 4 + j) for j in range(4)], writes=[sk])
            for j in range(4):
                self.mm(pss[:, 0:N], C['ones'][:], sq[:, j, 0:N], g == 0 and j == 0, g == 7 and j == 3, ['ones', sk], ['pss'])
        rz = C['rz']
        sy.op('act', lambda e: e.activation(out=rz[:, 0:N], in_=pss[:, 0:N], func=AF.Sqrt, bias=self.eps_sb[:], scale=1.0 / D),
              reads=['pss', 'eps'], writes=['rz'])
        sy.op('dve', lambda e: e.reciprocal(out=rz[:, 0:N], in_=rz[:, 0:N]), reads=['rz'], writes=['rz'])
        for kc in range(32):
            self.stt('dve', dst3[:, kc, :], src3[:, kc, :], C['gT'][:, gidx * 32 + kc:gidx * 32 + kc + 1], rz[:, 0:N],
                     ALU.mult, ALU.mult, [('hT', kc), 'gT', 'rz'], [dkey_fn(kc)])

    def proj_fm(self, C, name, N, rhs_fn, rkey_fn, evac_fn, r0=0, nkc=32, c0=0, nft=32):
        for b in range(0, nft, 2):
            W, wk = self.wblock(C, name, r0, nkc, c0 + b * 128)
            for j in range(2):
                ps, pk = self.ps_rr(C)
                for kc in range(nkc):
                    self.mm(ps[:, 0:N], W[:, kc, j * 128:(j + 1) * 128], rhs_fn(kc), kc == 0, kc == nkc - 1,
                            [wk, rkey_fn(kc)], [pk])
                evac_fn(b + j, ps, pk)

    def phase3_mem(self, slot, C):
        sy = self.sy
        memT = C['hT'][:, :, 0:256]
        self.load_T(C, lambda sub: self.mem[slot, sub * 128:(sub + 1) * 128, :], 2, memT)
        self.fm_norm(C, memT, 256, 2, C['mn'], lambda kc: 'hnA')
        mn = C['mn']
        KT, V = C['KT'], C['V']
        self.proj_fm(C, 'xk', 256, lambda kc: mn[:, kc, :], lambda kc: 'hnA',
                     lambda ft, ps, pk: self.copy_ev(KT[:, ft, :], ps[:, 0:256], [pk], ['KT']))
        for fb in range(16):
            W, wk = self.wblock(C, 'xv', 0, 32, fb * 256)
            for mt in range(2):
                ps, pk = self.ps_rr(C)
                for kc in range(32):
                    self.mm(ps[:, 0:256], mn[:, kc, mt * 128:(mt + 1) * 128], W[:, kc, :], kc == 0, kc == 31, [wk, 'hnA'], [pk])
                self.copy_ev(V[:, mt, fb * 256:(fb + 1) * 256], ps[:, 0:256], [pk], ['V'])

    def phase3_tile(self, slot, tok0, C):
        sy = self.sy
        hT, hn, QT, KT, V, E, rz = C['hT'], C['hn'], C['QT'], C['KT'], C['V'], C['E'], C['rz']
        hk = self.hk

        def add_evac(ft, ps, pk):
            self.tt('dve', hT[:, ft, :], ps[:, :], hT[:, ft, :], ALU.add, [pk, ('hT', ft)], [('hT', ft)])

        self.load_T(C, lambda sub: self.x3[slot, tok0 + sub * 128:tok0 + (sub + 1) * 128, :], 4, hT)
        for q in range(4):
            sy.dma('sp', hn[:, q * 8:(q + 1) * 8, :],
                   self.mT[slot, q * 1024:(q + 1) * 1024, tok0:tok0 + 512].rearrange("(kc p) t -> p kc t", p=128),
                   reads=[('mT', slot)], writes=[hk(q * 8)])
        self.proj_fm(C, 'w_out', 512, lambda kc: hn[:, kc, :], hk, add_evac)
        self.fm_norm(C, hT, 512, 1, hn, hk)
        self.proj_fm(C, 'xq', 512, lambda kc: hn[:, kc, :], hk,
                     lambda ft, ps, pk: self.copy_ev(QT[:, ft, :], ps[:, :], [pk], ['R']))
        for h in range(4):
            for mt in range(2):
                ps, pk = self.ps_rr(C)
                for j in range(8):
                    ft = h * 8 + j
                    self.mm(ps[:, :], KT[:, ft, mt * 128:(mt + 1) * 128], QT[:, ft, :], j == 0, j == 7, ['KT', 'R'], [pk])
                sy.op('act', lambda e, mt=mt, ps=ps: e.activation(out=E[:, mt, :], in_=ps[:, :], func=AF.Exp, scale=1.0 / 32.0),
                      reads=[pk], writes=['E'])
            for mt in range(2):
                self.mm(C['psz'][:, :], C['ones'][:], E[:, mt, :], mt == 0, mt == 1, ['ones', 'E'], ['psz'])
            sy.op('dve', lambda e: e.reciprocal(out=rz[:, :], in_=C['psz'][:, :]), reads=['psz'], writes=['rz'])
            for j in range(8):
                dt_ = h * 8 + j
                ps, pk = self.ps_rr(C)
                for mt in range(2):
                    self.mm(ps[:, :], V[:, mt, dt_ * 128:(dt_ + 1) * 128], E[:, mt, :], mt == 0, mt == 1, ['V', 'E'], [pk])
                self.tt('dve', hn[:, dt_, :], ps[:, :], rz[:, :], ALU.mult, [pk, 'rz'], [hk(dt_)])
        self.proj_fm(C, 'xo', 512, lambda kc: hn[:, kc, :], hk, add_evac)
        self.fm_norm(C, hT, 512, 3, hn, hk)
        aT = QT
        sg = [C['Eb'], C['rz']]
        sgk = ['E', 'rz']
        t0 = 0
        for npair in (11, 11, 11, 10):
            for p in range(npair):
                j0 = t0 + 2 * p
                Wg, wgk = self.wblock(C, 'ffn_w13', 0, 32, j0 * 128)
                for j in range(2):
                    ps, pk = self.ps_rr(C)
                    for kc in range(32):
                        self.mm(ps[:, :], Wg[:, kc, j * 128:(j + 1) * 128], hn[:, kc, :], kc == 0, kc == 31, [wgk, hk(kc)], [pk])
                    sy.op('act', lambda e, j=j, ps=ps: e.activation(out=sg[j][:, :], in_=ps[:, :], func=AF.Silu),
                          reads=[pk], writes=[sgk[j]])
                Wu, wuk = self.wblock(C, 'ffn_w13', 0, 32, FF + j0 * 128)
                for j in range(2):
                    ps, pk = self.ps_rr(C)
                    for kc in range(32):
                        self.mm(ps[:, :], Wu[:, kc, j * 128:(j + 1) * 128], hn[:, kc, :], kc == 0, kc == 31, [wuk, hk(kc)], [pk])
                    self.tt('dve', aT[:, 2 * p + j, :], ps[:, :], sg[j][:, :], ALU.mult, [pk, sgk[j]], ['R'])
            nt = 2 * npair
            self.proj_fm(C, 'ffn_w2', 512, lambda kc: aT[:, kc, :], lambda kc: 'R', add_evac, r0=t0 * 128, nkc=nt)
            t0 += nt
        self.fm_norm(C, hT, 512, 4, hT, lambda kc: ('hT', kc))
        for sub in range(4):
            ys = C['xs'][sub % 2]
            yk = self.HN[sub % 2]
            for g in range(8):
                pt = C['ptr'][g % 2]
                pk = ('ptr', g % 2)
                for j in range(4):
                    kc = g * 4 + j
                    sy.op('pe', lambda e, kc=kc, j=j, pt=pt, sub=sub: e.transpose(out=pt[:, j, :], in_=hT[:, kc, sub * 128:(sub + 1) * 128],
                                                                                 identity=C['idf'][:]),
                          reads=[('hT', kc), 'idf'], writes=[pk])
                self.copy_ev(ys[:, g * 512:(g + 1) * 512].rearrange("p (j t) -> p j t", t=128), pt[:, :, :], [pk], [yk])
            sy.dma('sp', self.y[slot, tok0 + sub * 128:tok0 + (sub + 1) * 128, :], ys, reads=[yk], writes=[('y', slot)], dkey=('y', slot))

    def phase3(self, slot, C):
        self.phase3_mem(slot, C)
        for tt in range(self.ntile3 or (self.t3[slot] // 512)):
            self.phase3_tile(slot, tt * 512, C)

    def barrier(self):
        for X in ('pe', 'dve', 'act', 'pool', 'sp'):
            self.sy.drain(X, skip_wb=True)

    def build(self):
        nc = self.nc
        sy = self.sy
        self.gstack = contextlib.ExitStack()
        self.stack = self.gstack
        self.eps_sb = self.sb("eps_sb", [128, 1], F32)
        self.ident_bf = self.sb("ident_bf", [128, 128], BF16)
        self.bones_bf = self.sb("bones_bf", [128, 128], BF16)
        sy.op('dve', lambda e: e.memset(self.eps_sb[:], 1e-6), writes=['eps'])
        self.late_casts = []
        if 0 in self.phases:
            for th in self.phase0(False):
                th()
        with contextlib.ExitStack() as st:
            self.stack = st
            tmp = self.sb("c_tmp", [128, 256], F32)
            sy.dma('sp', tmp[:], self.consts[:, 0:256], writes=['c_tmp'])
            sy.op('dve', lambda e: e.tensor_copy(out=self.ident_bf[:], in_=tmp[:, 0:128]), reads=['c_tmp'], writes=['ident_bf'])
            sy.op('dve', lambda e: e.tensor_copy(out=self.bones_bf[:], in_=tmp[:, 128:256]), reads=['c_tmp'], writes=['bones_bf'])
            self.barrier()
        self.stack = self.gstack
        if 1 in self.phases:
            with contextlib.ExitStack() as st:
                self.stack = st
                A = self.alloc_phase1()
                self.phase1_all(A)
                self.barrier()
            self.stack = self.gstack
        if 0 in self.phases:
            self.late_casts = self.phase0(True)
            if 2 not in self.phases:
                self.flush_late_casts(len(self.late_casts))
        if 2 in self.phases:
            with contextlib.ExitStack() as st:
                self.stack = st
                self.load_consts()
                B = self.alloc_phase2()
                B['pp'] = B['zr_full']
                B['pa'] = B['zk_full']
                B['pb'] = B['zv_full']
                for slot in range(self.nslot):
                    self.phase2(slot, B)
                self.barrier()
            self.stack = self.gstack
        self.flush_late_casts(len(self.late_casts))
        if 3 in self.phases:
            with contextlib.ExitStack() as st:
                self.stack = st
                C = self.alloc_phase3()
                for slot in range(self.nslot):
                    self.phase3(slot, C)
                self.barrier()
            self.stack = self.gstack
        sy.drain('sp')
        return nc


def make_consts():
    c = np.zeros((128, 1664), np.float32)
    p = np.arange(128)
    c[:, 0:128] = np.eye(128, dtype=np.float32)
    c[:, 128:256] = (p[:, None] // 64 == p[None, :] // 64).astype(np.float32)
    s = p[:, None]
    t = p[None, :]
    strict_f = (s < t).astype(np.float32)
    incl_f = (s <= t).astype(np.float32)
    strict_b = (s > t).astype(np.float32)
    incl_b = (s >= t).astype(np.float32)
    c[:, 256:768] = np.concatenate([strict_f, incl_f, strict_f, incl_f], axis=1)
    c[:, 768:1280] = np.concatenate([strict_b, incl_b, strict_b, incl_b], axis=1)
    c[:, 1280:1408] = strict_b
    c[:, 1408:1536] = strict_f
    c[:, 1536:1664] = 1.0
    c[:, 1536] = 0.0
    return c


def prep_shared(inp):
    sh = {}
    sh['gains'] = np.ascontiguousarray(np.stack([inp['norm_mix_g'][0], inp['norm_x_g'][0], inp['norm_mem_g'][0],
                                                 inp['norm_ffn_g'][0], inp['norm_final_g']], axis=0).astype(np.float32))
    sh['gainsT'] = np.ascontiguousarray(sh['gains'].reshape(5, 32, 128).transpose(2, 0, 1).reshape(128, 160))
    sw = np.zeros((3, 102 * 128), np.float32)
    sw[:, :RW] = inp['shift_w'][0]
    sh['taps'] = np.ascontiguousarray(sw.reshape(3, 102, 128).transpose(2, 1, 0).reshape(128, 306))
    vecs = [inp['w0_f'][0], inp['w0_b'][0], inp['a0_f'][0], inp['a0_b'][0], inp['k_k'][0], inp['k_a'][0],
            inp['r_k'][0].reshape(-1), inp['ln_x_g'][0], inp['ln_x_b'][0], inp['pool_scale'][0]]
    cp = np.stack([v.reshape(32, 128).T for v in vecs], axis=1)
    sh['cp'] = np.ascontiguousarray(cp.reshape(128, 320).astype(np.float32))
    sh['consts'] = make_consts()
    t = np.arange(T)
    ic = np.zeros((4, T), np.float32)
    for gi, win in enumerate((2, 4, 8, 16)):
        lo = np.clip(t - win // 2, 0, T)
        hi = np.clip(t + win - win // 2, 0, T)
        ic[gi] = 1.0 / (hi - lo).astype(np.float32)
    sh['invcnt'] = ic
    wmap = dict(w_in=inp['w_in'][0], w_out=inp['w_out'][0], xq=inp['xq'][0], xk=inp['xk'][0], xv=inp['xv'][0],
                xo=inp['xo'][0], ffn_w13=inp['ffn_w13'][0], ffn_w2=inp['ffn_w2'][0],
                pool_w=inp['pool_w'][0].reshape(PW, 1024), g_up=inp['g_up'][0],
                w_up_f=inp['w_up_f'][0], w_up_b=inp['w_up_b'][0], a_up_f=inp['a_up_f'][0], a_up_b=inp['a_up_b'][0])
    sh.update(wmap)
    return sh


WEIGHT_NAMES = ('w_in', 'w_out', 'xq', 'xk', 'xv', 'xo', 'ffn_w13', 'ffn_w2', 'pool_w', 'g_up',
                'w_up_f', 'w_up_b', 'a_up_f', 'a_up_b')


def kernel(**inputs):
    inp = {k: np.asarray(v) for k, v in inputs.items()}
    sh = prep_shared(inp)
    k = K()
    nc = k.build()
    in_maps = []
    for c in range(8):
        m = {"x": np.stack([inp['x_prompt'][c], inp['x_sample'][c % 4]], axis=0),
             "mem": np.stack([inp['mem_prompt'][c], inp['mem_sample'][c % 4]], axis=0)}
        hsel = c // 4
        m["x1h"] = np.ascontiguousarray(inp['x_sample'][c % 4][hsel * (T // 2):(hsel + 1) * (T // 2)])
        s = np.zeros((128, 2), np.float32)
        s[:, hsel] = 1.0
        m["sel"] = s
        for n in ('gains', 'gainsT', 'taps', 'cp', 'consts', 'invcnt') + WEIGHT_NAMES:
            m[n] = sh[n]
        in_maps.append(m)
    res = run_bass_kernel_spmd(nc, in_maps, core_ids=list(range(8)))
    y_prompt = np.stack([np.asarray(res.results[c]["y0"]) for c in range(8)], axis=0).astype(np.float32, copy=False)
    y_sample = np.stack([np.concatenate([np.asarray(res.results[c]["y1"]), np.asarray(res.results[c + 4]["y1"])], axis=0)
                         for c in range(4)], axis=0).astype(np.float32, copy=False)
    return (y_prompt, y_sample)
```

```python
import contextlib
import numpy as np
import ml_dtypes
import concourse.bass as bass
import concourse.mybir as mybir
from concourse.bass_utils import run_bass_kernel_spmd

F32 = mybir.dt.float32
BF16 = mybir.dt.bfloat16
AF = mybir.ActivationFunctionType
ALU = mybir.AluOpType
AX = mybir.AxisListType

D = 4096
T = 2048
NSLOT = 2
NMEM = 256
RW = 13024
PW = 2048
INC = 23264
FF = 11008
CH = 128
NCH = T // CH
P0 = RW
P1 = RW + PW
P2 = P1 + D
NEG_EXP_HALF = -0.6065306597126334


class Sy:
    SEG = 30000

    def __init__(self, nc):
        self.nc = nc
        self.eng = {'pe': nc.tensor, 'dve': nc.vector, 'act': nc.scalar, 'pool': nc.gpsimd, 'sp': nc.sync}
        self.cnt = {e: 0 for e in self.eng}
        self.segs = {e: [] for e in self.eng}
        self.waited = {e: {} for e in self.eng}
        self.last_w = {}
        self.readers = {}
        self.dsem = {}
        self.retired = []
        self.nsem = 0

    def _newsem(self, name):
        self.nsem += 1
        return self.nc.alloc_semaphore(name + "_%d" % self.nsem)

    def _wait(self, X, tok):
        if tok is None:
            return
        if tok[0] == 'e':
            _, E, n = tok
            if E == X and X == 'pe':
                return
            if self.waited[X].get(E, 0) >= n:
                return
            self.waited[X][E] = n
            seg = (n - 1) // self.SEG
            self.eng[X].wait_ge(self.segs[E][seg], (n - 1) % self.SEG + 1)
        else:
            _, sem, v, sid = tok
            if self.waited[X].get(sid, 0) >= v:
                return
            self.waited[X][sid] = v
            self.eng[X].wait_ge(sem, v)

    def _deps(self, X, reads, writes):
        for k in reads:
            self._wait(X, self.last_w.get(k))
        for k in writes:
            self._wait(X, self.last_w.get(k))
            for tk in self.readers.get(k, ()):
                self._wait(X, tk)

    def _record(self, tok, reads, writes):
        for k in reads:
            lst = self.readers.setdefault(k, [])
            if tok[0] == 'e':
                lst[:] = [t for t in lst if not (t[0] == 'e' and t[1] == tok[1])]
            lst.append(tok)
        for k in writes:
            self.last_w[k] = tok
            self.readers[k] = []

    def op(self, X, fn, reads=(), writes=()):
        self._deps(X, reads, writes)
        ins = fn(self.eng[X])
        n = self.cnt[X] + 1
        seg = (n - 1) // self.SEG
        while len(self.segs[X]) <= seg:
            self.segs[X].append(self._newsem("pg_" + X))
        ins.then_inc(self.segs[X][seg], 1)
        self.cnt[X] = n
        self._record(('e', X, n), reads, writes)
        return ins

    def dma(self, X, out, in_, reads=(), writes=(), dkey=None):
        self._deps(X, reads, writes)
        if dkey is None:
            dkey = writes[0] if writes else reads[0]
        ent = self.dsem.get(dkey)
        if ent is None or ent[1] + 16 > self.SEG:
            if ent is not None:
                self.retired.append(('d', ent[0], ent[1], ent[2]))
            ent = [self._newsem("dm"), 0, self.nsem]
            self.dsem[dkey] = ent
        ent[1] += 16
        self.eng[X].dma_start(out=out, in_=in_).then_inc(ent[0], 16)
        tok = ('d', ent[0], ent[1], ent[2])
        self._record(tok, reads, writes)
        return tok

    def drain(self, X, skip_wb=False):
        iswb = lambda k: skip_wb and isinstance(k, tuple) and k[0] == 'wb'
        for k, tk in list(self.last_w.items()):
            if not iswb(k):
                self._wait(X, tk)
        for k, lst in list(self.readers.items()):
            if iswb(k):
                continue
            for tk in lst:
                self._wait(X, tk)
        for tk in self.retired:
            self._wait(X, tk)
        for E in self.eng:
            if E != X and self.cnt[E] > 0:
                self._wait(X, ('e', E, self.cnt[E]))


def feature_tiles():
    tiles = []
    f = 0
    while f < RW:
        w = min(128, RW - f)
        tiles.append((f, w, 'rw'))
        f += w
    for seg0, n, kind in ((P0, PW, 'pool'), (P1, D, 'gate'), (P2, D, 'gate')):
        for i in range(n // 128):
            tiles.append((seg0 + i * 128, 128, kind))
    return tiles


def feature_blocks():
    tl = feature_tiles()
    blocks = []
    cur = []
    for t in tl:
        if cur and (len(cur) == 4 or cur[-1][2] != t[2] or cur[-1][1] != 128):
            blocks.append(cur)
            cur = []
        cur.append(t)
    if cur:
        blocks.append(cur)
    return blocks


class SlotAP:
    def __init__(self, aps):
        self.aps = aps

    def __getitem__(self, idx):
        return self.aps[idx[0]][idx[1:]]


class K:
    def __init__(self, nslot=NSLOT, phases=(0, 1, 2, 3), debug=False, ntile1=None, nhp=None, ntile3=None, half=True):
        self.nslot = nslot
        self.phases = phases
        self.debug = debug
        self.ntile1 = ntile1
        self.nhp = nhp
        self.ntile3 = ntile3
        nc = bass.Bass("TRN2", target_bir_lowering=False)
        self.nc = nc
        self.sy = Sy(nc)
        self.q = 0
        self.ee = 0
        ext_in = lambda name, shape, dt=F32: nc.dram_tensor(name, list(shape), dt, kind="ExternalInput").ap()
        self.x = ext_in("x", [nslot, T, D])
        self.mem = ext_in("mem", [nslot, NMEM, D])
        self.w = {}
        wshapes = dict(w_in=[D, INC], w_out=[D, D], xq=[D, D], xk=[D, D], xv=[D, D], xo=[D, D],
                       ffn_w13=[D, 2 * FF], ffn_w2=[FF, D], pool_w=[PW, 1024], g_up=[480, D],
                       w_up_f=[128, D], w_up_b=[128, D], a_up_f=[128, D], a_up_b=[128, D])
        self.wshapes = wshapes
        need = self._needed_weights()
        for name in need:
            self.w[name] = ext_in(name, wshapes[name])
        self.gains = ext_in("gains", [5, D])
        self.gainsT = ext_in("gainsT", [128, 160])
        self.taps = ext_in("taps", [128, 102 * 3])
        self.cp = ext_in("cp", [128, 10 * 32])
        self.consts = ext_in("consts", [128, 1664])
        kind_scr = "ExternalOutput" if debug else "Internal"
        self.wb = {}
        for name in need:
            self.wb[name] = nc.dram_tensor("wb_" + name, wshapes[name], BF16, kind="Internal").ap()
        if 1 in phases:
            self.zT = SlotAP([nc.dram_tensor("zT%d" % s, [INC, T], F32, kind=kind_scr).ap() for s in range(nslot)])
        elif 2 in phases:
            self.zT = SlotAP([ext_in("zT%d" % s, [INC, T]) for s in range(nslot)])
        self.half = half and nslot == 2
        self.t3 = [T] * nslot
        if self.half:
            self.t3[1] = T // 2
            self.sel = ext_in("sel", [128, 2])
            self.x1h = ext_in("x1h", [T // 2, D])
        if 2 in phases:
            self.mT = SlotAP([nc.dram_tensor("mT%d" % s, [D, self.t3[s]], BF16, kind=kind_scr).ap() for s in range(nslot)])
        elif 3 in phases:
            self.mT = SlotAP([ext_in("mT%d" % s, [D, self.t3[s]], BF16) for s in range(nslot)])
        if 3 in phases:
            self.y = SlotAP([nc.dram_tensor("y%d" % s, [self.t3[s], D], F32, kind="ExternalOutput").ap() for s in range(nslot)])
            self.x3 = SlotAP([self.x[s] if self.t3[s] == T else self.x1h for s in range(nslot)])
        if 2 in phases:
            self.gT = nc.dram_tensor("gT", [nslot, D, T], F32, kind="Internal").ap()
            self.ybT = nc.dram_tensor("ybT", [nslot, D, T], F32, kind="Internal").ap()
            self.invcnt = ext_in("invcnt", [4, T])

    def _needed_weights(self):
        need = []
        if 1 in self.phases:
            need += ['w_in']
        if 2 in self.phases:
            need += ['pool_w', 'g_up', 'w_up_f', 'w_up_b', 'a_up_f', 'a_up_b']
        if 3 in self.phases:
            need += ['w_out', 'xq', 'xk', 'xv', 'xo', 'ffn_w13', 'ffn_w2']
        return need

    def dq(self):
        return 'sp'

    def evq(self):
        self.ee ^= 1
        return 'dve' if self.ee else 'act'

    def sb(self, name, shape, dt):
        return self.stack.enter_context(self.nc.sbuf_tensor(name, list(shape), dt))

    def pt(self, name, shape, dt):
        return self.stack.enter_context(self.nc.psum_tensor(name, list(shape), dt))

    LATE = ('w_out', 'xq', 'xk', 'xv', 'xo', 'ffn_w13', 'ffn_w2')

    def phase0(self, late):
        sy = self.sy
        thunks = []
        for name in self._needed_weights():
            if (name in self.LATE) != late:
                continue
            src = self.w[name]
            dst = self.wb[name]
            rows = self.wshapes[name][0]
            if name == 'w_in':
                for bi, blk in enumerate(feature_blocks()):
                    b0 = blk[0][0]
                    bw = sum(t[1] for t in blk)
                    for r in range(0, rows, 1024):
                        thunks.append(lambda r=r, b0=b0, bw=bw, bi=bi, dst=dst, src=src: sy.dma(
                            'pool', dst[r:r + 1024, b0:b0 + bw], src[r:r + 1024, b0:b0 + bw],
                            writes=[('wb', 'w_in', bi)], dkey=('wbc', bi % 8)))
                continue
            step = 128 if self.wshapes[name][1] > 8192 else 512
            r = 0
            while r < rows:
                n = min(step, rows - r)
                thunks.append(lambda r=r, n=n, dst=dst, src=src, name=name: sy.dma(
                    'pool', dst[r:r + n, :], src[r:r + n, :], writes=[('wb', name)], dkey=('wb', name)))
                r += n
        return thunks

    def load_consts(self):
        sy = self.sy
        self.c_sb = self.sb("c_sb", [128, 1664], F32)
        sy.dma('sp', self.c_sb[:], self.consts, writes=['c_sb'])
        self.ident_f = self.c_sb[:, 0:128]
        self.bones_f = self.c_sb[:, 128:256]
        self.maskA = {0: self.c_sb[:, 256:768], 1: self.c_sb[:, 768:1280]}
        self.maskL = {0: self.c_sb[:, 1280:1408], 1: self.c_sb[:, 1408:1536]}
        self.scanm = self.c_sb[:, 1536:1536 + 128]
        if self.half:
            self.sel_sb = self.sb("sel_sb", [128, 2], F32)
            sy.dma('sp', self.sel_sb[:], self.sel, writes=['sel'])
        self.taps_sb = self.sb("taps_sb", [128, 306], F32)
        sy.dma('sp', self.taps_sb[:], self.taps, writes=['taps_sb'])
        self.cp_sb = self.sb("cp_sb", [128, 352], F32)
        sy.dma('sp', self.cp_sb[:, 0:320], self.cp, writes=['cp_sb'])
        sy.op('dve', lambda e: e.tensor_scalar(out=self.cp_sb[:, 320:352], in0=self.cp_sb[:, 5 * 32:6 * 32], scalar1=-1.0,
                                               scalar2=1.0, op0=ALU.mult, op1=ALU.add), reads=['cp_sb'], writes=['cp_sb'])

    def cpv(self, j, hp):
        return self.cp_sb[:, j * 32 + hp:j * 32 + hp + 1]

    def rms_tok(self, src_ap, src_key, gbc, out_bf, out_key, scr, tag):
        sy = self.sy
        sq, ss, rstd = scr['sq'], scr['ss'], scr['rstd']
        sy.op('act', lambda e: e.activation(out=sq[:], in_=src_ap, func=AF.Square, accum_out=ss[:]),
              reads=[src_key], writes=[tag + 'sq', tag + 'ss'])
        sy.op('act', lambda e: e.activation(out=rstd[:], in_=ss[:], func=AF.Sqrt, bias=self.eps_sb[:], scale=1.0 / D),
              reads=[tag + 'ss', 'eps'], writes=[tag + 'rstd'])
        sy.op('dve', lambda e: e.reciprocal(out=rstd[:], in_=rstd[:]), reads=[tag + 'rstd'], writes=[tag + 'rstd'])
        sy.op('dve', lambda e: e.scalar_tensor_tensor(out=out_bf, in0=src_ap, scalar=rstd[:, 0:1], in1=gbc[:],
                                                      op0=ALU.mult, op1=ALU.mult),
              reads=[src_key, tag + 'rstd', 'gbc'], writes=[out_key])

    def transpose_to_fm(self, src_bf, src_key, dst, dst_key, col0, pst, pst_key, nkc=32):
        sy = self.sy
        for g in range(nkc // 8):
            pk = (pst_key, g % 2)
            pt = pst[g % 2]
            for j in range(8):
                kc = g * 8 + j
                sy.op('pe', lambda e, kc=kc, j=j: e.transpose(out=pt[:, j, :], in_=src_bf[:, kc * 128:(kc + 1) * 128],
                                                             identity=self.ident_bf[:]),
                      reads=[src_key, 'ident_bf'], writes=[pk])
            sy.op(self.evq(), lambda e, g=g: e.tensor_copy(out=dst[:, g * 8:(g + 1) * 8, col0:col0 + 128], in_=pt[:, :, :])
                  if e is self.nc.vector else e.copy(out=dst[:, g * 8:(g + 1) * 8, col0:col0 + 128], in_=pt[:, :, :]),
                  reads=[pk], writes=[dst_key])

    def p1_load_w(self, A, gi):
        if gi >= len(A['items']):
            return
        bi = A['items'][gi][2]
        blk = A['blocks'][bi]
        b0 = blk[0][0]
        bw = sum(t[1] for t in blk)
        self.sy.dma('sp', A['wt'][gi % 3][:, :, 0:bw], self.wb['w_in'][:, b0:b0 + bw].rearrange("(kc p) f -> p kc f", p=128),
                    reads=[('wb', 'w_in', bi)], writes=[('wt', gi % 3)])

    def phase1_all(self, A):
        sy = self.sy
        blocks = feature_blocks()
        A['blocks'] = blocks
        ntile = self.ntile1 or (T // 512)
        A['items'] = [(slot, tt, bi) for slot in range(self.nslot) for tt in range(ntile) for bi in range(len(blocks))]
        self.p1_load_w(A, 0)
        self.p1_load_w(A, 1)
        for gi, (slot, tt, bi) in enumerate(A['items']):
            tok0 = tt * 512
            if bi == 0:
                for sub in range(4):
                    xs = A['xs']
                    sy.dma('sp', xs[:], self.x[slot, tok0 + sub * 128: tok0 + (sub + 1) * 128, :], writes=['xs'])
                    self.rms_tok(xs[:], 'xs', A['gbc'], A['xnb'][:], 'xnb', A, 'p1')
                    self.transpose_to_fm(A['xnb'], 'xnb', A['xnT'], 'xnT', sub * 128, A['pst'], 'pst')
            self.p1_load_w(A, gi + 2)
            blk = blocks[bi]
            b0 = blk[0][0]
            bw = sum(t[1] for t in blk)
            wbuf = A['wt'][gi % 3]
            wkey = ('wt', gi % 3)
            zs = A['zst'][gi % 2]
            zkey = ('zst', gi % 2)
            for ti, (f0, fw, kind) in enumerate(blk):
                A['psn'] = (A['psn'] + 1) % 4
                ps = A['ps'][A['psn']]
                pk = ('ps', A['psn'])
                o = f0 - b0
                for kc in range(32):
                    sy.op('pe', lambda e, kc=kc, o=o, fw=fw, ps=ps: e.matmul(ps[0:fw, :], lhsT=wbuf[:, kc, o:o + fw],
                                                                            rhs=A['xnT'][:, kc, :], start=(kc == 0), stop=(kc == 31)),
                          reads=[wkey, 'xnT'], writes=[pk])
                if kind == 'gate':
                    sy.op('act', lambda e, ti=ti, fw=fw, ps=ps: e.activation(out=zs[0:fw, ti, :], in_=ps[0:fw, :], func=AF.Sigmoid),
                          reads=[pk], writes=[zkey])
                else:
                    sy.op('dve', lambda e, ti=ti, fw=fw, ps=ps: e.tensor_copy(out=zs[0:fw, ti, :], in_=ps[0:fw, :]),
                          reads=[pk], writes=[zkey])
            if all(t[1] == 128 for t in blk):
                sy.dma('sp', self.zT[slot, b0:b0 + bw, tok0:tok0 + 512].rearrange("(ft p) t -> p ft t", p=128),
                       zs[:, 0:len(blk), :], reads=[zkey], writes=[('zT', slot)], dkey=('zT', slot))
            else:
                for ti, (f0, fw, kind) in enumerate(blk):
                    sy.dma('sp', self.zT[slot, f0:f0 + fw, tok0:tok0 + 512], zs[0:fw, ti, :],
                           reads=[zkey], writes=[('zT', slot)], dkey=('zT', slot))

    def alloc_phase1(self):
        A = {}
        A['xs'] = self.sb("p1_xs", [128, D], F32)
        A['sq'] = self.sb("p1_sq", [128, D], BF16)
        A['ss'] = self.sb("p1_ss", [128, 1], F32)
        A['rstd'] = self.sb("p1_rstd", [128, 1], F32)
        A['xnb'] = self.sb("p1_xnb", [128, D], BF16)
        A['xnT'] = self.sb("p1_xnT", [128, 32, 512], BF16)
        A['gbc'] = self.sb("p1_gbc", [128, D], F32)
        A['wt'] = [self.sb("p1_wt%d" % i, [128, 32, 512], BF16) for i in range(3)]
        A['zst'] = [self.sb("p1_zst%d" % i, [128, 4, 512], F32) for i in range(2)]
        A['pst'] = [self.pt("p1_pst%d" % i, [128, 8, 128], BF16) for i in range(2)]
        A['ps'] = [self.pt("p1_ps%d" % i, [128, 512], F32) for i in range(4)]
        A['psn'] = 0
        self.sy.dma('sp', A['gbc'][:], self.gains[0:1, :].to_broadcast([128, D]), writes=['gbc'])
        return A


    def tt(self, X, out, a, b, op, reads, writes):
        return self.sy.op(X, lambda e: e.tensor_tensor(out=out, in0=a, in1=b, op=op), reads=reads, writes=writes)

    def ts(self, X, out, a, s1, s2, op0, op1, reads, writes):
        if s2 is None:
            return self.sy.op(X, lambda e: e.tensor_scalar(out=out, in0=a, scalar1=s1, scalar2=None, op0=op0), reads=reads, writes=writes)
        return self.sy.op(X, lambda e: e.tensor_scalar(out=out, in0=a, scalar1=s1, scalar2=s2, op0=op0, op1=op1), reads=reads, writes=writes)

    def stt(self, X, out, a, sc, b, op0, op1, reads, writes):
        return self.sy.op(X, lambda e: e.scalar_tensor_tensor(out=out, in0=a, scalar=sc, in1=b, op0=op0, op1=op1), reads=reads, writes=writes)

    def actf(self, out, a, func, reads, writes, bias=0.0, scale=1.0):
        return self.sy.op('act', lambda e: e.activation(out=out, in_=a, func=func, bias=bias, scale=scale), reads=reads, writes=writes)

    def mm(self, out, lhsT, rhs, start, stop, reads, writes):
        return self.sy.op('pe', lambda e: e.matmul(out, lhsT=lhsT, rhs=rhs, start=start, stop=stop), reads=reads, writes=writes)

    def alloc_phase2(self):
        B = {}
        for n in ('zr', 'zk', 'zv', 'kkn', 'A1', 'A2', 'A3', 'A4', 'A6', 'A7', 'A8'):
            full = self.sb("p2_" + n, [128, T + 16], F32)
            B[n + '_full'] = full
            B[n] = full[:, 0:T]
        B['lw'] = self.sb("p2_lw", [128, T], BF16)
        B['la'] = self.sb("p2_la", [128, T], BF16)
        B['smask'] = self.sb("p2_smask", [128, T], BF16)
        B['wl'] = [self.sb("p2_wl%d" % i, [128, 8, 128], BF16) for i in range(2)]
        B['rkb'] = [self.sb("p2_rkb%d" % i, [128, T], BF16) for i in range(2)]
        B['AR'] = [[self.sb("p2_AR%d%d" % (d, h), [128, NCH, 256], BF16) for h in range(2)] for d in range(2)]
        B['bT'] = [self.sb("p2_bT%d" % d, [128, NCH, 128], BF16) for d in range(2)]
        B['kT'] = [self.sb("p2_kT%d" % d, [128, NCH, 128], BF16) for d in range(2)]
        B['bh'] = [self.sb("p2_bh%d" % d, [128, NCH, 128], BF16) for d in range(2)]
        B['kh'] = [self.sb("p2_kh%d" % d, [128, NCH, 128], BF16) for d in range(2)]
        B['Vpad'] = self.sb("p2_Vpad", [128, NCH, 2, 128], BF16)
        B['wc'] = [self.sb("p2_wc%d" % d, [128, NCH], F32) for d in range(2)]
        B['Hm'] = [self.sb("p2_Hm%d" % d, [128, 128], F32) for d in range(2)]
        B['Hb'] = [self.sb("p2_Hb%d" % d, [128, 128], BF16) for d in range(2)]
        B['ATm'] = [[self.sb("p2_ATm%d%d" % (d, h), [128, 512], BF16) for h in range(2)] for d in range(2)]
        B['Lm'] = [self.sb("p2_Lm%d" % d, [128, 2, 128], BF16) for d in range(2)]
        B['SQ'] = [[self.sb("p2_SQ%d%d" % (d, i), [128, 4, 128], BF16) for i in range(2)] for d in range(2)]
        B['Ub'] = [self.sb("p2_Ub%d" % d, [128, 128], BF16) for d in range(2)]
        B['Upad'] = [self.sb("p2_Upad%d" % d, [128, 2, 128], BF16) for d in range(2)]
        B['tmp5'] = B['kkn'][:, 0:512]
        B['PA'] = [[self.pt("p2_PA%d%d" % (d, h), [128, 512], F32) for h in range(2)] for d in range(2)]
        B['PB'] = [self.pt("p2_PB%d" % d, [128, 512], F32) for d in range(2)]
        B['PS'] = [self.pt("p2_PS%d" % d, [128, 512], F32) for d in range(2)]
        sy = self.sy
        sy.op('dve', lambda e: e.tensor_copy(out=B['smask'][:].rearrange("p (c t) -> p c t", t=CH),
                                             in_=self.scanm.unsqueeze(1).to_broadcast([128, NCH, CH])),
              reads=['c_sb'], writes=['smask'])
        for d in range(2):
            sy.op('pool', lambda e, d=d: e.memset(B['Upad'][d][:], 0.0), writes=[('Upad', d)])
        sy.op('pool', lambda e: e.memset(B['Vpad'][:], 0.0), writes=['Vpad'])
        for d in range(2):
            for h in range(2):
                sy.op('pool', lambda e, d=d, h=h: e.memset(B['AR'][d][h][:], 0.0), writes=[('AR', d, h)])
        return B

    def shift(self, src, dst, ti, rows, skey, dkey):
        tp = lambda j: self.taps_sb[0:rows, ti * 3 + j:ti * 3 + j + 1]
        self.ts('dve', dst[0:rows, :], src[0:rows, :], tp(1), None, ALU.mult, None, [skey, 'taps_sb'], [dkey])
        self.stt('dve', dst[0:rows, 1:T], src[0:rows, 0:T - 1], tp(0), dst[0:rows, 1:T], ALU.mult, ALU.add, [skey, 'taps_sb'], [dkey])
        self.stt('dve', dst[0:rows, 0:T - 1], src[0:rows, 1:T], tp(2), dst[0:rows, 0:T - 1], ALU.mult, ALU.add, [skey, 'taps_sb'], [dkey])

    def psum_rr(self, B):
        B['rr'] = (B.get('rr', -1) + 1) % 4
        i = B['rr']
        return B['PA'][i // 2][i % 2], ('PA', i // 2, i % 2)

    def phase2_partA(self, slot, B):
        sy = self.sy
        zT = self.zT
        for ti, dst, fn in ((96, B['lw'], AF.Tanh), (97, B['la'], AF.Identity)):
            f0 = ti * 128
            sy.dma('sp', B['A8'][:], zT[slot, f0:f0 + 128, :], reads=[('zT', slot)], writes=['A8'])
            self.shift(B['A8'], B['A7'], ti, 128, 'A8', 'A7')
            self.actf(dst[:], B['A7'][:], fn, ['A7'], [('lora', ti)])
        lg = [B['A1'][:].bitcast(BF16), B['A2'][:].bitcast(BF16)]
        lgv = lambda j: lg[j // 2][:, (j % 2) * T:(j % 2 + 1) * T]
        lgk = lambda j: 'A1' if j < 2 else 'A2'
        for j in range(4):
            ti = 98 + j
            rows = 128 if j < 3 else 96
            f0 = ti * 128
            sy.dma('sp', B['A8'][0:rows, :], zT[slot, f0:f0 + rows, :], reads=[('zT', slot)], writes=['A8'])
            self.shift(B['A8'], B['A7'], ti, rows, 'A8', 'A7')
            self.actf(lgv(j)[0:rows, :], B['A7'][0:rows, :], AF.Sigmoid, ['A7'], [lgk(j)])
        nhp = self.nhp or 32
        for hp in range(nhp):
            wl = B['wl'][hp % 2]
            wk = ('wl', hp % 2)
            gu = self.wb['g_up']
            sy.dma('sp', wl[:, 0:3, :], gu[0:384, hp * 128:(hp + 1) * 128].rearrange("(kc p) f -> p kc f", p=128),
                   reads=[('wb', 'g_up')], writes=[wk])
            sy.dma('sp', wl[0:96, 3, :], gu[384:480, hp * 128:(hp + 1) * 128], reads=[('wb', 'g_up')], writes=[wk])
            gst = B['A6'] if hp % 2 == 0 else B['A4']
            gk = 'A6' if hp % 2 == 0 else 'A4'
            for blk in range(4):
                ps, pk = self.psum_rr(B)
                for j in range(4):
                    rows = 128 if j < 3 else 96
                    self.mm(ps[:, :], wl[0:rows, j, :], lgv(j)[0:rows, blk * 512:(blk + 1) * 512], j == 0, j == 3,
                            [wk, lgk(j)], [pk])
                X = self.evq()
                sy.op(X, (lambda e, ps=ps, blk=blk: e.tensor_copy(out=gst[:, blk * 512:(blk + 1) * 512], in_=ps[:, :])) if X == 'dve'
                      else (lambda e, ps=ps, blk=blk: e.copy(out=gst[:, blk * 512:(blk + 1) * 512], in_=ps[:, :])),
                      reads=[pk], writes=[gk])
            sy.dma('sp', self.gT[slot, hp * 128:(hp + 1) * 128, :], gst[:], reads=[gk], writes=[('gT', slot)], dkey=('gT', slot))

    def phase2_pool(self, slot, B):
        sy = self.sy
        zT = self.zT
        pp = B['pp']
        sy.op('pool', lambda e: e.memset(pp[:, 0:8], 0.0), writes=['zr'])
        sy.op('pool', lambda e: e.memset(pp[:, T + 8:T + 16], 0.0), writes=['zr'])
        dT = [B['A1'][:].bitcast(BF16), B['A2'][:].bitcast(BF16)]
        dv = lambda j: dT[j // 2][:, (j % 2) * T:(j % 2 + 1) * T]
        dk = lambda j: 'A1' if j < 2 else 'A2'
        for gi, win in enumerate((2, 4, 8, 16)):
            h = win // 2
            sy.dma('sp', B['A7'][:], self.invcnt[gi:gi + 1, :].to_broadcast([128, T]), writes=['A7'])
            pwt = B['A3'][:].bitcast(BF16).rearrange("p (j f) -> p j f", j=4)
            sy.dma('sp', pwt, self.wb['pool_w'][gi * 512:(gi + 1) * 512, :].rearrange("(j p) f -> p j f", p=128),
                   reads=[('wb', 'pool_w')], writes=['A3'])
            for j in range(4):
                f0 = P0 + gi * 512 + j * 128
                sy.dma('sp', pp[:, 8:T + 8], zT[slot, f0:f0 + 128, :], reads=[('zT', slot)], writes=['zr'])
                cur = pp
                ck = 'zr'
                L = T + 16
                w = 1
                bufs = [(B['pa'], 'zk'), (B['pb'], 'zv')]
                bi = 0
                while w < win:
                    nxt, nk = bufs[bi]
                    bi ^= 1
                    n = L - (2 * w - 1)
                    self.tt('dve', nxt[:, 0:n], cur[:, 0:n], cur[:, w:w + n], ALU.add, [ck], [nk])
                    cur, ck = nxt, nk
                    w *= 2
                self.tt('dve', B['A4'][:], cur[:, 8 - h:8 - h + T], B['A7'][:], ALU.mult, [ck, 'A7'], ['A4'])
                self.tt('dve', dv(j), B['A4'][:], pp[:, 8:T + 8], ALU.subtract, ['A4', 'zr'], [dk(j)])
            for ot in range(8):
                hpi = gi * 8 + ot
                yst = B['A6'] if ot % 2 == 0 else B['A4']
                yk = 'A6' if ot % 2 == 0 else 'A4'
                for blk in range(4):
                    ps, pk = self.psum_rr(B)
                    for j in range(4):
                        self.mm(ps[:, :], pwt[:, j, ot * 128:(ot + 1) * 128], dv(j)[:, blk * 512:(blk + 1) * 512], j == 0, j == 3,
                                ['A3', dk(j)], [pk])
                    self.ts('dve', yst[:, blk * 512:(blk + 1) * 512], ps[:, :], self.cpv(9, hpi), None, ALU.mult, None, [pk, 'cp_sb'], [yk])
                sy.dma('sp', self.ybT[slot, hpi * 128:(hpi + 1) * 128, :], yst[:], reads=[yk], writes=[('ybT', slot)], dkey=('ybT', slot))

    def phase2_prep_dir(self, slot, hp, d, B):
        sy = self.sy
        wl = B['wl'][hp % 2]
        wk = ('wl', hp % 2)
        c3 = lambda ap: ap.rearrange("p (c t) -> p c t", t=CH)
        for blk in range(4):
            bs = slice(blk * 512, (blk + 1) * 512)
            ps, pk = self.psum_rr(B)
            self.mm(ps[:, :], wl[:, d, :], B['lw'][:, bs], True, True, [wk, ('lora', 96)], [pk])
            self.actf(B['A2'][:, bs], ps[:, :], AF.Sigmoid, [pk, 'cp_sb'], ['A2'], bias=self.cpv(0 + d, hp))
            ps, pk = self.psum_rr(B)
            self.mm(ps[:, :], wl[:, 2 + d, :], B['la'][:, bs], True, True, [wk, ('lora', 97)], [pk])
            self.actf(B['A1'][:, bs], ps[:, :], AF.Sigmoid, [pk, 'cp_sb'], ['A1'], bias=self.cpv(2 + d, hp))
        self.ts('dve', B['A2'][:], B['A2'][:], NEG_EXP_HALF, None, ALU.mult, None, ['A2'], ['A2'])
        self.tt('dve', B['A3'][:], B['kkn'][:], B['A1'][:], ALU.mult, ['kkn', 'A1'], ['A3'])
        self.ts('dve', B['A4'][:], B['A1'][:], self.cpv(5, hp), self.cpv(10, hp), ALU.mult, ALU.add, ['A1', 'cp_sb'], ['A4'])
        self.tt('dve', B['A4'][:], B['zk'][:], B['A4'][:], ALU.mult, ['zk', 'A4'], ['A4'])
        self.stt('dve', B['rkb'][d][:], B['zr'][:], self.cpv(6, hp), B['A4'][:], ALU.mult, ALU.mult, ['zr', 'A4', 'cp_sb'], [('rkb', d)])
        sy.op('dve', lambda e: e.tensor_tensor_scan(out=B['A1'][:], data0=B['smask'][:], data1=B['A2'][:], initial=0.0,
                                                    op0=ALU.mult, op1=ALU.add), reads=['smask', 'A2'], writes=['A1'])
        tot = c3(B['A1'][:])[:, :, CH - 1:CH]
        totb = tot.to_broadcast([128, NCH, CH])
        if d == 0:
            self.tt('dve', B['A6'][:], B['A1'][:], B['A2'][:], ALU.subtract, ['A1', 'A2'], ['A6'])
            cT, ck = B['A1'], 'A1'
        else:
            self.tt('dve', c3(B['A6'][:]), totb, c3(B['A1'][:]), ALU.subtract, ['A1'], ['A6'])
            self.tt('dve', B['A2'][:], B['A6'][:], B['A2'][:], ALU.add, ['A6', 'A2'], ['A2'])
            cT, ck = B['A2'], 'A2'
        self.tt('dve', c3(B['A7'][:]), totb, c3(cT[:]), ALU.subtract, ['A1', ck], ['A7'])
        self.actf(B['wc'][d][:], c3(B['A1'][:])[:, :, CH - 1], AF.Exp, ['A1'], [('wc', d)])
        self.actf(B['A8'][:], cT[:], AF.Exp, [ck], ['A8'])
        for h in range(2):
            hs = slice(h * 64, (h + 1) * 64)
            self.tt('dve', B['AR'][d][h][hs, :, 128:256], c3(B['zr'][hs, :]), c3(B['A8'][hs, :]), ALU.mult, ['zr', 'A8'], [('AR', d, h)])
        self.actf(B['A8'][:], B['A6'][:], AF.Exp, ['A6'], ['A8'])
        for h in range(2):
            hs = slice(h * 64, (h + 1) * 64)
            self.stt('dve', B['AR'][d][h][hs, :, 0:128], c3(B['kkn'][hs, :]), -1.0, c3(B['A8'][hs, :]), ALU.mult, ALU.mult,
                     ['kkn', 'A8'], [('AR', d, h)])
        self.actf(B['A8'][:], cT[:], AF.Exp, [ck], ['A8'], scale=-1.0)
        self.tt('dve', B['bT'][d][:].rearrange("p c t -> p (c t)"), B['A3'][:], B['A8'][:], ALU.mult, ['A3', 'A8'], [('bT', d)])
        self.tt('dve', B['kT'][d][:].rearrange("p c t -> p (c t)"), B['A4'][:], B['A8'][:], ALU.mult, ['A4', 'A8'], [('kT', d)])
        self.actf(B['A8'][:], B['A7'][:], AF.Exp, ['A7'], ['A8'])
        hatT = B['A6'][:].bitcast(BF16)
        self.tt('dve', hatT[:, 0:T], B['A3'][:], B['A8'][:], ALU.mult, ['A3', 'A8'], ['A6'])
        self.tt('dve', hatT[:, T:2 * T], B['A4'][:], B['A8'][:], ALU.mult, ['A4', 'A8'], ['A6'])
        for which, dst, dkn in ((0, B['bh'][d], ('bh', d)), (1, B['kh'][d], ('kh', d))):
            for g in range(NCH // 8):
                ps, pk = self.psum_rr(B)
                pv = ps[:, :].bitcast(BF16)[:, 0:1024].rearrange("p (j t) -> p j t", t=128)
                for j in range(8):
                    c = g * 8 + j
                    sy.op('pe', lambda e, c=c, j=j, pv=pv, which=which: e.transpose(
                        out=pv[:, j, :], in_=hatT[:, which * T + c * CH: which * T + (c + 1) * CH], identity=self.ident_bf[:]),
                        reads=['A6', 'ident_bf'], writes=[pk])
                X = self.evq()
                sy.op(X, (lambda e, g=g, pv=pv, dst=dst: e.tensor_copy(out=dst[:, g * 8:(g + 1) * 8, :], in_=pv)) if X == 'dve'
                      else (lambda e, g=g, pv=pv, dst=dst: e.copy(out=dst[:, g * 8:(g + 1) * 8, :], in_=pv)),
                      reads=[pk], writes=[dkn])

    def chunk_gen(self, hp, d, B):
        sy = self.sy
        AR, bT, kT, bh, kh = B['AR'][d], B['bT'][d], B['kT'][d], B['bh'][d], B['kh'][d]
        Vpad, Hm, Hb, ATm, Lm, SQ, Ub, Upad = B['Vpad'], B['Hm'][d], B['Hb'][d], B['ATm'][d], B['Lm'][d], B['SQ'][d], B['Ub'][d], B['Upad'][d]
        PA, PB, PS = B['PA'][d], B['PB'][d], B['PS'][d]
        kPA = [('PA', d, 0), ('PA', d, 1)]
        kL, kXU, kYH, kPS = ('PB', d), ('PB', d), ('PB', d), ('PS', d)
        kAR = [('AR', d, 0), ('AR', d, 1)]
        kATm = [('ATm', d, 0), ('ATm', d, 1)]
        kSQ = [('SQ', d, 0), ('SQ', d, 1)]
        PL = PB[:, 0:256].rearrange("p (h s) -> p h s", h=2)
        PXU = PB[:, 256:384]
        PYH = PB[:, 384:512]
        PSv = PS[:, :].rearrange("p (i s) -> p i s", i=4)
        YT = B['A1']
        sy.op('pool', lambda e: e.memset(Hm[:], 0.0), writes=[('Hm', d)])
        sy.op('pool', lambda e: e.memset(Hb[:], 0.0), writes=[('Hb', d)])
        order = range(NCH) if d == 0 else range(NCH - 1, -1, -1)
        for c in order:
            for h in range(2):
                self.mm(PA[h][:, 0:256], bT[:, c, :], AR[h][:, c, :], True, True, [('bT', d), kAR[h]], [kPA[h]])
                self.mm(PA[h][:, 256:512], kT[:, c, :], AR[h][:, c, :], True, True, [('kT', d), kAR[h]], [kPA[h]])
                self.mm(PL[:, h, :], AR[h][:, c, 0:128], bT[:, c, :], True, True, [('bT', d), kAR[h]], [kL])
            for h in range(2):
                self.tt('dve', ATm[h][:], PA[h][:, :], self.maskA[d], ALU.mult, [kPA[h], 'c_sb'], [kATm[h]])
            self.tt('dve', Lm[:], PL, self.maskL[d].unsqueeze(1).to_broadcast([128, 2, 128]), ALU.mult, [kL, 'c_sb'], [('Lm', d)])
            yield
            self.mm(PXU, AR[0][:, c, 0:128], Hb[:], True, False, [kAR[0], ('Hb', d)], [kXU])
            self.mm(PXU, AR[1][:, c, 0:128], Hb[:], False, False, [kAR[1], ('Hb', d)], [kXU])
            for h in range(2):
                hs = slice(h * 64, (h + 1) * 64)
                self.mm(PXU[:, hs], ATm[h][:, 256:384], Vpad[:, c, h, hs], False, h == 1, [kATm[h], 'Vpad'], [kXU])
            sy.op('act', lambda e: e.copy(out=Ub[:], in_=PXU), reads=[kXU], writes=[('Ub', d)])
            for h in range(2):
                self.mm(PSv[:, h, :], Lm[:, h, :], ATm[h][:, 0:128], True, True, [('Lm', d), kATm[h]], [kPS])
                self.mm(PSv[:, 2 + h, :], ATm[h][:, 0:128], Lm[:, h, :], True, True, [('Lm', d), kATm[h]], [kPS])
            sy.op('act', lambda e: e.copy(out=SQ[0][:], in_=PSv), reads=[kPS], writes=[kSQ[0]])
            yield
            for j in range(7):
                for h in range(2):
                    hs = slice(h * 64, (h + 1) * 64)
                    if j == 0:
                        Nj, nk = ATm[h][:, 0:128], kATm[h]
                    else:
                        Nj, nk = SQ[(j - 1) % 2][:, h, :], kSQ[(j - 1) % 2]
                    self.mm(PXU[:, hs], Nj, Ub[:, hs], True, True, [nk, ('Ub', d)], [kXU])
                self.tt('dve', Ub[:], PXU, Ub[:], ALU.add, [kXU, ('Ub', d)], [('Ub', d)])
                if j + 2 <= 6:
                    src, sk = SQ[j % 2], kSQ[j % 2]
                    for h in range(2):
                        self.mm(PSv[:, h, :], src[:, 2 + h, :], src[:, h, :], True, True, [sk], [kPS])
                        self.mm(PSv[:, 2 + h, :], src[:, h, :], src[:, 2 + h, :], True, True, [sk], [kPS])
                    sy.op('act', lambda e, j=j: e.copy(out=SQ[(j + 1) % 2][:], in_=PSv), reads=[kPS], writes=[kSQ[(j + 1) % 2]])
                yield
            for h in range(2):
                hs = slice(h * 64, (h + 1) * 64)
                sy.op('act', lambda e, h=h, hs=hs: e.copy(out=Upad[:, h, hs], in_=Ub[:, hs]), reads=[('Ub', d)], writes=[('Upad', d)])
            self.mm(PYH, Hb[:], AR[0][:, c, 128:256], True, False, [('Hb', d), kAR[0]], [kYH])
            self.mm(PYH, Hb[:], AR[1][:, c, 128:256], False, False, [('Hb', d), kAR[1]], [kYH])
            for h in range(2):
                self.mm(PYH, Upad[:, h, :], ATm[h][:, 128:256], False, False, [('Upad', d), kATm[h]], [kYH])
                self.mm(PYH, Vpad[:, c, h, :], ATm[h][:, 384:512], False, h == 1, ['Vpad', kATm[h]], [kYH])
            first = (d == 0 and c < NCH // 2) or (d == 1 and c >= NCH // 2)
            ys = YT[:, c * CH:(c + 1) * CH]
            if first:
                sy.op('act', lambda e, ys=ys: e.copy(out=ys, in_=PYH), reads=[kYH], writes=[('YT', c)])
            else:
                self.tt('dve', ys, PYH, ys, ALU.add, [kYH, ('YT', c)], [('YT', c)])
            self.mm(PXU, bh[:, c, :], Ub[:], True, False, [('bh', d), ('Ub', d)], [kXU])
            self.mm(PXU, kh[:, c, :], Vpad[:, c, 0, :], False, False, [('kh', d), 'Vpad'], [kXU])
            self.mm(PXU, kh[:, c, :], Vpad[:, c, 1, :], False, True, [('kh', d), 'Vpad'], [kXU])
            for h in range(2):
                hs = slice(h * 64, (h + 1) * 64)
                self.stt('dve', Hm[hs, hs], Hm[hs, hs], B['wc'][d][hs, c:c + 1], PXU[hs, hs], ALU.mult, ALU.add,
                         [('Hm', d), ('wc', d), kXU], [('Hm', d)])
            sy.op('act', lambda e: e.copy(out=Hb[:], in_=Hm[:]), reads=[('Hm', d)], writes=[('Hb', d)])
            yield

    def flush_late_casts(self, n):
        for _ in range(min(n, len(self.late_casts))):
            self.late_casts.pop(0)()

    def phase2_hp(self, slot, hp, B):
        sy = self.sy
        self.flush_late_casts(4)
        zT = self.zT
        c3 = lambda ap: ap.rearrange("p (c t) -> p c t", t=CH)
        wl = B['wl'][hp % 2]
        wk = ('wl', hp % 2)
        for i, name in enumerate(('w_up_f', 'w_up_b', 'a_up_f', 'a_up_b')):
            sy.dma('sp', wl[:, i, :], self.wb[name][:, hp * 128:(hp + 1) * 128], reads=[('wb', name)], writes=[wk])
        for j, (dst, dk, raw, rk) in enumerate(((B['zr'], 'zr', B['A8'], 'A8'), (B['zk'], 'zk', B['A7'], 'A7'), (B['zv'], 'zv', B['A6'], 'A6'))):
            f0 = j * D + hp * 128
            sy.dma('sp', raw[:], zT[slot, f0:f0 + 128, :], reads=[('zT', slot)], writes=[rk])
            self.shift(raw, dst, j * 32 + hp, 128, rk, dk)
        vb = B['A3'][:].bitcast(BF16)[:, 0:T]
        sy.op('act', lambda e: e.copy(out=vb, in_=B['zv'][:]), reads=['zv'], writes=['A3'])
        for g in range(NCH // 8):
            ps, pk = self.psum_rr(B)
            pv = ps[:, :].bitcast(BF16)[:, 0:1024].rearrange("p (j t) -> p j t", t=128)
            for j in range(8):
                c = g * 8 + j
                sy.op('pe', lambda e, c=c, j=j, pv=pv: e.transpose(out=pv[:, j, :], in_=vb[:, c * CH:(c + 1) * CH], identity=self.ident_bf[:]),
                      reads=['A3', 'ident_bf'], writes=[pk])
            for h in range(2):
                hs = slice(h * 64, (h + 1) * 64)
                X = self.evq()
                sy.op(X, (lambda e, g=g, pv=pv, h=h, hs=hs: e.tensor_copy(out=B['Vpad'][:, g * 8:(g + 1) * 8, h, hs], in_=pv[:, :, hs])) if X == 'dve'
                      else (lambda e, g=g, pv=pv, h=h, hs=hs: e.copy(out=B['Vpad'][:, g * 8:(g + 1) * 8, h, hs], in_=pv[:, :, hs])),
                      reads=[pk], writes=['Vpad'])
        self.ts('dve', B['kkn'][:], B['zk'][:], self.cpv(4, hp), None, ALU.mult, None, ['zk', 'cp_sb'], ['kkn'])
        sqb = B['A4'][:].bitcast(BF16)[:, 0:T]
        self.actf(sqb, B['kkn'][:], AF.Square, ['kkn'], ['A4'])
        for blk in range(4):
            bs = slice(blk * 512, (blk + 1) * 512)
            ps, pk = self.psum_rr(B)
            self.mm(ps[:, :], self.bones_bf[:], sqb[:, bs], True, True, ['bones_bf', 'A4'], [pk])
            self.ts('dve', B['A2'][:, bs], ps[:, :], 1e-12, None, ALU.max, None, [pk], ['A2'])
        self.actf(B['A2'][:], B['A2'][:], AF.Sqrt, ['A2'], ['A2'])
        sy.op('dve', lambda e: e.reciprocal(out=B['A2'][:], in_=B['A2'][:]), reads=['A2'], writes=['A2'])
        self.tt('dve', B['kkn'][:], B['kkn'][:], B['A2'][:], ALU.mult, ['kkn', 'A2'], ['kkn'])
        for d in range(2):
            self.phase2_prep_dir(slot, hp, d, B)
        gens = [self.chunk_gen(hp, 0, B), self.chunk_gen(hp, 1, B)]
        alive = [True, True]
        while any(alive):
            for i in range(2):
                if alive[i]:
                    try:
                        next(gens[i])
                    except StopIteration:
                        alive[i] = False
        YT = B['A1']
        ykeys = [('YT', c) for c in range(NCH)]
        sy.dma('sp', B['A8'][:], self.gT[slot, hp * 128:(hp + 1) * 128, :], reads=[('gT', slot)], writes=['A8'])
        sy.dma('sp', B['A7'][:], zT[slot, P1 + hp * 128:P1 + (hp + 1) * 128, :], reads=[('zT', slot)], writes=['A7'])
        sy.dma('sp', B['A6'][:], zT[slot, P2 + hp * 128:P2 + (hp + 1) * 128, :], reads=[('zT', slot)], writes=['A6'])
        sy.dma('sp', B['A4'][:], self.ybT[slot, hp * 128:(hp + 1) * 128, :], reads=[('ybT', slot)], writes=['A4'])
        self.tt('dve', B['A4'][:], B['A4'][:], B['A6'][:], ALU.mult, ['A4', 'A6'], ['A4'])
        self.tt('dve', B['A8'][:], B['A8'][:], B['A7'][:], ALU.mult, ['A8', 'A7'], ['A8'])
        NB = 4
        bsl = [slice(b * 512, (b + 1) * 512) for b in range(NB)]
        K2 = [('A2e', b) for b in range(NB)]
        K3 = [('A3e', b) for b in range(NB)]
        K5 = [('t5e', b) for b in range(NB)]
        t5 = [B['kkn'][:, bsl[b]] for b in range(NB)]
        yks = [ykeys[b * 4:(b + 1) * 4] for b in range(NB)]
        P = [None] * NB
        for b in range(NB):
            P[b] = self.psum_rr(B)
            self.mm(P[b][0][:, :], self.bones_f, YT[:, bsl[b]], True, True, ['c_sb'] + yks[b], [P[b][1]])
        for b in range(NB):
            self.stt('dve', B['A2'][:, bsl[b]], P[b][0][:, :], -1.0 / 64, YT[:, bsl[b]], ALU.mult, ALU.add,
                     [P[b][1]] + yks[b], ['A2', K2[b]])
        for b in range(NB):
            self.actf(B['A3'][:, bsl[b]], B['A2'][:, bsl[b]], AF.Square, [K2[b]], ['A3', K3[b]])
        for b in range(NB):
            P[b] = self.psum_rr(B)
            self.mm(P[b][0][:, :], self.bones_f, B['A3'][:, bsl[b]], True, True, ['c_sb', K3[b]], [P[b][1]])
        for b in range(NB):
            self.ts('dve', t5[b], P[b][0][:, :], 1.0 / 64, 64e-5, ALU.mult, ALU.add, [P[b][1]], ['kkn', K5[b]])
        for b in range(NB):
            self.actf(t5[b], t5[b], AF.Sqrt, [K5[b]], [K5[b]])
        for b in range(NB):
            sy.op('dve', lambda e, b=b: e.reciprocal(out=t5[b], in_=t5[b]), reads=[K5[b]], writes=[K5[b]])
        for b in range(NB):
            self.tt('dve', B['A2'][:, bsl[b]], B['A2'][:, bsl[b]], t5[b], ALU.mult, [K2[b], K5[b], 'kkn'], [K2[b]])
        for b in range(NB):
            self.ts('dve', B['A2'][:, bsl[b]], B['A2'][:, bsl[b]], self.cpv(7, hp), self.cpv(8, hp), ALU.mult, ALU.add,
                    [K2[b], 'cp_sb'], [K2[b]])
        for b in range(NB):
            P[b] = self.psum_rr(B)
            self.mm(P[b][0][:, :], self.bones_bf[:], B['rkb'][0][:, bsl[b]], True, False, ['bones_bf', ('rkb', 0)], [P[b][1]])
            self.mm(P[b][0][:, :], self.bones_bf[:], B['rkb'][1][:, bsl[b]], False, True, ['bones_bf', ('rkb', 1)], [P[b][1]])
        for b in range(NB):
            self.tt('dve', B['A3'][:, bsl[b]], P[b][0][:, :], B['zv'][:, bsl[b]], ALU.mult, [P[b][1], 'zv'], [K3[b]])
        for b in range(NB):
            self.tt('dve', B['A2'][:, bsl[b]], B['A2'][:, bsl[b]], B['A3'][:, bsl[b]], ALU.add, [K2[b], K3[b]], [K2[b]])
        self.tt('dve', B['A2'][:], B['A2'][:], B['A8'][:], ALU.mult, ['A2', 'A8'] + K2, ['A2'] + K2)
        mb = B['A3'][:].bitcast(BF16)[:, 0:T]
        self.tt('dve', mb, B['A2'][:], B['A4'][:], ALU.add, ['A2', 'A4'], ['A3'] + K3)
        if self.t3[slot] == T:
            sy.dma('sp', self.mT[slot, hp * 128:(hp + 1) * 128, :], mb, reads=['A3'], writes=[('mT', slot)], dkey=('mT', slot))
        else:
            H = T // 2
            ms = B['A2'][:].bitcast(BF16)[:, 0:H]
            self.ts('dve', ms, mb[:, 0:H], self.sel_sb[:, 0:1], None, ALU.mult, None, ['A3', 'sel'], ['A2'])
            self.stt('dve', ms, mb[:, H:T], self.sel_sb[:, 1:2], ms, ALU.mult, ALU.add, ['A3', 'sel', 'A2'], ['A2'])
            sy.dma('sp', self.mT[slot, hp * 128:(hp + 1) * 128, :], ms, reads=['A2'], writes=[('mT', slot)], dkey=('mT', slot))

    def phase2(self, slot, B):
        self.phase2_partA(slot, B)
        self.phase2_pool(slot, B)
        for hp in range(self.nhp or 32):
            self.phase2_hp(slot, hp, B)

    def alloc_phase3(self):
        C = {}
        sy = self.sy
        C['hT'] = self.sb("p3_hT", [128, 32, 512], F32)
        C['hnb'] = self.sb("p3_hnb", [128, 8192], F32)
        C['Rb'] = self.sb("p3_Rb", [128, 8192], F32)
        C['KT'] = self.sb("p3_KT", [128, 32, 256], BF16)
        C['V'] = self.sb("p3_V", [128, 2, 4096], BF16)
        C['W'] = [self.sb("p3_W%d" % i, [128, 32, 256], BF16) for i in range(2)]
        C['Eb'] = self.sb("p3_E", [128, 512], F32)
        C['rz'] = self.sb("p3_rz", [128, 512], F32)
        C['gT'] = self.sb("p3_gT", [128, 160], F32)
        C['ones'] = self.sb("p3_ones", [128, 128], BF16)
        C['idf'] = self.sb("p3_idf", [128, 128], F32)
        C['sq'] = [self.sb("p3_sq%d" % i, [128, 4, 512], BF16) for i in range(2)]
        C['ps'] = [self.pt("p3_ps%d" % i, [128, 512], F32) for i in range(4)]
        C['ptr'] = [self.pt("p3_ptr%d" % i, [128, 4, 128], F32) for i in range(2)]
        C['pss'] = self.pt("p3_pss", [128, 512], F32)
        C['psz'] = self.pt("p3_psz", [128, 512], F32)
        hnbf = C['hnb'][:].bitcast(BF16)
        C['hn'] = hnbf.rearrange("p (k t) -> p k t", t=512)
        C['mn'] = hnbf[:, 0:8192].rearrange("p (k t) -> p k t", t=256)
        C['xs'] = [C['hnb'][:, 0:4096], C['hnb'][:, 4096:8192]]
        Rbf = C['Rb'][:].bitcast(BF16)
        C['QT'] = Rbf.rearrange("p (k t) -> p k t", t=512)
        C['E'] = C['Eb'][:].bitcast(BF16).rearrange("p (m t) -> p m t", t=512)
        C['psn'] = 0
        C['wn'] = 0
        sy.dma('sp', C['gT'][:], self.gainsT, writes=['gT'])
        sy.dma('sp', C['idf'][:], self.consts[:, 0:128], writes=['idf'])
        sy.op('pool', lambda e: e.memset(C['ones'][:], 1.0), writes=['ones'])
        return C

    HN = ['hnA', 'hnB']

    @staticmethod
    def hk(kc):
        return 'hnA' if kc < 16 else 'hnB'

    def ps_rr(self, C):
        C['psn'] = (C['psn'] + 1) % 4
        return C['ps'][C['psn']], ('ps', C['psn'])

    def wblock(self, C, name, r0, nkc, c0, ncol=256):
        C['wn'] = (C['wn'] + 1) % 2
        i = C['wn']
        buf = C['W'][i]
        key = ('W', i)
        self.sy.dma('sp', buf[:, 0:nkc, 0:ncol],
                    self.wb[name][r0:r0 + nkc * 128, c0:c0 + ncol].rearrange("(kc p) f -> p kc f", p=128),
                    reads=[('wb', name)], writes=[key])
        return buf, key

    def copy_ev(self, out, in_, reads, writes):
        X = self.evq()
        if X == 'dve':
            return self.sy.op('dve', lambda e: e.tensor_copy(out=out, in_=in_), reads=reads, writes=writes)
        return self.sy.op('act', lambda e: e.copy(out=out, in_=in_), reads=reads, writes=writes)

    def load_T(self, C, rows_fn, nsub, dst3):
        sy = self.sy
        for sub in range(nsub):
            xs = C['xs'][sub % 2]
            xk = self.HN[sub % 2]
            sy.dma('sp', xs, rows_fn(sub), writes=[xk])
            for g in range(8):
                pt = C['ptr'][g % 2]
                pk = ('ptr', g % 2)
                for j in range(4):
                    kc = g * 4 + j
                    sy.op('pe', lambda e, kc=kc, j=j, pt=pt, xs=xs: e.transpose(out=pt[:, j, :], in_=xs[:, kc * 128:(kc + 1) * 128],
                                                                               identity=C['idf'][:]),
                          reads=[xk, 'idf'], writes=[pk])
                self.copy_ev(dst3[:, g * 4:(g + 1) * 4, sub * 128:(sub + 1) * 128], pt[:, :, :], [pk],
                             [('hT', g * 4 + j) for j in range(4)])

    def fm_norm(self, C, src3, N, gidx, dst3, dkey_fn):
        sy = self.sy
        pss = C['pss']
        for g in range(8):
            sq = C['sq'][g % 2]
            sk = ('sq', g % 2)
            sy.op('act', lambda e, g=g, sq=sq: e.activation(out=sq[:, :, 0:N], in_=src3[:, g * 4:(g + 1) * 4, :], func=AF.Square),
                  reads=[('hT', g * 4 + j) for j in range(4)], writes=[sk])
            for j in range(4):
                self.mm(pss[:, 0:N], C['ones'][:], sq[:, j, 0:N], g == 0 and j == 0, g == 7 and j == 3, ['ones', sk], ['pss'])
        rz = C['rz']
        sy.op('act', lambda e: e.activation(out=rz[:, 0:N], in_=pss[:, 0:N], func=AF.Sqrt, bias=self.eps_sb[:], scale=1.0 / D),
              reads=['pss', 'eps'], writes=['rz'])
        sy.op('dve', lambda e: e.reciprocal(out=rz[:, 0:N], in_=rz[:, 0:N]), reads=['rz'], writes=['rz'])
        for kc in range(32):
            self.stt('dve', dst3[:, kc, :], src3[:, kc, :], C['gT'][:, gidx * 32 + kc:gidx * 32 + kc + 1], rz[:, 0:N],
                     ALU.mult, ALU.mult, [('hT', kc), 'gT', 'rz'], [dkey_fn(kc)])

    def proj_fm(self, C, name, N, rhs_fn, rkey_fn, evac_fn, r0=0, nkc=32, c0=0, nft=32):
        for b in range(0, nft, 2):
            W, wk = self.wblock(C, name, r0, nkc, c0 + b * 128)
            for j in range(2):
                ps, pk = self.ps_rr(C)
                for kc in range(nkc):
                    self.mm(ps[:, 0:N], W[:, kc, j * 128:(j + 1) * 128], rhs_fn(kc), kc == 0, kc == nkc - 1,
                            [wk, rkey_fn(kc)], [pk])
                evac_fn(b + j, ps, pk)

    def phase3_mem(self, slot, C):
        sy = self.sy
        memT = C['hT'][:, :, 0:256]
        self.load_T(C, lambda sub: self.mem[slot, sub * 128:(sub + 1) * 128, :], 2, memT)
        self.fm_norm(C, memT, 256, 2, C['mn'], lambda kc: 'hnA')
        mn = C['mn']
        KT, V = C['KT'], C['V']
        self.proj_fm(C, 'xk', 256, lambda kc: mn[:, kc, :], lambda kc: 'hnA',
                     lambda ft, ps, pk: self.copy_ev(KT[:, ft, :], ps[:, 0:256], [pk], ['KT']))
        for fb in range(16):
            W, wk = self.wblock(C, 'xv', 0, 32, fb * 256)
            for mt in range(2):
                ps, pk = self.ps_rr(C)
                for kc in range(32):
                    self.mm(ps[:, 0:256], mn[:, kc, mt * 128:(mt + 1) * 128], W[:, kc, :], kc == 0, kc == 31, [wk, 'hnA'], [pk])
                self.copy_ev(V[:, mt, fb * 256:(fb + 1) * 256], ps[:, 0:256], [pk], ['V'])

    def phase3_tile(self, slot, tok0, C):
        sy = self.sy
        hT, hn, QT, KT, V, E, rz = C['hT'], C['hn'], C['QT'], C['KT'], C['V'], C['E'], C['rz']
        hk = self.hk

        def add_evac(ft, ps, pk):
            self.tt('dve', hT[:, ft, :], ps[:, :], hT[:, ft, :], ALU.add, [pk, ('hT', ft)], [('hT', ft)])

        self.load_T(C, lambda sub: self.x3[slot, tok0 + sub * 128:tok0 + (sub + 1) * 128, :], 4, hT)
        for q in range(4):
            sy.dma('sp', hn[:, q * 8:(q + 1) * 8, :],
                   self.mT[slot, q * 1024:(q + 1) * 1024, tok0:tok0 + 512].rearrange("(kc p) t -> p kc t", p=128),
                   reads=[('mT', slot)], writes=[hk(q * 8)])
        self.proj_fm(C, 'w_out', 512, lambda kc: hn[:, kc, :], hk, add_evac)
        self.fm_norm(C, hT, 512, 1, hn, hk)
        self.proj_fm(C, 'xq', 512, lambda kc: hn[:, kc, :], hk,
                     lambda ft, ps, pk: self.copy_ev(QT[:, ft, :], ps[:, :], [pk], ['R']))
        for h in range(4):
            for mt in range(2):
                ps, pk = self.ps_rr(C)
                for j in range(8):
                    ft = h * 8 + j
                    self.mm(ps[:, :], KT[:, ft, mt * 128:(mt + 1) * 128], QT[:, ft, :], j == 0, j == 7, ['KT', 'R'], [pk])
                sy.op('act', lambda e, mt=mt, ps=ps: e.activation(out=E[:, mt, :], in_=ps[:, :], func=AF.Exp, scale=1.0 / 32.0),
                      reads=[pk], writes=['E'])
            for mt in range(2):
                self.mm(C['psz'][:, :], C['ones'][:], E[:, mt, :], mt == 0, mt == 1, ['ones', 'E'], ['psz'])
            sy.op('dve', lambda e: e.reciprocal(out=rz[:, :], in_=C['psz'][:, :]), reads=['psz'], writes=['rz'])
            for j in range(8):
                dt_ = h * 8 + j
                ps, pk = self.ps_rr(C)
                for mt in range(2):
                    self.mm(ps[:, :], V[:, mt, dt_ * 128:(dt_ + 1) * 128], E[:, mt, :], mt == 0, mt == 1, ['V', 'E'], [pk])
                self.tt('dve', hn[:, dt_, :], ps[:, :], rz[:, :], ALU.mult, [pk, 'rz'], [hk(dt_)])
        self.proj_fm(C, 'xo', 512, lambda kc: hn[:, kc, :], hk, add_evac)
        self.fm_norm(C, hT, 512, 3, hn, hk)
        aT = QT
        sg = [C['Eb'], C['rz']]
        sgk = ['E', 'rz']
        t0 = 0
        for npair in (11, 11, 11, 10):
            for p in range(npair):
                j0 = t0 + 2 * p
                Wg, wgk = self.wblock(C, 'ffn_w13', 0, 32, j0 * 128)
                for j in range(2):
                    ps, pk = self.ps_rr(C)
                    for kc in range(32):
                        self.mm(ps[:, :], Wg[:, kc, j * 128:(j + 1) * 128], hn[:, kc, :], kc == 0, kc == 31, [wgk, hk(kc)], [pk])
                    sy.op('act', lambda e, j=j, ps=ps: e.activation(out=sg[j][:, :], in_=ps[:, :], func=AF.Silu),
                          reads=[pk], writes=[sgk[j]])
                Wu, wuk = self.wblock(C, 'ffn_w13', 0, 32, FF + j0 * 128)
                for j in range(2):
                    ps, pk = self.ps_rr(C)
                    for kc in range(32):
                        self.mm(ps[:, :], Wu[:, kc, j * 128:(j + 1) * 128], hn[:, kc, :], kc == 0, kc == 31, [wuk, hk(kc)], [pk])
                    self.tt('dve', aT[:, 2 * p + j, :], ps[:, :], sg[j][:, :], ALU.mult, [pk, sgk[j]], ['R'])
            nt = 2 * npair
            self.proj_fm(C, 'ffn_w2', 512, lambda kc: aT[:, kc, :], lambda kc: 'R', add_evac, r0=t0 * 128, nkc=nt)
            t0 += nt
        self.fm_norm(C, hT, 512, 4, hT, lambda kc: ('hT', kc))
        for sub in range(4):
            ys = C['xs'][sub % 2]
            yk = self.HN[sub % 2]
            for g in range(8):
                pt = C['ptr'][g % 2]
                pk = ('ptr', g % 2)
                for j in range(4):
                    kc = g * 4 + j
                    sy.op('pe', lambda e, kc=kc, j=j, pt=pt, sub=sub: e.transpose(out=pt[:, j, :], in_=hT[:, kc, sub * 128:(sub + 1) * 128],
                                                                                 identity=C['idf'][:]),
                          reads=[('hT', kc), 'idf'], writes=[pk])
                self.copy_ev(ys[:, g * 512:(g + 1) * 512].rearrange("p (j t) -> p j t", t=128), pt[:, :, :], [pk], [yk])
            sy.dma('sp', self.y[slot, tok0 + sub * 128:tok0 + (sub + 1) * 128, :], ys, reads=[yk], writes=[('y', slot)], dkey=('y', slot))

    def phase3(self, slot, C):
        self.phase3_mem(slot, C)
        for tt in range(self.ntile3 or (self.t3[slot] // 512)):
            self.phase3_tile(slot, tt * 512, C)

    def barrier(self):
        for X in ('pe', 'dve', 'act', 'pool', 'sp'):
            self.sy.drain(X, skip_wb=True)

    def build(self):
        nc = self.nc
        sy = self.sy
        self.gstack = contextlib.ExitStack()
        self.stack = self.gstack
        self.eps_sb = self.sb("eps_sb", [128, 1], F32)
        self.ident_bf = self.sb("ident_bf", [128, 128], BF16)
        self.bones_bf = self.sb("bones_bf", [128, 128], BF16)
        sy.op('dve', lambda e: e.memset(self.eps_sb[:], 1e-6), writes=['eps'])
        self.late_casts = []
        if 0 in self.phases:
            for th in self.phase0(False):
                th()
        with contextlib.ExitStack() as st:
            self.stack = st
            tmp = self.sb("c_tmp", [128, 256], F32)
            sy.dma('sp', tmp[:], self.consts[:, 0:256], writes=['c_tmp'])
            sy.op('dve', lambda e: e.tensor_copy(out=self.ident_bf[:], in_=tmp[:, 0:128]), reads=['c_tmp'], writes=['ident_bf'])
            sy.op('dve', lambda e: e.tensor_copy(out=self.bones_bf[:], in_=tmp[:, 128:256]), reads=['c_tmp'], writes=['bones_bf'])
            self.barrier()
        self.stack = self.gstack
        if 1 in self.phases:
            with contextlib.ExitStack() as st:
                self.stack = st
                A = self.alloc_phase1()
                self.phase1_all(A)
                self.barrier()
            self.stack = self.gstack
        if 0 in self.phases:
            self.late_casts = self.phase0(True)
            if 2 not in self.phases:
                self.flush_late_casts(len(self.late_casts))
        if 2 in self.phases:
            with contextlib.ExitStack() as st:
                self.stack = st
                self.load_consts()
                B = self.alloc_phase2()
                B['pp'] = B['zr_full']
                B['pa'] = B['zk_full']
                B['pb'] = B['zv_full']
                for slot in range(self.nslot):
                    self.phase2(slot, B)
                self.barrier()
            self.stack = self.gstack
        self.flush_late_casts(len(self.late_casts))
        if 3 in self.phases:
            with contextlib.ExitStack() as st:
                self.stack = st
                C = self.alloc_phase3()
                for slot in range(self.nslot):
                    self.phase3(slot, C)
                self.barrier()
            self.stack = self.gstack
        sy.drain('sp')
        return nc


def make_consts():
    c = np.zeros((128, 1664), np.float32)
    p = np.arange(128)
    c[:, 0:128] = np.eye(128, dtype=np.float32)
    c[:, 128:256] = (p[:, None] // 64 == p[None, :] // 64).astype(np.float32)
    s = p[:, None]
    t = p[None, :]
    strict_f = (s < t).astype(np.float32)
    incl_f = (s <= t).astype(np.float32)
    strict_b = (s > t).astype(np.float32)
    incl_b = (s >= t).astype(np.float32)
    c[:, 256:768] = np.concatenate([strict_f, incl_f, strict_f, incl_f], axis=1)
    c[:, 768:1280] = np.concatenate([strict_b, incl_b, strict_b, incl_b], axis=1)
    c[:, 1280:1408] = strict_b
    c[:, 1408:1536] = strict_f
    c[:, 1536:1664] = 1.0
    c[:, 1536] = 0.0
    return c


def prep_shared(inp):
    sh = {}
    sh['gains'] = np.ascontiguousarray(np.stack([inp['norm_mix_g'][0], inp['norm_x_g'][0], inp['norm_mem_g'][0],
                                                 inp['norm_ffn_g'][0], inp['norm_final_g']], axis=0).astype(np.float32))
    sh['gainsT'] = np.ascontiguousarray(sh['gains'].reshape(5, 32, 128).transpose(2, 0, 1).reshape(128, 160))
    sw = np.zeros((3, 102 * 128), np.float32)
    sw[:, :RW] = inp['shift_w'][0]
    sh['taps'] = np.ascontiguousarray(sw.reshape(3, 102, 128).transpose(2, 1, 0).reshape(128, 306))
    vecs = [inp['w0_f'][0], inp['w0_b'][0], inp['a0_f'][0], inp['a0_b'][0], inp['k_k'][0], inp['k_a'][0],
            inp['r_k'][0].reshape(-1), inp['ln_x_g'][0], inp['ln_x_b'][0], inp['pool_scale'][0]]
    cp = np.stack([v.reshape(32, 128).T for v in vecs], axis=1)
    sh['cp'] = np.ascontiguousarray(cp.reshape(128, 320).astype(np.float32))
    sh['consts'] = make_consts()
    t = np.arange(T)
    ic = np.zeros((4, T), np.float32)
    for gi, win in enumerate((2, 4, 8, 16)):
        lo = np.clip(t - win // 2, 0, T)
        hi = np.clip(t + win - win // 2, 0, T)
        ic[gi] = 1.0 / (hi - lo).astype(np.float32)
    sh['invcnt'] = ic
    wmap = dict(w_in=inp['w_in'][0], w_out=inp['w_out'][0], xq=inp['xq'][0], xk=inp['xk'][0], xv=inp['xv'][0],
                xo=inp['xo'][0], ffn_w13=inp['ffn_w13'][0], ffn_w2=inp['ffn_w2'][0],
                pool_w=inp['pool_w'][0].reshape(PW, 1024), g_up=inp['g_up'][0],
                w_up_f=inp['w_up_f'][0], w_up_b=inp['w_up_b'][0], a_up_f=inp['a_up_f'][0], a_up_b=inp['a_up_b'][0])
    sh.update(wmap)
    return sh


WEIGHT_NAMES = ('w_in', 'w_out', 'xq', 'xk', 'xv', 'xo', 'ffn_w13', 'ffn_w2', 'pool_w', 'g_up',
                'w_up_f', 'w_up_b', 'a_up_f', 'a_up_b')


def kernel(**inputs):
    inp = {k: np.asarray(v) for k, v in inputs.items()}
    sh = prep_shared(inp)
    k = K()
    nc = k.build()
    in_maps = []
    for c in range(8):
        m = {"x": np.stack([inp['x_prompt'][c], inp['x_sample'][c % 4]], axis=0),
             "mem": np.stack([inp['mem_prompt'][c], inp['mem_sample'][c % 4]], axis=0)}
        hsel = c // 4
        m["x1h"] = np.ascontiguousarray(inp['x_sample'][c % 4][hsel * (T // 2):(hsel + 1) * (T // 2)])
        s = np.zeros((128, 2), np.float32)
        s[:, hsel] = 1.0
        m["sel"] = s
        for n in ('gains', 'gainsT', 'taps', 'cp', 'consts', 'invcnt') + WEIGHT_NAMES:
            m[n] = sh[n]
        in_maps.append(m)
    res = run_bass_kernel_spmd(nc, in_maps, core_ids=list(range(8)))
    y_prompt = np.stack([np.asarray(res.results[c]["y0"]) for c in range(8)], axis=0).astype(np.float32, copy=False)
    y_sample = np.stack([np.concatenate([np.asarray(res.results[c]["y1"]), np.asarray(res.results[c + 4]["y1"])], axis=0)
                         for c in range(4)], axis=0).astype(np.float32, copy=False)
    return (y_prompt, y_sample)
```
